# Optimizing a Trainium2 kernel written in Bass

```python
import jax, jax.numpy as jnp
from jax import lax
import numpy as np

D_MODEL = 1024
BATCH = 2
SEQ = 16384
DEPTH = 1
DEC_BATCH = 16
DEC_SEQ = 32
PAST_LEN = 2048

CHUNK = 64
BAND_CHUNKS = 8
A_WINDOW = BAND_CHUNKS * CHUNK
N_HEADS_A = 8
HEAD_DIM_A = 64
WIDTH_A = N_HEADS_A * HEAD_DIM_A
REL_CLIP = 256
N_HEADS_B = 4
KEY_DIM_B = 64
VAL_DIM_B = 128
WIDTH_BK = N_HEADS_B * KEY_DIM_B
WIDTH_BV = N_HEADS_B * VAL_DIM_B
GATE_RANK = 16
GATE_TAU = 16.0
GLA_BLOCK = 16
N_MEM = 256
N_HEADS_MEM = 4
HEAD_DIM_MEM = D_MODEL // N_HEADS_MEM
D_FF = -(-(8 * D_MODEL) // (3 * 256)) * 256
MIX_WIDTH = WIDTH_A + WIDTH_BV
IN_SIZES = (WIDTH_A, WIDTH_A, WIDTH_A, WIDTH_BK, WIDTH_BK, WIDTH_BV, GATE_RANK, WIDTH_BV)
IN_OFFSETS = tuple(int(v) for v in np.cumsum(IN_SIZES)[:-1])
IN_WIDTH = sum(IN_SIZES)
EPS = 1e-6

kernel_name = "hybrid_chunkband_gla_stream_step"


def rmsnorm(x, g):
    xf = x.astype(jnp.float32)
    y = xf * lax.rsqrt(jnp.mean(xf * xf, axis=-1, keepdims=True) + EPS)
    return (y * g.astype(jnp.float32)).astype(x.dtype)


def mix_inputs(h, w_in, w_alpha2, b_alpha):
    n, t = h.shape[:2]
    qa, ka, va, qb, kb, vb, g_low, r = jnp.split(h @ w_in, IN_OFFSETS, axis=-1)
    heads_a = lambda z: z.reshape(n, t, N_HEADS_A, HEAD_DIM_A)
    z = (g_low @ w_alpha2 + b_alpha).astype(jnp.float32)
    log_a = (jax.nn.log_sigmoid(z) / GATE_TAU).reshape(n, t, N_HEADS_B, KEY_DIM_B)
    return (heads_a(qa), heads_a(ka), heads_a(va),
            qb.reshape(n, t, N_HEADS_B, KEY_DIM_B), kb.reshape(n, t, N_HEADS_B, KEY_DIM_B),
            vb.reshape(n, t, N_HEADS_B, VAL_DIM_B), log_a, r)


def rel_bias_block(table, n_q, n_k, offset):
    dist = offset + jnp.arange(n_q)[:, None] - jnp.arange(n_k)[None, :]
    idx = jnp.clip(dist, -REL_CLIP, REL_CLIP) + REL_CLIP
    return table[:, idx]


def band_attention(q, k, v, bias, valid):
    s = jnp.einsum('...qhd,...khd->...hqk', q, k).astype(jnp.float32) * (HEAD_DIM_A ** -0.5)
    s = jnp.where(valid, s + bias.astype(jnp.float32), -1e30)
    p = jax.nn.softmax(s, axis=-1).astype(v.dtype)
    return jnp.einsum('...hqk,...khd->...qhd', p, v)


def chunk_band_prompt(qa, ka, va, table):
    b, s = qa.shape[:2]
    nc = s // CHUNK
    shp = (b, nc, CHUNK, N_HEADS_A, HEAD_DIM_A)
    qc = qa.reshape(shp)
    pad = ((0, 0), (BAND_CHUNKS, 0), (0, 0), (0, 0), (0, 0))
    kp = jnp.pad(ka.reshape(shp), pad)
    vp = jnp.pad(va.reshape(shp), pad)
    n_band = (BAND_CHUNKS + 1) * CHUNK
    kband = jnp.stack([kp[:, o:o + nc] for o in range(BAND_CHUNKS + 1)], axis=2).reshape(b, nc, n_band, N_HEADS_A, HEAD_DIM_A)
    vband = jnp.stack([vp[:, o:o + nc] for o in range(BAND_CHUNKS + 1)], axis=2).reshape(b, nc, n_band, N_HEADS_A, HEAD_DIM_A)
    chunk_ok = (jnp.arange(nc)[:, None] + jnp.arange(BAND_CHUNKS + 1)[None, :] - BAND_CHUNKS) >= 0
    valid = jnp.repeat(chunk_ok, CHUNK, axis=1)[None, :, None, None, :]
    bias = rel_bias_block(table, CHUNK, n_band, A_WINDOW)
    o = band_attention(qc, kband, vband, bias, valid)
    return o.reshape(b, s, N_HEADS_A, HEAD_DIM_A)


def chunk_band_sample(qa, ka, va, cache_k, cache_v, table):
    t = qa.shape[1]
    l = cache_k.shape[1]
    k_all = jnp.concatenate([cache_k.astype(ka.dtype), ka], axis=1)
    v_all = jnp.concatenate([cache_v.astype(va.dtype), va], axis=1)
    bias = rel_bias_block(table, t, l + t, l)
    valid = jnp.ones((l + t,), dtype=bool)
    return band_attention(qa, k_all, v_all, bias, valid)


def gla_recurrence(q, k, v, log_a, s0, block):
    f32 = jnp.float32
    n, t, h, dk = q.shape
    dv = v.shape[-1]
    nb = t // block
    qb = (q.astype(f32) * (dk ** -0.5)).reshape(n, nb, block, h, dk)
    kb = k.astype(f32).reshape(n, nb, block, h, dk)
    vb = v.astype(f32).reshape(n, nb, block, h, dv)
    bcum = jnp.cumsum(log_a.astype(f32).reshape(n, nb, block, h, dk), axis=2)
    b_last = bcum[:, :, -1]
    causal = jnp.tril(jnp.ones((block, block), dtype=bool))
    diff = bcum[:, :, :, None] - bcum[:, :, None, :]
    w = jnp.exp(jnp.where(causal[:, :, None, None], diff, -jnp.inf))
    scores = jnp.einsum('nbihd,nbjhd,nbijhd->nbhij', qb, kb, w)
    o_intra = jnp.einsum('nbhij,nbjhv->nbihv', scores, vb)
    k_dec = kb * jnp.exp(b_last[:, :, None] - bcum)
    ds = jnp.einsum('nbjhd,nbjhv->nbhdv', k_dec, vb)
    decay = jnp.exp(b_last)

    def step(s, inp):
        dec, d = inp
        return dec[..., None] * s + d, s

    s_final, s_starts = lax.scan(step, s0.astype(f32), (jnp.moveaxis(decay, 1, 0), jnp.moveaxis(ds, 1, 0)))
    s_starts = jnp.moveaxis(s_starts, 0, 1)
    o_inter = jnp.einsum('nbihd,nbhdv->nbihv', qb * jnp.exp(bcum), s_starts)
    return (o_intra + o_inter).reshape(n, t, h, dv), s_final


def mix_output(o_a, o_b, r, g_gla, w_o):
    n, t = r.shape[:2]
    ob = o_b * lax.rsqrt(jnp.mean(o_b * o_b, axis=-1, keepdims=True) + EPS)
    y_b = ob.reshape(n, t, WIDTH_BV) * g_gla.astype(jnp.float32) * jax.nn.silu(r.astype(jnp.float32))
    y = jnp.concatenate([o_a.reshape(n, t, WIDTH_A), y_b.astype(o_a.dtype)], axis=-1)
    return y @ w_o


def mem_kv(mem, g_mem, w_mk, w_mv):
    n = mem.shape[0]
    m = rmsnorm(mem, g_mem)
    k = (m @ w_mk).reshape(n, N_MEM, N_HEADS_MEM, HEAD_DIM_MEM)
    v = (m @ w_mv).reshape(n, N_MEM, N_HEADS_MEM, HEAD_DIM_MEM)
    return k, v


def mem_attention(h, k, v, w_mq, w_mo):
    n, t = h.shape[:2]
    q = (h @ w_mq).reshape(n, t, N_HEADS_MEM, HEAD_DIM_MEM)
    s = jnp.einsum('nqhd,nkhd->nhqk', q, k.astype(q.dtype)).astype(jnp.float32) * (HEAD_DIM_MEM ** -0.5)
    p = jax.nn.softmax(s, axis=-1).astype(q.dtype)
    o = jnp.einsum('nhqk,nkhd->nqhd', p, v.astype(q.dtype)).reshape(n, t, D_MODEL)
    return o @ w_mo


def swiglu(h, w_gate, w_up, w_down):
    return (jax.nn.silu(h @ w_gate) * (h @ w_up)) @ w_down


def setup_inputs(seed: int = 0) -> dict:
    key = jax.random.key(seed)
    ks = iter(jax.random.split(key, 40))
    nrm = lambda shape, scale: jax.random.normal(next(ks), shape, jnp.float32) * scale
    gain = lambda shape: 1.0 + nrm(shape, 0.05)
    a_keep = min(A_WINDOW, PAST_LEN)
    L = DEPTH
    return {
        "x_prompt": nrm((BATCH, SEQ, D_MODEL), 1.0),
        "x_sample": nrm((DEC_BATCH, DEC_SEQ, D_MODEL), 1.0),
        "mem_prompt": nrm((BATCH, N_MEM, D_MODEL), 1.0),
        "cache_a_k": nrm((L, DEC_BATCH, a_keep, N_HEADS_A, HEAD_DIM_A), 1.0),
        "cache_a_v": nrm((L, DEC_BATCH, a_keep, N_HEADS_A, HEAD_DIM_A), 1.0),
        "state_gla": nrm((L, DEC_BATCH, N_HEADS_B, KEY_DIM_B, VAL_DIM_B), 1.0),
        "cache_mem_k": nrm((L, DEC_BATCH, N_MEM, N_HEADS_MEM, HEAD_DIM_MEM), 1.0),
        "cache_mem_v": nrm((L, DEC_BATCH, N_MEM, N_HEADS_MEM, HEAD_DIM_MEM), 1.0),
        "g_pre_mix": gain((L, D_MODEL)),
        "w_in": nrm((L, D_MODEL, IN_WIDTH), D_MODEL ** -0.5),
        "rel_bias": nrm((L, N_HEADS_A, 2 * REL_CLIP + 1), 0.2),
        "w_alpha2": nrm((L, GATE_RANK, WIDTH_BK), GATE_RANK ** -0.5),
        "b_alpha": nrm((L, WIDTH_BK), 0.1),
        "g_gla_out": gain((L, WIDTH_BV)),
        "w_o": nrm((L, MIX_WIDTH, D_MODEL), MIX_WIDTH ** -0.5),
        "g_post_mix": gain((L, D_MODEL)),
        "g_pre_mem": gain((L, D_MODEL)),
        "g_mem": gain((L, D_MODEL)),
        "w_mq": nrm((L, D_MODEL, D_MODEL), D_MODEL ** -0.5),
        "w_mk": nrm((L, D_MODEL, D_MODEL), D_MODEL ** -0.5),
        "w_mv": nrm((L, D_MODEL, D_MODEL), D_MODEL ** -0.5),
        "w_mo": nrm((L, D_MODEL, D_MODEL), D_MODEL ** -0.5),
        "g_post_mem": gain((L, D_MODEL)),
        "g_pre_ffn": gain((L, D_MODEL)),
        "w_ffn_gate": nrm((L, D_MODEL, D_FF), D_MODEL ** -0.5),
        "w_ffn_up": nrm((L, D_MODEL, D_FF), D_MODEL ** -0.5),
        "w_ffn_down": nrm((L, D_FF, D_MODEL), D_FF ** -0.5),
        "g_post_ffn": gain((L, D_MODEL)),
    }


def reference(x_prompt, x_sample, mem_prompt, cache_a_k, cache_a_v, state_gla, cache_mem_k, cache_mem_v,
              g_pre_mix, w_in, rel_bias, w_alpha2, b_alpha, g_gla_out, w_o, g_post_mix,
              g_pre_mem, g_mem, w_mq, w_mk, w_mv, w_mo, g_post_mem,
              g_pre_ffn, w_ffn_gate, w_ffn_up, w_ffn_down, g_post_ffn):
    xp, xs = x_prompt, x_sample
    pa_k, pa_v, p_gla, p_mk, p_mv, sa_k, sa_v, s_gla = [], [], [], [], [], [], [], []
    for l in range(DEPTH):
        hp = rmsnorm(xp, g_pre_mix[l])
        qa, ka, va, qb, kb, vb, la, r = mix_inputs(hp, w_in[l], w_alpha2[l], b_alpha[l])
        oa = chunk_band_prompt(qa, ka, va, rel_bias[l])
        s0 = jnp.zeros((xp.shape[0], N_HEADS_B, KEY_DIM_B, VAL_DIM_B), jnp.float32)
        ob, sp = gla_recurrence(qb, kb, vb, la, s0, GLA_BLOCK)
        xp = xp + rmsnorm(mix_output(oa, ob, r, g_gla_out[l], w_o[l]), g_post_mix[l])
        keep = min(A_WINDOW, ka.shape[1])
        pa_k.append(ka[:, -keep:])
        pa_v.append(va[:, -keep:])
        p_gla.append(sp)

        hs = rmsnorm(xs, g_pre_mix[l])
        qa_s, ka_s, va_s, qb_s, kb_s, vb_s, la_s, r_s = mix_inputs(hs, w_in[l], w_alpha2[l], b_alpha[l])
        oa_s = chunk_band_sample(qa_s, ka_s, va_s, cache_a_k[l], cache_a_v[l], rel_bias[l])
        ob_s, ss = gla_recurrence(qb_s, kb_s, vb_s, la_s, state_gla[l], xs.shape[1])
        xs = xs + rmsnorm(mix_output(oa_s, ob_s, r_s, g_gla_out[l], w_o[l]), g_post_mix[l])
        sa_k.append(ka_s)
        sa_v.append(va_s)
        s_gla.append(ss)

        mk, mv = mem_kv(mem_prompt, g_mem[l], w_mk[l], w_mv[l])
        xp = xp + rmsnorm(mem_attention(rmsnorm(xp, g_pre_mem[l]), mk, mv, w_mq[l], w_mo[l]), g_post_mem[l])
        xs = xs + rmsnorm(mem_attention(rmsnorm(xs, g_pre_mem[l]), cache_mem_k[l], cache_mem_v[l], w_mq[l], w_mo[l]), g_post_mem[l])
        p_mk.append(mk)
        p_mv.append(mv)

        xp = xp + rmsnorm(swiglu(rmsnorm(xp, g_pre_ffn[l]), w_ffn_gate[l], w_ffn_up[l], w_ffn_down[l]), g_post_ffn[l])
        xs = xs + rmsnorm(swiglu(rmsnorm(xs, g_pre_ffn[l]), w_ffn_gate[l], w_ffn_up[l], w_ffn_down[l]), g_post_ffn[l])

    return (xp, xs, jnp.stack(pa_k), jnp.stack(pa_v), jnp.stack(p_gla), jnp.stack(p_mk), jnp.stack(p_mv),
            jnp.stack(sa_k), jnp.stack(sa_v), jnp.stack(s_gla))
```

```python
import numpy as np
import ml_dtypes
from contextlib import ExitStack
import concourse.bass as bass
import concourse.mybir as mybir
from concourse.bass_utils import run_bass_kernel_spmd

F32 = mybir.dt.float32
BF16 = mybir.dt.bfloat16
AF = mybir.ActivationFunctionType
ALU = mybir.AluOpType
AX = mybir.AxisListType

COMPUTE = ("pe", "act", "dve", "pool")
QUEUES = ("sp", "act", "pool")


class Buf:
    __slots__ = ("name", "t", "w", "r", "excl")

    def __init__(self, name, t=None, excl=False):
        self.name = name
        self.t = t
        self.w = None
        self.r = {}
        self.excl = excl

    def sub(self, n):
        return [Buf(f"{self.name}.{i}", self.t, self.excl) for i in range(n)]


class Prog:
    ENGS = ("pe", "act", "dve", "pool", "sp")

    def __init__(self, nc, es, dma_pool=10):
        self.nc = nc
        self.es = es
        self.ops = {e: [] for e in self.ENGS}
        self.seen_e = {e: {} for e in self.ENGS}
        self.seen_d = {e: {} for e in self.ENGS}
        self.dpool = {q: [(q, i) for i in range(dma_pool)] for q in ("sp", "pool", "act")}
        self.dnext = {q: 0 for q in self.dpool}
        self.dcount = {}
        self.out_tokens = {}
        self.dram = {}
        self.n_sb = 0

    def sb(self, name, shape, dtype):
        t = self.es.enter_context(self.nc.sbuf_tensor(name, list(shape), dtype))
        return Buf(name, t)

    def ps(self, name, shape, dtype):
        t = self.es.enter_context(self.nc.psum_tensor(name, list(shape), dtype))
        return Buf(name, t, excl=True)

    def dbuf(self, name):
        if name not in self.dram:
            self.dram[name] = Buf("dram:" + name)
        return self.dram[name]

    def _deps(self, eng, reads, writes, is_dma=False):
        raw, other = [], []
        for b in reads:
            if b.w is not None:
                raw.append(b.w)
            if b.excl:
                other.extend(b.r.values())
        for b in writes:
            if b.w is not None:
                other.append(b.w)
            other.extend(b.r.values())
        waits = []
        for tok, is_raw in [(t, True) for t in raw] + [(t, False) for t in other]:
            if tok[0] == "e":
                _, pe, idx = tok
                if pe == eng and not is_raw and not is_dma:
                    continue
                if pe == eng and eng == "pe":
                    continue
                if self.seen_e[eng].get(pe, -1) >= idx:
                    continue
                self.seen_e[eng][pe] = idx
                self.ops[pe][idx]["signal"] = True
                waits.append(tok)
            else:
                _, key, val = tok
                if self.seen_d[eng].get(key, 0) >= val:
                    continue
                self.seen_d[eng][key] = val
                waits.append(tok)
        return waits

    def _mark(self, tok, reads, writes):
        for b in reads:
            k = (tok[0], tok[1])
            b.r[k] = tok
        for b in writes:
            b.w = tok
            b.r = {}

    def op(self, eng, fn, reads=(), writes=()):
        waits = self._deps(eng, reads, writes)
        idx = len(self.ops[eng])
        self.ops[eng].append({"fn": fn, "waits": waits, "signal": False, "dma": None,
                              "stage": getattr(self, "cur_stage", "")})
        tok = ("e", eng, idx)
        self._mark(tok, [b for b in reads if b not in writes], writes)
        return tok

    def dma(self, q, out_ap, in_ap, reads=(), writes=(), out=False, dram_r=(), dram_w=(), **kw):
        reads = list(reads) + [self.dbuf(n) for n in dram_r]
        writes = list(writes) + [self.dbuf(n) for n in dram_w]
        pool = self.dpool[q]
        key = pool[self.dnext[q] % len(pool)]
        self.dnext[q] += 1
        k = self.dcount.get(key, 0)
        waits = self._deps(q, reads, writes, is_dma=True)
        if k > 0 and self.seen_d[q].get(key, 0) < 16 * k:
            self.seen_d[q][key] = 16 * k
            waits.append(("d", key, 16 * k))
        self.dcount[key] = k + 1
        tok = ("d", key, 16 * (k + 1))
        fn = lambda e, o=out_ap, i=in_ap, kw=kw: e.dma_start(out=o, in_=i, **kw)
        self.ops[q].append({"fn": fn, "waits": waits, "signal": False, "dma": key})
        self._mark(tok, reads, writes)
        if out:
            self.out_tokens[key] = tok
        return tok

    def finish(self):
        nc = self.nc
        fin = []
        for key, tok in self.out_tokens.items():
            if self.seen_d["sp"].get(key, 0) < tok[2]:
                fin.append(tok)
        es = self.es
        esem = {e: es.enter_context(nc.semaphore("tl_" + e)) for e in COMPUTE}
        dsem = {}
        for key in self.dcount:
            dsem[key] = es.enter_context(nc.semaphore(f"d_{key[0]}_{key[1]}"))
        cum = {}
        for e in COMPUTE:
            c = 0
            arr = []
            for o in self.ops[e]:
                if o["signal"]:
                    c += 1
                arr.append(c)
            cum[e] = arr
            assert c < 60000, (e, c)
        self.stats = {e: (len(self.ops[e]), cum[e][-1] if e in cum and cum[e] else 0) for e in self.ENGS}

        def emit(eng_name, eng):
            for o in self.ops[eng_name]:
                for tok in o["waits"]:
                    if tok[0] == "e":
                        eng.wait_ge(esem[tok[1]], cum[tok[1]][tok[2]])
                    else:
                        eng.wait_ge(dsem[tok[1]], tok[2])
                inst = o["fn"](eng)
                if o["dma"] is not None:
                    inst.then_inc(dsem[o["dma"]], 16)
                elif o["signal"]:
                    inst.then_inc(esem[eng_name], 1)
            if eng_name == "sp":
                for tok in fin:
                    eng.wait_ge(dsem[tok[1]], tok[2])

        with nc.Block() as block:
            @block.sync
            def _(e):
                emit("sp", e)

            @block.tensor
            def _(e):
                emit("pe", e)

            @block.scalar
            def _(e):
                emit("act", e)

            @block.vector
            def _(e):
                emit("dve", e)

            @block.gpsimd
            def _(e):
                emit("pool", e)


D = 1024
IN_W = 3088
DFF = 2816
NFF = DFF // 128
C_QA, C_KA, C_VA, C_QB, C_KB, C_VB, C_GL, C_R = 0, 512, 1024, 1536, 1792, 2048, 2560, 2576
EPS = 1e-6
NEG = -1e30
SLOTW = 528


class Cfg:
    def __init__(self, npt=32, pref_t=96, gt=4, ring=3, warm_y=0, warm_g=0):
        self.warm_y, self.warm_g = warm_y, warm_g
        self.npt = npt
        self.pref_t = pref_t
        self.gt = gt
        self.ring = ring
        assert npt % gt == 0 and pref_t % gt == 0


class Tile:
    def __init__(self, i, n, c0, kind):
        self.i, self.n, self.c0, self.kind = i, n, c0, kind

    @property
    def cs(self):
        return slice(self.c0, self.c0 + self.n)


class Builder:
    def __init__(self, cfg):
        self.cfg = cfg
        self.nc = bass.Bass("TRN2", target_bir_lowering=False)

    def dram_in(self, name, shape, dt=F32):
        return self.nc.dram_tensor(name, list(shape), dt, kind="ExternalInput").ap()

    def dram_out(self, name, shape, dt=F32):
        return self.nc.dram_tensor(name, list(shape), dt, kind="ExternalOutput").ap()

    def dram_tmp(self, name, shape, dt):
        return self.nc.dram_tensor(name, list(shape), dt).ap()

    def build(self):
        cfg = self.cfg
        with ExitStack() as es:
            self.P = Prog(self.nc, es)
            self.declare_io()
            self.alloc()
            import os
            stop = int(os.environ.get("KSTOP", "99"))
            self.setup_consts()
            if stop >= 2:
                self.convert_weights()
            self.make_schedule()
            if stop >= 3:
                self.mem_kv_prompt()
            if stop >= 4:
                self.halo_pass()
            if stop >= 5:
                self.prefix_pass()
            ng = cfg.npt // cfg.gt
            if stop >= 6:
                for g in range(ng):
                    tiles = [Tile(i, 128, i * 128, "p") for i in range(cfg.gt)]
                    self.group(tiles, first=(g == 0), last=(g == ng - 1), g=g)
            if stop >= 7:
                self.sample_setup()
                tiles = [Tile(i, 32, i * 32, "s") for i in range(2)]
                self.group(tiles, first=False, last=True, g=ng)
            self.P.finish()
        return self.nc

    def declare_io(self):
        c = self.cfg
        di, do = self.dram_in, self.dram_out
        self.xp = di("xp", [c.npt * 128, D])
        self.xh = di("xh", [512, D])
        self.xpre = di("xpre", [max(c.pref_t, 1) * 128, D])
        self.hmask = di("hmask", [1, 512])
        self.xs = di("xs", [64, D])
        self.memp = di("memp", [256, D])
        self.cak = di("cak", [2, 512, 512])
        self.cav = di("cav", [2, 512, 512])
        self.sgla = di("sgla", [2, 256, 128])
        self.cmk = di("cmk", [2, 256, D])
        self.cmv = di("cmv", [2, 256, D])
        self.bm_in = di("bm", [8, 128, 640])
        self.g = {n: di(n, [1, D]) for n in ("g_pre_mix", "g_post_mix", "g_pre_mem", "g_mem", "g_post_mem",
                                            "g_pre_ffn", "g_post_ffn")}
        self.g_gla = di("g_gla_out", [1, 512])
        self.w_alpha2 = di("w_alpha2", [16, 256])
        self.b_alpha = di("b_alpha", [1, 256])
        self.w = {"w_in": di("w_in", [D, IN_W]), "w_o": di("w_o", [D, D]), "w_mq": di("w_mq", [D, D]),
                  "w_mk": di("w_mk", [D, D]), "w_mv": di("w_mv", [D, D]), "w_mo": di("w_mo", [D, D]),
                  "w_g": di("w_ffn_gate", [D, DFF]), "w_u": di("w_ffn_up", [D, DFF]),
                  "w_d": di("w_ffn_down", [DFF, D])}
        self.wb = {k: self.dram_out(k + "_bf", list(v.shape), BF16) for k, v in self.w.items()}
        self.yp = do("yp", [c.npt * 128, D])
        self.ys = do("ys", [64, D])
        self.pak = do("pak", [512, 512])
        self.pav = do("pav", [512, 512])
        self.pgla = do("pgla", [256, 128])
        self.pmk = do("pmk", [256, D])
        self.pmv = do("pmv", [256, D])
        self.sak = do("sak", [64, 512])
        self.sav = do("sav", [64, 512])
        self.sglao = do("sglao", [2, 256, 128])

    def alloc(self):
        P, c = self.P, self.cfg
        sb, ps = P.sb, P.ps
        gt = c.gt
        NT = gt * 128
        self.NT = NT
        self.X = [sb(f"X{i}", [128, D], F32) for i in range(gt)]
        self.Y = [sb(f"Y{i}", [128, D], F32) for i in range(gt)]
        self.hT = sb("hT", [128, 8, NT], BF16)
        self.QT = sb("QT", [128, 8, NT], BF16)
        self.kaT = [sb(f"kaT{i}", [128, 4, NT], BF16) for i in range(2)]
        self.va = [sb(f"va{i}", [128, gt, 512], BF16) for i in range(2)]
        self.kb = sb("kb", [128, gt, 256], BF16)
        self.vb = sb("vb", [128, gt, 512], BF16)
        self.sp = sb("sp", [128, gt, 256], F32)
        self.glT = sb("glT", [33, NT], BF16)
        self.rT = sb("rT", [128, 4, NT], BF16)
        self.mixT = sb("mixT", [128, 8, NT], BF16)
        self.hidT = sb("hidT", [128, NFF, NT], BF16)
        self.Bm = sb("Bm", [128, 8, 640], BF16)
        self.gpost = sb("gpost", [128, D], F32)
        self.gpre = sb("gpre", [128, 4, 8], F32)
        self.ggla = sb("ggla", [128, 4], F32)
        self.waext = sb("waext", [33, 256], BF16)
        self.hm = sb("hm", [1, 512], BF16)
        self.memKT = sb("memKT", [128, 8, 256], BF16)
        self.memV = sb("memV", [128, 2, D], BF16)
        self.ident = sb("ident", [128, 128], BF16)
        self.Uf = sb("Uf", [128, 128], F32)
        self.SLf = sb("SLf", [128, 128], F32)
        self.Ub = sb("Ub", [128, 128], BF16)
        self.onesb = sb("onesb", [128, 128], BF16)
        self.onesf = sb("onesf", [128, 1], F32)
        self.S = [sb(f"S{p}", [128, 128], F32) for p in range(2)]
        self.Sbf = [[sb(f"Sbf{p}{hh}", [128, 128], BF16) for hh in range(2)] for p in range(2)]
        self.xsb = [sb(f"xsb{i}", [128, D], BF16) for i in range(4)]
        self.st = [sb(f"st{i}", [128, 8], F32) for i in range(4)]
        self.Pb = [sb(f"Pb{i}", [128, 1024], BF16) for i in range(2)]
        self.Pn = self.Pb
        self.PT = [sb(f"PT{i}", [128, 8, 128], BF16) for i in range(2)]
        self.rs8 = [sb(f"rs8{i}", [128, 8], F32) for i in range(2)]
        self.ri8 = [sb(f"ri8{i}", [128, 8], F32) for i in range(2)]
        self.Ep = [sb(f"Ep{p}", [128, 128], F32) for p in range(2)]
        self.Em = [sb(f"Em{p}", [128, 128], F32) for p in range(2)]
        self.qtl = [sb(f"qtl{p}", [128, 128], BF16) for p in range(2)]
        self.ktl = [[sb(f"ktl{p}{hh}", [128, 128], BF16) for hh in range(2)] for p in range(2)]
        self.kdw = sb("kdw", [128, 256], F32)
        self.kdec = sb("kdec", [128, 256], BF16)
        self.kdec4 = [self.kdec] + [sb(f"kdec{i}", [128, 256], BF16) for i in range(1, 4)]
        self.dec8 = sb("dec8", [128, 8], F32)
        self.dec = [sb(f"dec{p}", [128, 1], F32) for p in range(2)]
        self.ATm = [sb(f"ATm{i}", [128, 128], BF16) for i in range(4)]
        self.sq = [sb(f"sq{i}", [128, 128], BF16) for i in range(4)]
        self.rs = [sb(f"rs{i}", [128, 128], F32) for i in range(4)]
        self.t1 = [sb(f"t1{i}", [128, 128], F32) for i in range(4)]
        self.silu = [sb(f"silu{i}", [128, NT], BF16) for i in range(2)]
        self.ring = [sb(f"ring{i}", [128, 8, SLOTW], BF16) for i in range(c.ring)]
        self.ckT = self.kaT
        self.cv = self.va
        self.kaTs = sb("kaTs", [128, 4, 64], BF16)
        self.vas = [sb(f"vas{b}", [32, 512], BF16) for b in range(2)]
        self.memKTs = [self.memKT, sb("memKTs1", [128, 8, 256], BF16)]
        self.memVs = [self.memV, sb("memVs1", [128, 2, D], BF16)]
        self.A = [ps(f"psA{i}", [128, 1024], F32) for i in range(2)]
        self.B = [ps(f"psB{i}", [128, 512], F32) for i in range(4)]
        self.rr = {}
        self.init_psum_regions()

    def rot(self, key, n):
        v = self.rr.get(key, 0)
        self.rr[key] = v + 1
        return v % n

    def mm(self, out, lhsT, rhs, start, stop, reads, writes):
        self.P.op("pe", lambda e: e.matmul(out, lhsT, rhs, start=start, stop=stop), reads, writes)

    def tr(self, out, in_, n, reads, writes):
        idn = self.ident.t[:n, :n]
        self.P.op("pe", lambda e: e.transpose(out, in_, idn), list(reads) + [self.ident], writes)

    def act(self, out, in_, func, reads, writes, **kw):
        self.P.op("act", lambda e: e.activation(out, in_, func, **kw), reads, writes)

    def copy(self, eng, out, in_, reads, writes):
        if eng == "act":
            self.P.op("act", lambda e: e.copy(out, in_), reads, writes)
        else:
            self.P.op(eng, lambda e: e.tensor_copy(out, in_), reads, writes)

    def tt(self, eng, out, a, b, op, reads, writes):
        self.P.op(eng, lambda e: e.tensor_tensor(out, a, b, op), reads, writes)

    def ts(self, eng, out, a, s1, op0, reads, writes, s2=None, op1=None):
        if op1 is None:
            self.P.op(eng, lambda e: e.tensor_scalar(out, a, s1, None, op0), reads, writes)
        else:
            self.P.op(eng, lambda e: e.tensor_scalar(out, a, s1, s2, op0, op1), reads, writes)

    def stt(self, eng, out, a, s, b, op0, op1, reads, writes):
        self.P.op(eng, lambda e: e.scalar_tensor_tensor(out, a, s, b, op0, op1), reads, writes)

    def memset(self, eng, ap, val, writes):
        self.P.op(eng, lambda e: e.memset(ap, val), [], writes)

    def rstd(self, dst, src, inv_n, reads, writes):
        self.act(dst, src, AF.Ln, list(reads) + [self.epsc], writes, scale=inv_n, bias=self.epsc.t[:dst.shape[0], 0:1])
        self.act(dst, dst, AF.Exp, writes, writes, scale=-0.5)

    def setup_consts(self):
        P = self.P
        self.epsc = P.sb("epsc", [128, 1], F32)
        self.memset("pool", self.epsc.t[:], EPS, [self.epsc])
        tmpf = P.sb("tmpf", [128, 128], F32)
        self.memset("pool", tmpf.t[:], 1.0, [tmpf])
        P.op("pool", lambda e: e.affine_select(tmpf.t[:], tmpf.t[:], [[-1, 128]], ALU.is_equal, 0.0,
                                              base=0, channel_multiplier=1), [tmpf], [tmpf])
        self.copy("dve", self.ident.t[:], tmpf.t[:], [tmpf], [self.ident])
        self.memset("pool", self.Uf.t[:], 1.0, [self.Uf])
        P.op("pool", lambda e: e.affine_select(self.Uf.t[:], self.Uf.t[:], [[1, 128]], ALU.is_ge, 0.0,
                                              base=0, channel_multiplier=-1), [self.Uf], [self.Uf])
        self.copy("dve", self.Ub.t[:], self.Uf.t[:], [self.Uf], [self.Ub])
        self.memset("pool", self.SLf.t[:], 1.0, [self.SLf])
        P.op("pool", lambda e: e.affine_select(self.SLf.t[:], self.SLf.t[:], [[-1, 128]], ALU.is_gt, 0.0,
                                              base=0, channel_multiplier=1), [self.SLf], [self.SLf])
        self.memset("pool", self.onesb.t[:], 1.0, [self.onesb])
        self.memset("pool", self.onesf.t[:], 1.0, [self.onesf])
        for p in range(2):
            self.memset("pool", self.S[p].t[:], 0.0, [self.S[p]])
            for hh in range(2):
                self.memset("pool", self.Sbf[p][hh].t[:], 0.0, [self.Sbf[p][hh]])
                self.memset("pool", self.ktl[p][hh].t[:], 0.0, [self.ktl[p][hh]])
        self.identf = tmpf
        g8 = P.sb("g8", [8, 5, 128], F32)
        for i, n in enumerate(("g_pre_mix", "g_pre_mem", "g_pre_ffn", "g_mem")):
            P.dma("sp", g8.t[:, i, :], self.g[n].rearrange("o (c p) -> (o c) p", p=128), [], [g8])
        P.dma("sp", g8.t[0:4, 4, :], self.g_gla.rearrange("o (c p) -> (o c) p", p=128), [], [g8])
        pg = self.B[0].t[:, 0:40]
        for i in range(5):
            nr = 8 if i < 4 else 4
            P.op("pe", lambda e, i=i, nr=nr: e.transpose(pg[:, i * 8:i * 8 + nr], g8.t[0:nr, i, :], tmpf.t[0:nr, 0:nr]),
                 [g8, tmpf], self.Bq[0])
        self.copy("dve", self.gpre.t[:, :, :], pg[:, 0:32].rearrange("p (a c) -> p a c", a=4), self.Bq[0], [self.gpre])
        self.copy("dve", self.ggla.t[:, :], pg[:, 32:36], self.Bq[0], [self.ggla])
        self.memset("pool", self.waext.t[:], 0.0, [self.waext])
        P.dma("pool", self.waext.t[0:16, :], self.w_alpha2[:, :], [], [self.waext])
        P.dma("pool", self.waext.t[32:33, :], self.b_alpha[:, :], [], [self.waext])
        self.memset("pool", self.glT.t[0:32, :], 0.0, [self.glT])
        self.memset("pool", self.glT.t[32:33, :], 1.0, [self.glT])
        P.dma("pool", self.Bm.t[:], self.bm_in.rearrange("h p j -> p h j"), [], [self.Bm])
        P.dma("pool", self.hm.t[:], self.hmask[:, :], [], [self.hm])

    def load_gpost(self, name):
        src = self.g[name][0:1, :].partition_broadcast(128)
        self.P.dma("sp", self.gpost.t[:], src[:, 0, :], [], [self.gpost])

    def convert_weights(self):
        self.conv_todo = []
        for k in ["w_mk", "w_mv", "w_in", "w_o", "w_mq", "w_mo", "w_g", "w_u", "w_d"]:
            rows = self.w[k].shape[0]
            for r0 in range(0, rows, 256):
                self.conv_todo.append((k, r0, min(rows, r0 + 256)))
        n_now = 16 if self.cfg.pref_t else len(self.conv_todo)
        self.convert_more(n_now)

    def convert_more(self, n, after=()):
        for _ in range(n):
            if not self.conv_todo:
                return
            k, r0, r1 = self.conv_todo.pop(0)
            self.P.dma("pool", self.wb[k][r0:r1, :], self.w[k][r0:r1, :], list(after), [], dram_w=[f"{k}:{r0 // 256}"])

    def warm(self, n):
        for _ in range(n):
            self.mm(self.B[1].t[:, 0:512], self.ident.t[:, :], self.Bm.t[:, 0, 0:512], True, True,
                    [self.ident, self.Bm], self.Bq[1])

    def make_schedule(self):
        c = self.cfg
        W8 = lambda k, c0, n: (k, 0, 8, [(c0, n, 0)])
        halo = [W8("w_in", C_KA, 512), W8("w_in", C_VA, 512)]
        pref = [("w_in", 0, 8, [(C_KB, 256, 0), (C_GL, 16, 256)]), W8("w_in", C_VB, 512)]
        memkv = [W8("w_mk", 0, 512), W8("w_mk", 512, 512), W8("w_mv", 0, 512), W8("w_mv", 512, 512)]
        grp = [W8("w_in", C_QA, 512), W8("w_in", C_KA, 512), W8("w_in", C_VA, 512), W8("w_in", C_QB, 512),
               W8("w_in", C_VB, 512), W8("w_in", C_GL, 528),
               W8("w_o", 0, 512), W8("w_o", 512, 512), W8("w_mq", 0, 512), W8("w_mq", 512, 512),
               W8("w_mo", 0, 512), W8("w_mo", 512, 512)]
        for j in range(6):
            n = 512 if j < 5 else 256
            grp += [W8("w_g", j * 512, n), W8("w_u", j * 512, n)]
        for cc in range(2):
            for kg in range(3):
                nk = 8 if kg < 2 else 6
                grp.append(("w_d", kg * 1024, nk, [(cc * 512, 512, 0)]))
        ng = c.npt // c.gt + 1
        self.sched = memkv + halo + (pref if c.pref_t else []) + grp * ng
        self.r_cur = 0
        self.r_loaded = 0
        self.r_rel = set()

    def _pump(self):
        n = len(self.ring)
        while (self.r_loaded < len(self.sched) and self.r_loaded < self.r_cur + n
               and (self.r_loaded < n or (self.r_loaded - n) in self.r_rel)):
            j = self.r_loaded
            key, row0, nk, pieces = self.sched[j]
            slot = self.ring[j % n]
            for (c0, ncol, s0) in pieces:
                src = self.wb[key][row0:row0 + nk * 128, c0:c0 + ncol].rearrange("(c p) f -> p c f", p=128)
                blks = [f"{key}:{b}" for b in range(row0 // 256, (row0 + nk * 128 + 255) // 256)]
                self.P.dma("sp", slot.t[:, 0:nk, s0:s0 + ncol], src, [], [slot], dram_r=blks)
            self.r_loaded += 1

    def wget(self, expect=None):
        j = self.r_cur
        self.r_cur += 1
        self._pump()
        assert self.r_loaded > j, ("ring stall", j)
        if expect is not None:
            assert self.sched[j][0] == expect, (self.sched[j], expect)
        return j, self.ring[j % len(self.ring)]

    def wrel(self, j):
        self.r_rel.add(j)
        self._pump()

    def init_psum_regions(self):
        self.Aq = []
        for a in self.A:
            lo, hi = a.sub(2)
            self.Aq.append([lo, lo, hi, hi])
        self.Bq = [[b, b, b, b] for b in self.B]

    def Ahalf(self, i):
        a, h = i // 2, i % 2
        return self.A[a].t[:, h * 512:(h + 1) * 512], self.Aq[a][2 * h:2 * h + 2]

    def ev_eng(self):
        return ("act", "dve")[self.rot("ev", 2)]

    def evac(self, out, in_, reads, writes, scale=None, eng=None):
        eng = eng or self.ev_eng()
        if scale is None:
            self.copy(eng, out, in_, reads, writes)
        elif eng == "act":
            self.act(out, in_, AF.Copy, reads, writes, scale=scale)
        else:
            self.ts("dve", out, in_, scale, ALU.mult, reads, writes)

    def prenorm(self, X, n, gi, dst, cs):
        self.prenorm_many([(X, n, cs)], gi, dst)

    def prenorm_many(self, items, gi, dst):
        self.prenorm_B(items, gi, dst, self.prenorm_A(items))

    def prenorm_A(self, items):
        return self.prenorm_A2(items, self.prenorm_A1(items))

    def prenorm_A1(self, items):
        sts = [self.st[self.rot("st", 4)] for _ in items]
        xbs = [self.xsb[self.rot("xsb", 4)] for _ in items]
        for (X, n, cs), st, xb in zip(items, sts, xbs):
            self.act(xb.t[:n, :], X.t[:n, :], AF.Square, [X, st], [xb, st], accum_out=st.t[:n, 0:1])
        for (X, n, cs), st in zip(items, sts):
            self.act(st.t[:n, 1:2], st.t[:n, 0:1], AF.Ln, [st, self.epsc], [st], scale=1.0 / D, bias=self.epsc.t[:n, 0:1])
        for (X, n, cs), st in zip(items, sts):
            self.act(st.t[:n, 1:2], st.t[:n, 1:2], AF.Exp, [st], [st], scale=-0.5)
        return (sts, xbs)

    def prenorm_A2(self, items, sx):
        sts, xbs = sx
        for (X, n, cs), st, xb in zip(items, sts, xbs):
            self.ts("dve", xb.t[:n, :], X.t[:n, :], st.t[:n, 1:2], ALU.mult, [X, st], [xb])
        return xbs

    def prenorm_B(self, items, gi, dst, xbs):
        for (X, n, cs), xb in zip(items, xbs):
            bi = 2 + self.rot("trb", 2)
            ptv = self.B[bi].t[:].bitcast(BF16).rearrange("p (c t) -> p c t", c=8)
            for c in range(8):
                self.tr(ptv[:, c, 0:n], xb.t[:n, c * 128:(c + 1) * 128], n, [xb], self.Bq[bi])
            if gi is None:
                self.evac(dst.t[:, :, cs], ptv[:, :, 0:n], self.Bq[bi], [dst])
            else:
                gb = self.gpre.t[:, gi, :].unsqueeze(2).to_broadcast([128, 8, n])
                self.tt("dve", dst.t[:, :, cs], ptv[:, :, 0:n], gb, ALU.mult, self.Bq[bi] + [self.gpre], [dst])

    def norm_transition(self, tiles, gi):
        items = [(self.X[t.i], t.n, t.cs) for t in tiles]
        pairs = [(tiles[i:i + 2], items[i:i + 2]) for i in range(0, len(tiles), 2)]
        for tp, _ in pairs:
            self.postnorm_many(tp)
        xbs = [self.prenorm_A(ip) for _, ip in pairs]
        for (_, ip), xb in zip(pairs, xbs):
            self.prenorm_B(ip, gi, self.hT, xb)

    def postnorm_many(self, tiles):
        sts = [self.st[self.rot("st", 4)] for _ in tiles]
        for t, st in zip(tiles, sts):
            Y = self.Y[t.i]
            jb = self.xsb[self.rot("xsb", 4)]
            self.act(jb.t[:t.n, :], Y.t[:t.n, :], AF.Square, [Y, st], [jb, st], accum_out=st.t[:t.n, 0:1])
        for t, st in zip(tiles, sts):
            self.act(st.t[:t.n, 1:2], st.t[:t.n, 0:1], AF.Ln, [st, self.epsc], [st], scale=1.0 / D, bias=self.epsc.t[:t.n, 0:1])
        for t, st in zip(tiles, sts):
            self.act(st.t[:t.n, 1:2], st.t[:t.n, 1:2], AF.Exp, [st], [st], scale=-0.5)
        for ix, (t, st) in enumerate(zip(tiles, sts)):
            Y = self.Y[t.i]
            eng = "pool" if ix == len(tiles) - 1 and len(tiles) > 2 else "dve"
            self.tt(eng, Y.t[:t.n, :], Y.t[:t.n, :], self.gpost.t[:t.n, :], ALU.mult, [Y, self.gpost], [Y])
        for t, st in zip(tiles, sts):
            X, Y = self.X[t.i], self.Y[t.i]
            self.stt("dve", X.t[:t.n, :], Y.t[:t.n, :], st.t[:t.n, 1:2], X.t[:t.n, :], ALU.mult, ALU.add, [Y, st, X], [X])

    def proj_fm(self, slot, scol0, m, srcT, ncols, evac_fn):
        bi = self.rot("fmb", 2)
        ps = self.B[bi].t[:m, 0:ncols]
        for k in range(8):
            self.mm(ps, slot.t[:, k, scol0:scol0 + m], srcT.t[:, k, 0:ncols], k == 0, k == 7,
                    [slot, srcT], self.Bq[bi])
        evac_fn(ps, self.Bq[bi])

    def proj_tm(self, slot, scol0, ncol, srcT, t, evac_fn, nk=8):
        ap, bufs = self.Ahalf(self.rot("tmb", 4))
        ps = ap[:t.n, 0:ncol]
        for k in range(nk):
            self.mm(ps, srcT.t[:, k, t.cs], slot.t[:, k, scol0:scol0 + ncol], k == 0, k == nk - 1,
                    [slot, srcT], bufs)
        evac_fn(ps, bufs)

    def stage_win(self, tiles, last, kaT_cur, va_cur):
        P = self.P
        sample = tiles[0].kind == "s"
        NTc = sum(t.n for t in tiles)
        hT, QT = self.hT, self.QT
        j, slot = self.wget("w_in")
        for c in range(4):
            self.proj_fm(slot, c * 128, 128, hT, NTc,
                         lambda ps, pb, c=c: self.evac(QT.t[:, c, 0:NTc], ps, pb, [QT], scale=0.125))
        self.wrel(j)
        j, slot = self.wget("w_in")
        kdst = self.kaTs if sample else kaT_cur
        for c in range(4):
            self.proj_fm(slot, c * 128, 128, hT, NTc,
                         lambda ps, pb, c=c: self.evac(kdst.t[:, c, 0:NTc], ps, pb, [kdst]))
        if last:
            for t in tiles:
                Y = self.Y[t.i]
                self.proj_tm(slot, 0, 512, hT, t,
                             lambda ps, pb, t=t, Y=Y: self.evac(Y.t[:t.n, 0:512], ps, pb, [Y]))
        self.wrel(j)
        j, slot = self.wget("w_in")
        for t in tiles:
            vdst = self.vas[t.i].t[:t.n, :] if sample else va_cur.t[:t.n, t.i, :]
            vbuf = self.vas[t.i] if sample else va_cur
            Y = self.Y[t.i]

            def ev(ps, pb, t=t, vdst=vdst, vbuf=vbuf, Y=Y):
                self.evac(vdst, ps, pb, [vbuf])
                if last:
                    self.evac(Y.t[:t.n, 512:1024], ps, pb, [Y])
            self.proj_tm(slot, 0, 512, hT, t, ev)
        self.wrel(j)
        if last:
            for t in tiles:
                Y = self.Y[t.i]
                if sample:
                    dk, dv = self.sak[t.c0:t.c0 + t.n, :], self.sav[t.c0:t.c0 + t.n, :]
                else:
                    dk, dv = self.pak[t.c0:t.c0 + t.n, :], self.pav[t.c0:t.c0 + t.n, :]
                P.dma("pool", dk, Y.t[:t.n, 0:512], [Y], [], out=True)
                P.dma("pool", dv, Y.t[:t.n, 512:1024], [Y], [], out=True)
        j, slot = self.wget("w_in")
        for c in range(4):
            sc = 0.125 if c < 2 else None
            self.proj_fm(slot, c * 128, 128, hT, NTc,
                         lambda ps, pb, c=c, sc=sc: self.evac(QT.t[:, 4 + c, 0:NTc], ps, pb, [QT], scale=sc))
        for t in tiles:
            self.proj_tm(slot, 256, 256, hT, t,
                         lambda ps, pb, t=t: self.evac(self.kb.t[:t.n, t.i, :], ps, pb, [self.kb]))
        self.wrel(j)
        j, slot = self.wget("w_in")
        for t in tiles:
            self.proj_tm(slot, 0, 512, hT, t,
                         lambda ps, pb, t=t: self.evac(self.vb.t[:t.n, t.i, :], ps, pb, [self.vb]))
        self.wrel(j)
        j, slot = self.wget("w_in")
        self.proj_fm(slot, 0, 16, hT, NTc,
                     lambda ps, pb: self.evac(self.glT.t[0:16, 0:NTc], ps, pb, [self.glT]))
        for c in range(4):
            self.proj_fm(slot, 16 + c * 128, 128, hT, NTc,
                         lambda ps, pb, c=c: self.act(self.rT.t[:, c, 0:NTc], ps, AF.Silu, pb, [self.rT]))
        self.wrel(j)
        self.softplus_tiles(tiles)

    def softplus_tiles(self, tiles, glT=None, sp=None, sp_ap=None):
        glT = glT or self.glT
        sp = sp or self.sp
        spt = sp_ap if sp_ap is not None else sp.t
        glt = glT.t if glT.t is not None else self.glT1_ap
        pss = []
        for t in tiles:
            ap, bufs = self.Ahalf(self.rot("tmb", 4))
            ps = ap[:t.n, 0:256]
            self.mm(ps, glt[0:33, t.cs], self.waext.t[0:33, :], True, True, [glT, self.waext], bufs)
            pss.append((ps, bufs))
        for t, (ps, bufs) in zip(tiles, pss):
            self.act(spt[:t.n, t.i, :], ps, AF.Exp, bufs, [sp], scale=-1.0)
        for t in tiles:
            dst = spt[:t.n, t.i, :]
            self.act(dst, dst, AF.Ln, [sp, self.onesf], [sp], bias=self.onesf.t[:t.n, 0:1])

    def gla_stages(self, t):
        C = t.n
        sp, kb, vb, QT = self.sp, self.kb, self.vb, self.QT
        B0, B1 = self.B[0].t, self.B[1].t
        q0, q1 = self.Bq[0], self.Bq[1]

        def g0():
            suf = B0[:C, 256:512]
            self.mm(suf, self.SLf.t[:C, :C], sp.t[:C, t.i, :], True, True, [self.SLf, sp], q0)
            csts = []
            for p in range(2):
                cst = B0[:, p * 128:p * 128 + C]
                self.mm(cst, sp.t[:C, t.i, p * 128:(p + 1) * 128], self.Uf.t[:C, :C], True, True, [sp, self.Uf], q0)
                csts.append(cst)
            self.act(self.kdw.t[:C, :], suf, AF.Exp, q0, [self.kdw], scale=-1.0 / 16)
            for p in range(2):
                self.act(self.Ep[p].t[:, :C], csts[p], AF.Exp, q0, [self.Ep[p]], scale=-1.0 / 16)
                self.act(self.Em[p].t[:, :C], csts[p], AF.Exp, q0, [self.Em[p]], scale=1.0 / 16)
            self.tt("dve", self.kdec.t[:C, :], kb.t[:C, t.i, :], self.kdw.t[:C, :], ALU.mult, [kb, self.kdw], [self.kdec])
            for p in range(2):
                self.tt("dve", self.qtl[p].t[:, :C], QT.t[:, 4 + p, t.cs], self.Ep[p].t[:, :C], ALU.mult,
                        [QT, self.Ep[p]], [self.qtl[p]])
                for hh in range(2):
                    r = slice(hh * 64, hh * 64 + 64)
                    self.tt("dve", self.ktl[p][hh].t[r, :C], QT.t[r, 6 + p, t.cs], self.Em[p].t[r, :C], ALU.mult,
                            [QT, self.Em[p]], [self.ktl[p][hh]])
                self.copy("dve", self.dec[p].t[:, 0:1], self.Ep[p].t[:, C - 1:C], [self.Ep[p]], [self.dec[p]])

        def g1():
            for h in range(4):
                p, r = h // 2, slice((h % 2) * 64, (h % 2) * 64 + 64)
                self.mm(B1[:C, h * 128:h * 128 + C], self.ktl[p][h % 2].t[:, :C], self.qtl[p].t[:, :C], True, True,
                        [self.ktl[p][h % 2], self.qtl[p]], q1)
            for h in range(4):
                self.tt("dve", self.ATm[h].t[:C, :C], B1[:C, h * 128:h * 128 + C], self.Ub.t[:C, :C], ALU.mult,
                        q1 + [self.Ub], [self.ATm[h]])

        def g2():
            for h in range(4):
                p, r = h // 2, slice((h % 2) * 64, (h % 2) * 64 + 64)
                oT = B0[:, h * 128:h * 128 + C]
                self.mm(oT, vb.t[:C, t.i, h * 128:(h + 1) * 128], self.ATm[h].t[:C, :C], True, False, [vb, self.ATm[h]], q0)
                self.mm(oT, self.Sbf[p][h % 2].t[:, :], self.qtl[p].t[:, :C], False, True, [self.Sbf[p][h % 2], self.qtl[p]], q0)
            for h in range(4):
                self.act(self.sq[h].t[:, :C], B0[:, h * 128:h * 128 + C], AF.Square, q0, [self.sq[h]])

        def g3():
            for h in range(4):
                self.mm(B1[:, h * 128:h * 128 + C], self.onesb.t[:, :], self.sq[h].t[:, :C], True, True,
                        [self.onesb, self.sq[h]], q1)
            for h in range(4):
                self.act(self.rs[h].t[:, :C], B1[:, h * 128:h * 128 + C], AF.Ln, q1 + [self.epsc], [self.rs[h]], scale=1.0 / 128,
                         bias=self.epsc.t[:, 0:1])
            for h in range(4):
                self.act(self.rs[h].t[:, :C], self.rs[h].t[:, :C], AF.Exp, [self.rs[h]], [self.rs[h]], scale=-0.5)
            for h in range(4):
                self.stt("dve", self.t1[h].t[:, :C], B0[:, h * 128:h * 128 + C], self.ggla.t[:, h:h + 1],
                         self.rs[h].t[:, :C], ALU.mult, ALU.mult, q0 + [self.ggla, self.rs[h]], [self.t1[h]])
            for h in range(4):
                self.tt("dve", self.mixT.t[:, 4 + h, t.cs], self.t1[h].t[:, :C], self.rT.t[:, h, t.cs], ALU.mult,
                        [self.t1[h], self.rT], [self.mixT])

        def g4():
            for p in range(2):
                ds = B0[:, p * 256:(p + 1) * 256]
                self.mm(ds, self.kdec.t[:C, p * 128:(p + 1) * 128], vb.t[:C, t.i, p * 256:(p + 1) * 256], True, True,
                        [self.kdec, vb], q0)
            for p in range(2):
                S = self.S[p]
                for hh in range(2):
                    r = slice(hh * 64, hh * 64 + 64)
                    self.stt("dve", S.t[r, :], S.t[r, :], self.dec[p].t[r, 0:1],
                             B0[r, p * 256 + hh * 128:p * 256 + (hh + 1) * 128],
                             ALU.mult, ALU.add, [S, self.dec[p]] + q0, [S])
                self.copy_Sbf(p)

        return [g0, g1, g2, g3, g4]

    def prefix_sufs(self, tiles, bset):
        kb_ap, vb_ap, sp_ap, kbB, vbB, spB = bset
        regs = []
        for t in tiles:
            ap, bufs = self.Ahalf(t.i)
            self.mm(ap[:, 0:256], self.SLf.t[:, :], sp_ap[:, t.i, :], True, True, [self.SLf, spB], bufs)
            for p in range(2):
                self.mm(ap[:, 256 + p:257 + p], sp_ap[:, t.i, p * 128:(p + 1) * 128], self.onesf.t[:, 0:1], True, True,
                        [spB, self.onesf], bufs)
            regs.append((ap, bufs))
        for t, (ap, bufs) in zip(tiles, regs):
            Y = self.Y[t.i]
            self.act(Y.t[:, 0:256], ap[:, 0:256], AF.Exp, bufs, [Y], scale=-1.0 / 16)
            self.act(self.dec8.t[:, 2 * t.i:2 * t.i + 2], ap[:, 256:258], AF.Exp, bufs, [self.dec8], scale=-1.0 / 16)
        for t in tiles:
            Y = self.Y[t.i]
            self.tt("dve", self.kdec4[t.i].t[:, :], kb_ap[:, t.i, :], Y.t[:, 0:256], ALU.mult, [kbB, Y], [self.kdec4[t.i]])

    def prefix_ds(self, tiles, bset):
        kb_ap, vb_ap, sp_ap, kbB, vbB, spB = bset
        for t in tiles:
            kd = self.kdec4[t.i]
            for p in range(2):
                bi = self.rot("pfx", 2)
                ds = self.B[bi].t[:, 0:256]
                self.mm(ds, kd.t[:, p * 128:(p + 1) * 128], vb_ap[:, t.i, p * 256:(p + 1) * 256], True, True,
                        [kd, vbB], self.Bq[bi])
                self.stt("dve", self.S2[p], self.S2[p], self.dec8.t[:, 2 * t.i + p:2 * t.i + p + 1], ds,
                         ALU.mult, ALU.add, [self.S2b[p], self.dec8] + self.Bq[bi], [self.S2b[p]])

    def attn_S(self, nq, h, qT_ap, qbufs, ksegs, bias_ap, extra_bias):
        ai = self.rot("attA", 2)
        A, Aq = self.A[ai].t, self.Aq[ai]
        nk = sum(n for _, n, _ in ksegs)
        sbufs = Aq[0:3]
        ops = []
        for c0 in range(0, nk, 512):
            c1 = min(nk, c0 + 512)
            ops.append((c0, c1, self.ident.t[:nq, :nq], bias_ap[:, c0:c1], [self.ident, self.Bm]))
        if extra_bias is not None:
            l_ap, r_ap, n_e, ebufs = extra_bias
            ops.append((0, n_e, l_ap, r_ap, ebufs))
        col = 0
        for (k_ap, n, kbufs) in ksegs:
            o = 0
            while o < n:
                e = min(n, o + (512 - (col + o) % 512))
                ops.append((col + o, col + e, qT_ap, k_ap[:, o:e], qbufs + kbufs))
                o = e
            col += n
        for bank in range(2):
            bops = [x for x in ops if x[0] // 512 == bank]
            for ix, (c0, c1, l_ap, r_ap, rb) in enumerate(bops):
                self.mm(A[:nq, c0:c1], l_ap, r_ap, ix == 0, ix == len(bops) - 1, rb, sbufs)
        bi = self.rot("attP", 2)
        Pb, rs8, ri8 = self.Pb[bi], self.rs8[bi], self.ri8[bi]
        self.act(Pb.t[:nq, 0:nk], A[:nq, 0:nk], AF.Exp, sbufs + [rs8], [Pb, rs8], accum_out=rs8.t[:nq, 0:1])
        self.P.op("dve", lambda e: e.reciprocal(ri8.t[:nq, 0:1], rs8.t[:nq, 0:1]), [rs8], [ri8])
        self.ts("dve", Pb.t[:nq, 0:nk], Pb.t[:nq, 0:nk], ri8.t[:nq, 0:1], ALU.mult, [Pb, ri8], [Pb])
        return {"ai": ai, "bi": bi, "nk": nk, "h": h, "nq": nq}

    def attn_rest(self, ctx, vblocks, out_ap, out_buf):
        self.attn_T(ctx, vblocks)
        self.attn_PV(ctx, vblocks, out_ap, out_buf)

    def attn_T(self, ctx, vblocks):
        nq, bi = ctx["nq"], ctx["bi"]
        Pn, PT = self.Pb[bi], self.PT[bi]
        ti = 2 + self.rot("trb", 2)
        ptv = self.B[ti].t[:].bitcast(BF16).rearrange("p (c t) -> p c t", c=8)
        col = 0
        for bix, (_, n, _) in enumerate(vblocks):
            self.tr(ptv[:n, bix, 0:nq], Pn.t[:nq, col:col + n], nq, [Pn], self.Bq[ti])
            col += n
        nb = len(vblocks)
        nmax = max(n for _, n, _ in vblocks)
        self.evac(PT.t[:nmax, 0:nb, 0:nq], ptv[:nmax, 0:nb, 0:nq], self.Bq[ti], [PT])

    def attn_PV(self, ctx, vblocks, out_ap, out_buf):
        nq, h, ai, bi = ctx["nq"], ctx["h"], ctx["ai"], ctx["bi"]
        A, Aq = self.A[ai].t, self.Aq[ai]
        PT = self.PT[bi]
        r = slice((h % 2) * 64, (h % 2) * 64 + 64)
        nb = len(vblocks)
        o_ps = A[:, 768:768 + nq]
        for bix, (v_ap, n, vbufs) in enumerate(vblocks):
            self.mm(o_ps, v_ap, PT.t[:n, bix, 0:nq], bix == 0, bix == nb - 1, vbufs + [PT], [Aq[3]])
        self.evac(out_ap, A[r, 768:768 + nq], [Aq[3]], [out_buf])

    def mix_tile(self, t, head_args):
        import os
        km = int(os.environ.get("KM", "0"))
        stages = self.gla_stages(t)
        if km == 1:
            stages = []
        args = [head_args(h) for h in range(8)]
        if km == 2:
            for h in range(8):
                ctx = self.attn_S(*args[h][0])
                self.attn_rest(ctx, args[h][1], args[h][2], self.mixT)
                if h < len(stages):
                    stages[h]()
            return
        if km >= 4:
            stages = stages[:km - 3]
        if km >= 3:
            for h in range(8):
                ctx = self.attn_S(*args[h][0])
                self.attn_rest(ctx, args[h][1], args[h][2], self.mixT)
            for st in stages:
                st()
            return
        ctx = self.attn_S(*args[0][0])
        pend = None
        for h in range(8):
            nxt = self.attn_S(*args[h + 1][0]) if h + 1 < 8 else None
            self.attn_T(ctx, args[h][1])
            if pend is not None:
                self.attn_PV(pend[0], pend[1], pend[2], self.mixT)
            pend = (ctx, args[h][1], args[h][2])
            if h < len(stages):
                stages[h]()
            ctx = nxt
        self.attn_PV(pend[0], pend[1], pend[2], self.mixT)

    def proj_to_Y(self, srcT, wkey, tiles, kgroups=(8,)):
        for cc in range(2):
            accs = [self.Ahalf(t.i) for t in tiles]
            nkg = len(kgroups)
            kbase = 0
            for gi, nk in enumerate(kgroups):
                j, slot = self.wget(wkey)
                for t, (ap, bufs) in zip(tiles, accs):
                    for k in range(nk):
                        self.mm(ap[:t.n, :], srcT.t[:, kbase + k, t.cs], slot.t[:, k, 0:512],
                                gi == 0 and k == 0, gi == nkg - 1 and k == nk - 1, [srcT, slot], bufs)
                self.wrel(j)
                kbase += nk
            for t, (ap, bufs) in zip(tiles, accs):
                Y = self.Y[t.i]
                self.evac(Y.t[:t.n, cc * 512:(cc + 1) * 512], ap[:t.n, :], bufs, [Y])
        self.warm(self.cfg.warm_y)

    def mem_S(self, t, KT):
        n = t.n
        QT = self.QT
        ai = self.rot("attA", 2)
        A, Aq = self.A[ai].t, self.Aq[ai]
        for h in range(4):
            for kk in range(2):
                self.mm(A[:n, h * 256:(h + 1) * 256], QT.t[:, 2 * h + kk, t.cs], KT.t[:, 2 * h + kk, :],
                        kk == 0, kk == 1, [QT, KT], [Aq[h]])
        bi = self.rot("attP", 2)
        Pb, rs8, ri8 = self.Pb[bi], self.rs8[bi], self.ri8[bi]
        for h in range(4):
            self.act(Pb.t[:n, h * 256:(h + 1) * 256], A[:n, h * 256:(h + 1) * 256], AF.Exp, [Aq[h], rs8], [Pb, rs8],
                     scale=1.0 / 16, accum_out=rs8.t[:n, h:h + 1])
        self.P.op("dve", lambda e: e.reciprocal(ri8.t[:n, 0:4], rs8.t[:n, 0:4]), [rs8], [ri8])
        rb = ri8.t[:n, 0:4].unsqueeze(2).to_broadcast([n, 4, 256])
        pv = Pb.t[:n, :].rearrange("p (h k) -> p h k", h=4)
        self.tt("dve", pv, pv, rb, ALU.mult, [Pb, ri8], [Pb])
        return {"bi": bi, "t": t}

    def mem_T(self, ctx):
        t, bi = ctx["t"], ctx["bi"]
        n = t.n
        Pn, PT = self.Pb[bi], self.PT[bi]
        ti = 2 + self.rot("trb", 2)
        ptv = self.B[ti].t[:].bitcast(BF16).rearrange("p (c t) -> p c t", c=8)
        for h in range(4):
            for kb in range(2):
                c0 = h * 256 + kb * 128
                self.tr(ptv[:, 2 * h + kb, 0:n], Pn.t[:n, c0:c0 + 128], n, [Pn], self.Bq[ti])
        self.evac(PT.t[:, :, 0:n], ptv[:, :, 0:n], self.Bq[ti], [PT])

    def mem_PV(self, ctx, V):
        t, bi = ctx["t"], ctx["bi"]
        n = t.n
        PT = self.PT[bi]
        for h in range(4):
            for dc in range(2):
                c = 2 * h + dc
                bk = c // 4
                out = self.B[bk].t[:, (c % 4) * 128:(c % 4) * 128 + n]
                for kb in range(2):
                    self.mm(out, V.t[:, kb, h * 256 + dc * 128:h * 256 + dc * 128 + 128],
                            PT.t[:, 2 * h + kb, 0:n], kb == 0, kb == 1, [V, PT], self.Bq[bk])
        for bk in range(2):
            src = self.B[bk].t[:, :].rearrange("p (c t) -> p c t", c=4)[:, :, 0:n]
            self.evac(self.mixT.t[:, 4 * bk:4 * bk + 4, t.cs], src, self.Bq[bk], [self.mixT])

    def mem_attn_group(self, tiles, kv_of):
        ctx = self.mem_S(tiles[0], kv_of(tiles[0])[0])
        pend = None
        for ix, t in enumerate(tiles):
            nxt = self.mem_S(tiles[ix + 1], kv_of(tiles[ix + 1])[0]) if ix + 1 < len(tiles) else None
            self.mem_T(ctx)
            if pend is not None:
                self.mem_PV(pend[0], pend[1])
            pend = (ctx, kv_of(t)[1])
            ctx = nxt
        self.mem_PV(pend[0], pend[1])

    def stage_ffn(self, tiles):
        NTc = sum(t.n for t in tiles)
        hT = self.hT
        for jp in range(6):
            nch = 4 if jp < 5 else 2
            jg, sg = self.wget("w_g")
            gates = []
            for cc in range(nch):
                a, gb = self.Ahalf(cc)
                gps = a[:, 0:NTc]
                for k in range(8):
                    self.mm(gps, sg.t[:, k, cc * 128:(cc + 1) * 128], hT.t[:, k, 0:NTc], k == 0, k == 7, [sg, hT], gb)
                gates.append((gps, gb))
            self.wrel(jg)
            ju, su = self.wget("w_u")
            for cc in range(nch):
                f = jp * 4 + cc
                gps, gb = gates[cc]
                si = self.silu[self.rot("silu", 2)]
                self.act(si.t[:, 0:NTc], gps, AF.Silu, gb, [si])
                bi = self.rot("ffu", 2)
                ups = self.B[bi].t[:, 0:NTc]
                for k in range(8):
                    self.mm(ups, su.t[:, k, cc * 128:(cc + 1) * 128], hT.t[:, k, 0:NTc], k == 0, k == 7, [su, hT], self.Bq[bi])
                self.tt("dve", self.hidT.t[:, f, 0:NTc], si.t[:, 0:NTc], ups, ALU.mult, [si] + self.Bq[bi], [self.hidT])
            self.wrel(ju)
        self.load_gpost("g_post_ffn")
        self.proj_to_Y(self.hidT, "w_d", tiles, kgroups=(8, 8, 6))

    def group(self, tiles, first, last, g):
        P, cfg = self.P, self.cfg
        sample = tiles[0].kind == "s"
        NTc = sum(t.n for t in tiles)
        cur, prev = g % 2, (g + 1) % 2
        kaT_cur, va_cur = self.kaT[cur], self.va[cur]
        kaT_prev, va_prev = self.kaT[prev], self.va[prev]
        P.cur_stage = f"g{g}:load+norm"
        for t in tiles:
            src = self.xs[t.c0:t.c0 + t.n, :] if sample else self.xp[(g * cfg.gt + t.i) * 128:(g * cfg.gt + t.i + 1) * 128, :]
            P.dma("sp", self.X[t.i].t[:t.n, :], src, [], [self.X[t.i]])
        self.warm(self.cfg.warm_g)
        items = [(self.X[t.i], t.n, t.cs) for t in tiles]
        prs = [items[i:i + 2] for i in range(0, len(items), 2)]
        xbs = [self.prenorm_A(ip) for ip in prs]
        for ip, xb in zip(prs, xbs):
            self.prenorm_B(ip, 0, self.hT, xb)
        P.cur_stage = f"g{g}:win"
        self.stage_win(tiles, last, kaT_cur, va_cur)
        P.cur_stage = f"g{g}:mix"
        import os
        kg = int(os.environ.get("KG", "9"))
        if kg < 2:
            return
        for t in tiles:
            if sample:
                b = t.i
                self.load_state(b)

                def head_args(h, t=t, b=b):
                    p, hh = h // 2, h % 2
                    r = slice(hh * 64, hh * 64 + 64)
                    ckT, cv = self.ckT[b], self.cv[b]
                    ksegs = [(ckT.t[r, p, 0:512], 512, [ckT]), (self.kaTs.t[r, p, t.cs], 32, [self.kaTs])]
                    vbl = [(cv.t[:, kb, p * 128:(p + 1) * 128], 128, [cv]) for kb in range(4)]
                    vbl.append((self.vas[b].t[0:32, p * 128:(p + 1) * 128], 32, [self.vas[b]]))
                    return ((32, h, self.QT.t[r, p, t.cs], [self.QT], ksegs, self.Bm.t[0:32, h, 0:544], None),
                            vbl, self.mixT.t[r, p, t.cs])
                self.mix_tile(t, head_args)
                self.store_state(self.sglao[b, :, :])
            else:
                def head_args(h, t=t):
                    i = t.i
                    p, hh = h // 2, h % 2
                    r = slice(hh * 64, hh * 64 + 64)
                    ksegs, vbl = [], []
                    ksegs.append((kaT_prev.t[r, p, i * 128:512], (4 - i) * 128, [kaT_prev]))
                    ksegs.append((kaT_cur.t[r, p, 0:(i + 1) * 128], (i + 1) * 128, [kaT_cur]))
                    for kb in range(5):
                        if kb < 4 - i:
                            vbl.append((va_prev.t[:, i + kb, p * 128:(p + 1) * 128], 128, [va_prev]))
                        else:
                            vbl.append((va_cur.t[:, kb - (4 - i), p * 128:(p + 1) * 128], 128, [va_cur]))
                    eb = None
                    if first:
                        ne = (4 - i) * 128
                        eb = (self.onesb.t[0:1, 0:128], self.hm.t[0:1, i * 128:512], ne, [self.onesb, self.hm])
                    return ((128, h, self.QT.t[r, p, t.cs], [self.QT], ksegs, self.Bm.t[:, h, :], eb),
                            vbl, self.mixT.t[r, p, t.cs])
                self.mix_tile(t, head_args)
        if last and not sample:
            self.store_state(self.pgla[:, :])
        if kg < 3:
            return
        P.cur_stage = f"g{g}:wo"
        self.load_gpost("g_post_mix")
        self.proj_to_Y(self.mixT, "w_o", tiles)
        self.norm_transition(tiles, 1)
        P.cur_stage = f"g{g}:mem"
        for cc in range(2):
            j, slot = self.wget("w_mq")
            for c in range(4):
                self.proj_fm(slot, c * 128, 128, self.hT, NTc,
                             lambda ps, pb, c=c, cc=cc: self.evac(self.QT.t[:, cc * 4 + c, 0:NTc], ps, pb, [self.QT]))
            self.wrel(j)
        if sample:
            self.mem_attn_group(tiles, lambda t: (self.memKTs[t.i], self.memVs[t.i]))
        else:
            self.mem_attn_group(tiles, lambda t: (self.memKT, self.memV))
        self.load_gpost("g_post_mem")
        self.proj_to_Y(self.mixT, "w_mo", tiles)
        self.norm_transition(tiles, 2)
        P.cur_stage = f"g{g}:ffn"
        self.stage_ffn(tiles)
        P.cur_stage = f"g{g}:end"
        for i0 in range(0, len(tiles), 2):
            tp = tiles[i0:i0 + 2]
            self.postnorm_many(tp)
            for t in tp:
                if sample:
                    dst = self.ys[t.c0:t.c0 + t.n, :]
                else:
                    dst = self.yp[(g * cfg.gt + t.i) * 128:(g * cfg.gt + t.i + 1) * 128, :]
                P.dma("pool", dst, self.X[t.i].t[:t.n, :], [self.X[t.i]], [], out=True)

    def store_state(self, dst):
        for p in range(2):
            self.P.dma("pool", dst[p * 128:(p + 1) * 128, :], self.S[p].t[:, :], [self.S[p]], [], out=True)

    def load_state(self, b):
        for p in range(2):
            self.P.dma("sp", self.S[p].t[:, :], self.sgla[b, p * 128:(p + 1) * 128, :], [], [self.S[p]])
            self.copy_Sbf(p)

    def copy_Sbf(self, p):
        for hh in range(2):
            r = slice(hh * 64, hh * 64 + 64)
            self.copy("act", self.Sbf[p][hh].t[r, :], self.S[p].t[r, :], [self.S[p]], [self.Sbf[p][hh]])

    def mem_kv_prompt(self):
        import os
        P = self.P
        P.cur_stage = "memkv"
        sub = int(os.environ.get("KSUB", "9"))
        tiles = [Tile(i, 128, i * 128, "m") for i in range(2)]
        for t in tiles:
            P.dma("sp", self.X[t.i].t[:, :], self.memp[t.c0:t.c0 + 128, :], [], [self.X[t.i]])
        self.prenorm_many([(self.X[t.i], 128, t.cs) for t in tiles], 3, self.hT)
        if sub < 2:
            return
        for which in range(2):
            for cc in range(2):
                j, slot = self.wget("w_mk" if which == 0 else "w_mv")
                kvar = int(os.environ.get("KVAR", "9"))
                if kvar == 1:
                    self.copy("dve", self.Y[0].t[:, 0:512], slot.t[:, 0, 0:512], [slot], [self.Y[0]])
                    self.wrel(j)
                    continue
                for t in tiles:
                    if kvar == 2 and (which, cc, t.i) != (0, 0, 0):
                        continue
                    if kvar == 3 and (which, cc) != (0, 0):
                        continue
                    if kvar == 4 and which != 0:
                        continue
                    Y = self.Y[2 * which + t.i]

                    def ev(ps, pb, t=t, Y=Y, cc=cc, which=which):
                        e1 = self.ev_eng()
                        self.evac(Y.t[:, cc * 512:(cc + 1) * 512], ps, pb, [Y], eng=e1)
                        if which == 1 and kvar != 5:
                            self.evac(self.memV.t[:, t.i, cc * 512:(cc + 1) * 512], ps, pb, [self.memV],
                                      eng=(e1 if kvar == 7 else None))
                    self.proj_tm(slot, 0, 512, self.hT, t, ev)
                if which == 0 and sub >= 3:
                    for c in range(4):
                        self.proj_fm(slot, c * 128, 128, self.hT, 256,
                                     lambda ps, pb, c=c, cc=cc: self.evac(self.memKT.t[:, cc * 4 + c, :], ps, pb,
                                                                          [self.memKT]))
                self.wrel(j)
        if sub < 5:
            return
        for t in tiles:
            P.dma("pool", self.pmk[t.c0:t.c0 + 128, :], self.Y[t.i].t[:, :], [self.Y[t.i]], [], out=True)
            P.dma("pool", self.pmv[t.c0:t.c0 + 128, :], self.Y[2 + t.i].t[:, :], [self.Y[2 + t.i]], [], out=True)

    def halo_pass(self):
        P = self.P
        P.cur_stage = "halo"
        tiles = [Tile(i, 128, i * 128, "h") for i in range(4)]
        for t in tiles:
            P.dma("sp", self.X[t.i].t[:, :], self.xh[t.c0:t.c0 + 128, :], [], [self.X[t.i]])
        self.prenorm_many([(self.X[t.i], 128, t.cs) for t in tiles], 0, self.hT)
        kdst, vdst = self.kaT[1], self.va[1]
        j, slot = self.wget("w_in")
        for c in range(4):
            self.proj_fm(slot, c * 128, 128, self.hT, 512,
                         lambda ps, pb, c=c: self.evac(kdst.t[:, c, 0:512], ps, pb, [kdst]))
        self.wrel(j)
        j, slot = self.wget("w_in")
        for t in tiles:
            self.proj_tm(slot, 0, 512, self.hT, t,
                         lambda ps, pb, t=t: self.evac(vdst.t[:, t.i, :], ps, pb, [vdst]))
        self.wrel(j)

    def prefix_pass(self):
        P, cfg = self.P, self.cfg
        P.cur_stage = "prefix"
        if cfg.pref_t == 0:
            return
        ja, sa = self.wget("w_in")
        jb, sb_ = self.wget("w_in")
        for kc in range(8):
            self.ts("pool", sa.t[:, kc, 0:272], sa.t[:, kc, 0:272], self.gpre.t[:, 0, kc:kc + 1], ALU.mult,
                    [sa, self.gpre], [sa])
            self.ts("dve", sb_.t[:, kc, 0:512], sb_.t[:, kc, 0:512], self.gpre.t[:, 0, kc:kc + 1], ALU.mult,
                    [sb_, self.gpre], [sb_])
        S2 = [self.Y[p].t[:, 512:768] for p in range(2)]
        S2b = [Buf(f"S2_{p}") for p in range(2)]
        for p in range(2):
            self.memset("pool", S2[p], 0.0, [S2b[p], self.Y[p]])
        self.S2, self.S2b = S2, S2b
        G = cfg.pref_t // cfg.gt
        tiles = [Tile(i, 128, i * 128, "x") for i in range(cfg.gt)]
        items = [(self.X[t.i], 128, t.cs) for t in tiles]
        h = self.hidT.t
        kb1 = h[:, 0:2, :].rearrange("p a b -> p (a b)").rearrange("p (t f) -> p t f", f=256)
        vb1 = h[:, 2:6, :]
        sp1 = h[:, 6:10, :].bitcast(F32)
        glT1 = h[0:33, 10, :]
        B1 = [Buf("kb1"), Buf("vb1"), Buf("sp1"), Buf("glT1")]
        self.memset("pool", h[0:32, 10, :], 0.0, [B1[3]])
        self.memset("pool", h[32:33, 10, :], 1.0, [B1[3]])
        sets = [
            (self.kb.t, self.vb.t, self.sp.t, self.kb, self.vb, self.sp, self.glT.t, self.glT),
            (kb1, vb1, sp1, B1[0], B1[1], B1[2], glT1, B1[3]),
        ]

        def load(g):
            for t in tiles:
                r0 = (g * cfg.gt + t.i) * 128
                P.dma("sp", self.X[t.i].t[:, :], self.xpre[r0:r0 + 128, :], [], [self.X[t.i]])

        def proj(g):
            kb_ap, vb_ap, sp_ap, kbB, vbB, spB, glt, glB = sets[g % 2]
            self.proj_fm(sa, 256, 16, self.hT, self.NT,
                         lambda ps, pb: self.evac(glt[0:16, 0:self.NT], ps, pb, [glB]))
            for t in tiles:
                self.proj_tm(sa, 0, 256, self.hT, t,
                             lambda ps, pb, t=t: self.evac(kb_ap[:, t.i, :], ps, pb, [kbB]))
                self.proj_tm(sb_, 0, 512, self.hT, t,
                             lambda ps, pb, t=t: self.evac(vb_ap[:, t.i, :], ps, pb, [vbB]))
            pss = []
            for t in tiles:
                ap, bufs = self.Ahalf(self.rot("tmb", 4))
                ps = ap[:, 0:256]
                self.mm(ps, glt[0:33, t.cs], self.waext.t[0:33, :], True, True, [glB, self.waext], bufs)
                pss.append((ps, bufs))
            for t, (ps, bufs) in zip(tiles, pss):
                self.act(sp_ap[:, t.i, :], ps, AF.Exp, bufs, [spB], scale=-1.0)
            for t in tiles:
                self.act(sp_ap[:, t.i, :], sp_ap[:, t.i, :], AF.Ln, [spB, self.onesf], [spB], bias=self.onesf.t[:, 0:1])

        def bset(g):
            kb_ap, vb_ap, sp_ap, kbB, vbB, spB, glt, glB = sets[g % 2]
            return (kb_ap, vb_ap, sp_ap, kbB, vbB, spB)

        load(0)
        xbs = self.prenorm_A(items)
        self.prenorm_B(items, None, self.hT, xbs)
        for g in range(G):
            if g + 1 < G:
                load(g + 1)
                sx = self.prenorm_A1(items)
            proj(g)
            if g + 1 < G:
                xbs = self.prenorm_A2(items, sx)
            self.convert_more(1, after=[sets[g % 2][3]])
            if g >= 1:
                self.prefix_sufs(tiles, bset(g - 1))
            if g + 1 < G:
                self.prenorm_B(items, None, self.hT, xbs)
            if g >= 1:
                self.prefix_ds(tiles, bset(g - 1))
        self.prefix_sufs(tiles, bset(G - 1))
        self.prefix_ds(tiles, bset(G - 1))
        self.convert_more(len(self.conv_todo))
        for p in range(2):
            for hh in range(2):
                r = slice(hh * 64, hh * 64 + 64)
                self.copy("dve", self.S[p].t[r, :], self.S2[p][r, hh * 128:(hh + 1) * 128], [self.S2b[p]], [self.S[p]])
            self.copy_Sbf(p)
        self.wrel(ja)
        self.wrel(jb)

    def sample_setup(self):
        P = self.P
        P.cur_stage = "ssetup"
        ctok_b, cmtok_b = self.Y[0], self.Y[1]
        ctok = ctok_b.t[:, :].bitcast(BF16).rearrange("p (k f) -> p k f", k=4)
        cmtok = cmtok_b.t[:, :].bitcast(BF16).rearrange("p (k f) -> p k f", k=2)
        for b in range(2):
            ckT, cv = self.ckT[b], self.cv[b]
            P.dma("pool", ctok[:, :, :], self.cak[b].rearrange("(k p) f -> p k f", p=128), [], [ctok_b])
            P.dma("pool", cv.t[:, :, :], self.cav[b].rearrange("(k p) f -> p k f", p=128), [], [cv])
            for half in range(2):
                ti = 2 + self.rot("trb", 2)
                ptv = self.B[ti].t[:].bitcast(BF16).rearrange("p (c t) -> p c t", c=8)
                for pl in range(2):
                    p = 2 * half + pl
                    for kb in range(4):
                        self.tr(ptv[:, pl * 4 + kb, :], ctok[:, kb, p * 128:(p + 1) * 128], 128,
                                [ctok_b], self.Bq[ti])
                dst = ckT.t[:, 2 * half:2 * half + 2, :].rearrange("p a (k t) -> p (a k) t", t=128)
                self.evac(dst, ptv[:, :, :], self.Bq[ti], [ckT])
            P.dma("pool", cmtok[:, :, :], self.cmk[b].rearrange("(k p) f -> p k f", p=128), [], [cmtok_b])
            P.dma("pool", self.memVs[b].t[:, :, :], self.cmv[b].rearrange("(k p) f -> p k f", p=128), [],
                  [self.memVs[b]])
            for half in range(2):
                ti = 2 + self.rot("trb", 2)
                ptv = self.B[ti].t[:].bitcast(BF16).rearrange("p (c t) -> p c t", c=8)
                for cl in range(4):
                    c = 4 * half + cl
                    for kb in range(2):
                        self.tr(ptv[:, cl * 2 + kb, :], cmtok[:, kb, c * 128:(c + 1) * 128], 128,
                                [cmtok_b], self.Bq[ti])
                dst = self.memKTs[b].t[:, 4 * half:4 * half + 4, :].rearrange("p c (k t) -> p (c k) t", t=128)
                self.evac(dst, ptv[:, :, :], self.Bq[ti], [self.memKTs[b]])


def _bias_toeplitz(table):
    i = np.arange(128)[:, None]
    j = np.arange(640)[None, :]
    idx = np.clip(512 + i - j, -256, 256) + 256
    bm = table[:, idx]
    d = (j // 64) - (i // 64)
    valid = (d >= 0) & (d <= 8)
    return np.where(valid[None], bm, np.float32(NEG)).astype(np.float32)


_CACHE = {}


def _get_nc(cfg_key):
    if cfg_key not in _CACHE:
        _CACHE[cfg_key] = Builder(Cfg(*cfg_key)).build()
    return _CACHE[cfg_key]


def kernel(x_prompt, x_sample, mem_prompt, cache_a_k, cache_a_v, state_gla, cache_mem_k, cache_mem_v,
           g_pre_mix, w_in, rel_bias, w_alpha2, b_alpha, g_gla_out, w_o, g_post_mix,
           g_pre_mem, g_mem, w_mq, w_mk, w_mv, w_mo, g_post_mem,
           g_pre_ffn, w_ffn_gate, w_ffn_up, w_ffn_down, g_post_ffn, _ring=3, _warm_y=0, _warm_g=0):
    f = lambda a: np.ascontiguousarray(np.asarray(a, dtype=np.float32))
    x_prompt, x_sample, mem_prompt = f(x_prompt), f(x_sample), f(mem_prompt)
    nb, seq, _ = x_prompt.shape
    ncore = 8
    qn = ncore // nb
    tpc = seq // qn
    npt = tpc // 128
    pref_t = (qn - 1) * npt
    cfg_key = (npt, pref_t, 4, _ring, _warm_y, _warm_g)
    nc = _get_nc(cfg_key)
    bm = _bias_toeplitz(f(rel_bias)[0])
    shared = {
        "bm": bm,
        "g_pre_mix": f(g_pre_mix), "g_post_mix": f(g_post_mix), "g_pre_mem": f(g_pre_mem), "g_mem": f(g_mem),
        "g_post_mem": f(g_post_mem), "g_pre_ffn": f(g_pre_ffn), "g_post_ffn": f(g_post_ffn),
        "g_gla_out": f(g_gla_out), "w_alpha2": f(w_alpha2)[0], "b_alpha": f(b_alpha),
        "w_in": f(w_in)[0], "w_o": f(w_o)[0], "w_mq": f(w_mq)[0], "w_mk": f(w_mk)[0], "w_mv": f(w_mv)[0],
        "w_mo": f(w_mo)[0], "w_ffn_gate": f(w_ffn_gate)[0], "w_ffn_up": f(w_ffn_up)[0],
        "w_ffn_down": f(w_ffn_down)[0],
    }
    cak, cav = f(cache_a_k)[0].reshape(16, 512, 512), f(cache_a_v)[0].reshape(16, 512, 512)
    sg = f(state_gla)[0].reshape(16, 256, 128)
    cmk, cmv = f(cache_mem_k)[0].reshape(16, 256, 1024), f(cache_mem_v)[0].reshape(16, 256, 1024)
    in_maps = []
    for c in range(ncore):
        s, q = c // qn, c % qn
        t0 = q * tpc
        xpre = np.zeros((max(pref_t, 1) * 128, D), np.float32)
        if q > 0:
            xpre[pref_t * 128 - t0:] = x_prompt[s, 0:t0]
        xh = np.zeros((512, D), np.float32)
        if q > 0:
            xh[:] = x_prompt[s, t0 - 512:t0]
        hmask = np.full((1, 512), NEG if q == 0 else 0.0, np.float32)
        m = dict(shared)
        m.update({
            "xp": x_prompt[s, t0:t0 + tpc], "xh": xh, "xpre": xpre,
            "hmask": hmask,
            "xs": x_sample[2 * c:2 * c + 2].reshape(64, D), "memp": mem_prompt[s],
            "cak": cak[2 * c:2 * c + 2], "cav": cav[2 * c:2 * c + 2], "sgla": sg[2 * c:2 * c + 2],
            "cmk": cmk[2 * c:2 * c + 2], "cmv": cmv[2 * c:2 * c + 2],
        })
        in_maps.append({k: np.ascontiguousarray(v) for k, v in m.items()})
    import os as _os
    if _os.environ.get("KONE"):
        res = run_bass_kernel_spmd(nc, in_maps[:1], core_ids=[0])
        R = [res.results[0]] * ncore
    else:
        res = run_bass_kernel_spmd(nc, in_maps, core_ids=list(range(ncore)))
        R = res.results
    yp = np.stack([np.concatenate([R[s * qn + q]["yp"] for q in range(qn)], 0) for s in range(nb)])
    ys = np.concatenate([R[c]["ys"] for c in range(ncore)], 0).reshape(16, 32, D)
    last = [s * qn + qn - 1 for s in range(nb)]
    pak = np.stack([R[c]["pak"] for c in last]).reshape(1, nb, 512, 8, 64)
    pav = np.stack([R[c]["pav"] for c in last]).reshape(1, nb, 512, 8, 64)
    pgla = np.stack([R[c]["pgla"] for c in last]).reshape(1, nb, 4, 64, 128)
    firstc = [s * qn for s in range(nb)]
    pmk = np.stack([R[c]["pmk"] for c in firstc]).reshape(1, nb, 256, 4, 256)
    pmv = np.stack([R[c]["pmv"] for c in firstc]).reshape(1, nb, 256, 4, 256)
    sak = np.concatenate([R[c]["sak"] for c in range(ncore)], 0).reshape(1, 16, 32, 8, 64)
    sav = np.concatenate([R[c]["sav"] for c in range(ncore)], 0).reshape(1, 16, 32, 8, 64)
    sgl = np.concatenate([R[c]["sglao"] for c in range(ncore)], 0).reshape(1, 16, 4, 64, 128)
    outs = (yp, ys, pak, pav, pgla, pmk, pmv, sak, sav, sgl)
    return tuple(np.ascontiguousarray(o.astype(np.float32)) for o in outs)
```

```python
import numpy as np
import ml_dtypes
from contextlib import ExitStack
import concourse.bass as bass
import concourse.mybir as mybir
from concourse.bass_utils import run_bass_kernel_spmd

F32 = mybir.dt.float32
BF16 = mybir.dt.bfloat16
AF = mybir.ActivationFunctionType
ALU = mybir.AluOpType
AX = mybir.AxisListType

COMPUTE = ("pe", "act", "dve", "pool")
QUEUES = ("sp", "act", "pool")


class Buf:
    __slots__ = ("name", "t", "w", "r", "excl")

    def __init__(self, name, t=None, excl=False):
        self.name = name
        self.t = t
        self.w = None
        self.r = {}
        self.excl = excl

    def sub(self, n):
        return [Buf(f"{self.name}.{i}", self.t, self.excl) for i in range(n)]


class Prog:
    ENGS = ("pe", "act", "dve", "pool", "sp")

    def __init__(self, nc, es, dma_pool=10):
        self.nc = nc
        self.es = es
        self.ops = {e: [] for e in self.ENGS}
        self.seen_e = {e: {} for e in self.ENGS}
        self.seen_d = {e: {} for e in self.ENGS}
        self.dpool = {q: [(q, i) for i in range(dma_pool)] for q in ("sp", "pool", "act")}
        self.dnext = {q: 0 for q in self.dpool}
        self.dcount = {}
        self.out_tokens = {}
        self.dram = {}
        self.n_sb = 0

    def sb(self, name, shape, dtype):
        t = self.es.enter_context(self.nc.sbuf_tensor(name, list(shape), dtype))
        return Buf(name, t)

    def ps(self, name, shape, dtype):
        t = self.es.enter_context(self.nc.psum_tensor(name, list(shape), dtype))
        return Buf(name, t, excl=True)

    def dbuf(self, name):
        if name not in self.dram:
            self.dram[name] = Buf("dram:" + name)
        return self.dram[name]

    def _deps(self, eng, reads, writes, is_dma=False):
        raw, other = [], []
        for b in reads:
            if b.w is not None:
                raw.append(b.w)
            if b.excl:
                other.extend(b.r.values())
        for b in writes:
            if b.w is not None:
                other.append(b.w)
            other.extend(b.r.values())
        waits = []
        for tok, is_raw in [(t, True) for t in raw] + [(t, False) for t in other]:
            if tok[0] == "e":
                _, pe, idx = tok
                if pe == eng and not is_raw and not is_dma:
                    continue
                if pe == eng and eng == "pe":
                    continue
                if self.seen_e[eng].get(pe, -1) >= idx:
                    continue
                self.seen_e[eng][pe] = idx
                self.ops[pe][idx]["signal"] = True
                waits.append(tok)
            else:
                _, key, val = tok
                if self.seen_d[eng].get(key, 0) >= val:
                    continue
                self.seen_d[eng][key] = val
                waits.append(tok)
        return waits

    def _mark(self, tok, reads, writes):
        for b in reads:
            k = (tok[0], tok[1])
            b.r[k] = tok
        for b in writes:
            b.w = tok
            b.r = {}

    def op(self, eng, fn, reads=(), writes=()):
        waits = self._deps(eng, reads, writes)
        idx = len(self.ops[eng])
        self.ops[eng].append({"fn": fn, "waits": waits, "signal": False, "dma": None,
                              "stage": getattr(self, "cur_stage", "")})
        tok = ("e", eng, idx)
        self._mark(tok, [b for b in reads if b not in writes], writes)
        return tok

    def dma(self, q, out_ap, in_ap, reads=(), writes=(), out=False, dram_r=(), dram_w=(), **kw):
        reads = list(reads) + [self.dbuf(n) for n in dram_r]
        writes = list(writes) + [self.dbuf(n) for n in dram_w]
        pool = self.dpool[q]
        key = pool[self.dnext[q] % len(pool)]
        self.dnext[q] += 1
        k = self.dcount.get(key, 0)
        waits = self._deps(q, reads, writes, is_dma=True)
        if k > 0 and self.seen_d[q].get(key, 0) < 16 * k:
            self.seen_d[q][key] = 16 * k
            waits.append(("d", key, 16 * k))
        self.dcount[key] = k + 1
        tok = ("d", key, 16 * (k + 1))
        fn = lambda e, o=out_ap, i=in_ap, kw=kw: e.dma_start(out=o, in_=i, **kw)
        self.ops[q].append({"fn": fn, "waits": waits, "signal": False, "dma": key})
        self._mark(tok, reads, writes)
        if out:
            self.out_tokens[key] = tok
        return tok

    def finish(self):
        nc = self.nc
        fin = []
        for key, tok in self.out_tokens.items():
            if self.seen_d["sp"].get(key, 0) < tok[2]:
                fin.append(tok)
        es = self.es
        esem = {e: es.enter_context(nc.semaphore("tl_" + e)) for e in COMPUTE}
        dsem = {}
        for key in self.dcount:
            dsem[key] = es.enter_context(nc.semaphore(f"d_{key[0]}_{key[1]}"))
        cum = {}
        for e in COMPUTE:
            c = 0
            arr = []
            for o in self.ops[e]:
                if o["signal"]:
                    c += 1
                arr.append(c)
            cum[e] = arr
            assert c < 60000, (e, c)
        self.stats = {e: (len(self.ops[e]), cum[e][-1] if e in cum and cum[e] else 0) for e in self.ENGS}

        def emit(eng_name, eng):
            for o in self.ops[eng_name]:
                for tok in o["waits"]:
                    if tok[0] == "e":
                        eng.wait_ge(esem[tok[1]], cum[tok[1]][tok[2]])
                    else:
                        eng.wait_ge(dsem[tok[1]], tok[2])
                inst = o["fn"](eng)
                if o["dma"] is not None:
                    inst.then_inc(dsem[o["dma"]], 16)
                elif o["signal"]:
                    inst.then_inc(esem[eng_name], 1)
            if eng_name == "sp":
                for tok in fin:
                    eng.wait_ge(dsem[tok[1]], tok[2])

        with nc.Block() as block:
            @block.sync
            def _(e):
                emit("sp", e)

            @block.tensor
            def _(e):
                emit("pe", e)

            @block.scalar
            def _(e):
                emit("act", e)

            @block.vector
            def _(e):
                emit("dve", e)

            @block.gpsimd
            def _(e):
                emit("pool", e)


D = 1024
IN_W = 3088
DFF = 2816
NFF = DFF // 128
C_QA, C_KA, C_VA, C_QB, C_KB, C_VB, C_GL, C_R = 0, 512, 1024, 1536, 1792, 2048, 2560, 2576
EPS = 1e-6
NEG = -1e30
SLOTW = 528


class Cfg:
    def __init__(self, npt=32, pref_t=96, gt=4, ring=3, warm_y=0, warm_g=0):
        self.warm_y, self.warm_g = warm_y, warm_g
        self.npt = npt
        self.pref_t = pref_t
        self.gt = gt
        self.ring = ring
        assert npt % gt == 0 and pref_t % gt == 0


class Tile:
    def __init__(self, i, n, c0, kind):
        self.i, self.n, self.c0, self.kind = i, n, c0, kind

    @property
    def cs(self):
        return slice(self.c0, self.c0 + self.n)


class Builder:
    def __init__(self, cfg):
        self.cfg = cfg
        self.nc = bass.Bass("TRN2", target_bir_lowering=False)

    def dram_in(self, name, shape, dt=F32):
        return self.nc.dram_tensor(name, list(shape), dt, kind="ExternalInput").ap()

    def dram_out(self, name, shape, dt=F32):
        return self.nc.dram_tensor(name, list(shape), dt, kind="ExternalOutput").ap()

    def dram_tmp(self, name, shape, dt):
        return self.nc.dram_tensor(name, list(shape), dt).ap()

    def build(self):
        cfg = self.cfg
        with ExitStack() as es:
            self.P = Prog(self.nc, es)
            self.declare_io()
            self.alloc()
            import os
            stop = int(os.environ.get("KSTOP", "99"))
            self.setup_consts()
            if stop >= 2:
                self.convert_weights()
            self.make_schedule()
            if stop >= 3:
                self.mem_kv_prompt()
            if stop >= 4:
                self.halo_pass()
            if stop >= 5:
                self.prefix_pass()
            ng = cfg.npt // cfg.gt
            if stop >= 6:
                for g in range(ng):
                    tiles = [Tile(i, 128, i * 128, "p") for i in range(cfg.gt)]
                    self.group(tiles, first=(g == 0), last=(g == ng - 1), g=g)
            if stop >= 7:
                self.sample_setup()
                tiles = [Tile(i, 32, i * 32, "s") for i in range(2)]
                self.group(tiles, first=False, last=True, g=ng)
            self.P.finish()
        return self.nc

    def declare_io(self):
        c = self.cfg
        di, do = self.dram_in, self.dram_out
        self.xp = di("xp", [c.npt * 128, D])
        self.xh = di("xh", [512, D])
        self.xpre = di("xpre", [max(c.pref_t, 1) * 128, D])
        self.hmask = di("hmask", [1, 512])
        self.xs = di("xs", [64, D])
        self.memp = di("memp", [256, D])
        self.cak = di("cak", [2, 512, 512])
        self.cav = di("cav", [2, 512, 512])
        self.sgla = di("sgla", [2, 256, 128])
        self.cmk = di("cmk", [2, 256, D])
        self.cmv = di("cmv", [2, 256, D])
        self.bm_in = di("bm", [8, 128, 640])
        self.g = {n: di(n, [1, D]) for n in ("g_pre_mix", "g_post_mix", "g_pre_mem", "g_mem", "g_post_mem",
                                            "g_pre_ffn", "g_post_ffn")}
        self.g_gla = di("g_gla_out", [1, 512])
        self.w_alpha2 = di("w_alpha2", [16, 256])
        self.b_alpha = di("b_alpha", [1, 256])
        self.w = {"w_in": di("w_in", [D, IN_W]), "w_o": di("w_o", [D, D]), "w_mq": di("w_mq", [D, D]),
                  "w_mk": di("w_mk", [D, D]), "w_mv": di("w_mv", [D, D]), "w_mo": di("w_mo", [D, D]),
                  "w_g": di("w_ffn_gate", [D, DFF]), "w_u": di("w_ffn_up", [D, DFF]),
                  "w_d": di("w_ffn_down", [DFF, D])}
        self.wb = {k: self.dram_out(k + "_bf", list(v.shape), BF16) for k, v in self.w.items()}
        self.yp = do("yp", [c.npt * 128, D])
        self.ys = do("ys", [64, D])
        self.pak = do("pak", [512, 512])
        self.pav = do("pav", [512, 512])
        self.pgla = do("pgla", [256, 128])
        self.pmk = do("pmk", [256, D])
        self.pmv = do("pmv", [256, D])
        self.sak = do("sak", [64, 512])
        self.sav = do("sav", [64, 512])
        self.sglao = do("sglao", [2, 256, 128])

    def alloc(self):
        P, c = self.P, self.cfg
        sb, ps = P.sb, P.ps
        gt = c.gt
        NT = gt * 128
        self.NT = NT
        self.X = [sb(f"X{i}", [128, D], F32) for i in range(gt)]
        self.Y = [sb(f"Y{i}", [128, D], F32) for i in range(gt)]
        self.hT = sb("hT", [128, 8, NT], BF16)
        self.QT = sb("QT", [128, 8, NT], BF16)
        self.kaT = [sb(f"kaT{i}", [128, 4, NT], BF16) for i in range(2)]
        self.va = [sb(f"va{i}", [128, gt, 512], BF16) for i in range(2)]
        self.kb = sb("kb", [128, gt, 256], BF16)
        self.vb = sb("vb", [128, gt, 512], BF16)
        self.sp = sb("sp", [128, gt, 256], F32)
        self.glT = sb("glT", [33, NT], BF16)
        self.rT = sb("rT", [128, 4, NT], BF16)
        self.mixT = sb("mixT", [128, 8, NT], BF16)
        self.hidT = sb("hidT", [128, NFF, NT], BF16)
        self.Bm = sb("Bm", [128, 8, 640], BF16)
        self.gpost = sb("gpost", [128, D], F32)
        self.gpre = sb("gpre", [128, 4, 8], F32)
        self.ggla = sb("ggla", [128, 4], F32)
        self.waext = sb("waext", [33, 256], BF16)
        self.hm = sb("hm", [1, 512], BF16)
        self.memKT = sb("memKT", [128, 8, 256], BF16)
        self.memV = sb("memV", [128, 2, D], BF16)
        self.ident = sb("ident", [128, 128], BF16)
        self.Uf = sb("Uf", [128, 128], F32)
        self.SLf = sb("SLf", [128, 128], F32)
        self.Ub = sb("Ub", [128, 128], BF16)
        self.onesb = sb("onesb", [128, 128], BF16)
        self.onesf = sb("onesf", [128, 1], F32)
        self.S = [sb(f"S{p}", [128, 128], F32) for p in range(2)]
        self.Sbf = [[sb(f"Sbf{p}{hh}", [128, 128], BF16) for hh in range(2)] for p in range(2)]
        self.xsb = [sb(f"xsb{i}", [128, D], BF16) for i in range(4)]
        self.st = [sb(f"st{i}", [128, 8], F32) for i in range(4)]
        self.Pb = [sb(f"Pb{i}", [128, 1024], BF16) for i in range(2)]
        self.Pn = self.Pb
        self.PT = [sb(f"PT{i}", [128, 8, 128], BF16) for i in range(2)]
        self.rs8 = [sb(f"rs8{i}", [128, 8], F32) for i in range(2)]
        self.ri8 = [sb(f"ri8{i}", [128, 8], F32) for i in range(2)]
        self.Ep = [sb(f"Ep{p}", [128, 128], F32) for p in range(2)]
        self.Em = [sb(f"Em{p}", [128, 128], F32) for p in range(2)]
        self.qtl = [sb(f"qtl{p}", [128, 128], BF16) for p in range(2)]
        self.ktl = [[sb(f"ktl{p}{hh}", [128, 128], BF16) for hh in range(2)] for p in range(2)]
        self.kdw = sb("kdw", [128, 256], F32)
        self.kdec = sb("kdec", [128, 256], BF16)
        self.kdec4 = [self.kdec] + [sb(f"kdec{i}", [128, 256], BF16) for i in range(1, 4)]
        self.dec8 = sb("dec8", [128, 8], F32)
        self.dec = [sb(f"dec{p}", [128, 1], F32) for p in range(2)]
        self.ATm = [sb(f"ATm{i}", [128, 128], BF16) for i in range(4)]
        self.sq = [sb(f"sq{i}", [128, 128], BF16) for i in range(4)]
        self.rs = [sb(f"rs{i}", [128, 128], F32) for i in range(4)]
        self.t1 = [sb(f"t1{i}", [128, 128], F32) for i in range(4)]
        self.silu = [sb(f"silu{i}", [128, NT], BF16) for i in range(2)]
        self.ring = [sb(f"ring{i}", [128, 8, SLOTW], BF16) for i in range(c.ring)]
        self.ckT = self.kaT
        self.cv = self.va
        self.kaTs = sb("kaTs", [128, 4, 64], BF16)
        self.vas = [sb(f"vas{b}", [32, 512], BF16) for b in range(2)]
        self.memKTs = [self.memKT, sb("memKTs1", [128, 8, 256], BF16)]
        self.memVs = [self.memV, sb("memVs1", [128, 2, D], BF16)]
        self.A = [ps(f"psA{i}", [128, 1024], F32) for i in range(2)]
        self.B = [ps(f"psB{i}", [128, 512], F32) for i in range(4)]
        self.rr = {}
        self.init_psum_regions()

    def rot(self, key, n):
        v = self.rr.get(key, 0)
        self.rr[key] = v + 1
        return v % n

    def mm(self, out, lhsT, rhs, start, stop, reads, writes):
        self.P.op("pe", lambda e: e.matmul(out, lhsT, rhs, start=start, stop=stop), reads, writes)

    def tr(self, out, in_, n, reads, writes):
        idn = self.ident.t[:n, :n]
        self.P.op("pe", lambda e: e.transpose(out, in_, idn), list(reads) + [self.ident], writes)

    def act(self, out, in_, func, reads, writes, **kw):
        self.P.op("act", lambda e: e.activation(out, in_, func, **kw), reads, writes)

    def copy(self, eng, out, in_, reads, writes):
        if eng == "act":
            self.P.op("act", lambda e: e.copy(out, in_), reads, writes)
        else:
            self.P.op(eng, lambda e: e.tensor_copy(out, in_), reads, writes)

    def tt(self, eng, out, a, b, op, reads, writes):
        self.P.op(eng, lambda e: e.tensor_tensor(out, a, b, op), reads, writes)

    def ts(self, eng, out, a, s1, op0, reads, writes, s2=None, op1=None):
        if op1 is None:
            self.P.op(eng, lambda e: e.tensor_scalar(out, a, s1, None, op0), reads, writes)
        else:
            self.P.op(eng, lambda e: e.tensor_scalar(out, a, s1, s2, op0, op1), reads, writes)

    def stt(self, eng, out, a, s, b, op0, op1, reads, writes):
        self.P.op(eng, lambda e: e.scalar_tensor_tensor(out, a, s, b, op0, op1), reads, writes)

    def memset(self, eng, ap, val, writes):
        self.P.op(eng, lambda e: e.memset(ap, val), [], writes)

    def rstd(self, dst, src, inv_n, reads, writes):
        self.act(dst, src, AF.Ln, list(reads) + [self.epsc], writes, scale=inv_n, bias=self.epsc.t[:dst.shape[0], 0:1])
        self.act(dst, dst, AF.Exp, writes, writes, scale=-0.5)

    def setup_consts(self):
        P = self.P
        self.epsc = P.sb("epsc", [128, 1], F32)
        self.memset("pool", self.epsc.t[:], EPS, [self.epsc])
        tmpf = P.sb("tmpf", [128, 128], F32)
        self.memset("pool", tmpf.t[:], 1.0, [tmpf])
        P.op("pool", lambda e: e.affine_select(tmpf.t[:], tmpf.t[:], [[-1, 128]], ALU.is_equal, 0.0,
                                              base=0, channel_multiplier=1), [tmpf], [tmpf])
        self.copy("dve", self.ident.t[:], tmpf.t[:], [tmpf], [self.ident])
        self.memset("pool", self.Uf.t[:], 1.0, [self.Uf])
        P.op("pool", lambda e: e.affine_select(self.Uf.t[:], self.Uf.t[:], [[1, 128]], ALU.is_ge, 0.0,
                                              base=0, channel_multiplier=-1), [self.Uf], [self.Uf])
        self.copy("dve", self.Ub.t[:], self.Uf.t[:], [self.Uf], [self.Ub])
        self.memset("pool", self.SLf.t[:], 1.0, [self.SLf])
        P.op("pool", lambda e: e.affine_select(self.SLf.t[:], self.SLf.t[:], [[-1, 128]], ALU.is_gt, 0.0,
                                              base=0, channel_multiplier=1), [self.SLf], [self.SLf])
        self.memset("pool", self.onesb.t[:], 1.0, [self.onesb])
        self.memset("pool", self.onesf.t[:], 1.0, [self.onesf])
        for p in range(2):
            self.memset("pool", self.S[p].t[:], 0.0, [self.S[p]])
            for hh in range(2):
                self.memset("pool", self.Sbf[p][hh].t[:], 0.0, [self.Sbf[p][hh]])
                self.memset("pool", self.ktl[p][hh].t[:], 0.0, [self.ktl[p][hh]])
        self.identf = tmpf
        g8 = P.sb("g8", [8, 5, 128], F32)
        for i, n in enumerate(("g_pre_mix", "g_pre_mem", "g_pre_ffn", "g_mem")):
            P.dma("sp", g8.t[:, i, :], self.g[n].rearrange("o (c p) -> (o c) p", p=128), [], [g8])
        P.dma("sp", g8.t[0:4, 4, :], self.g_gla.rearrange("o (c p) -> (o c) p", p=128), [], [g8])
        pg = self.B[0].t[:, 0:40]
        for i in range(5):
            nr = 8 if i < 4 else 4
            P.op("pe", lambda e, i=i, nr=nr: e.transpose(pg[:, i * 8:i * 8 + nr], g8.t[0:nr, i, :], tmpf.t[0:nr, 0:nr]),
                 [g8, tmpf], self.Bq[0])
        self.copy("dve", self.gpre.t[:, :, :], pg[:, 0:32].rearrange("p (a c) -> p a c", a=4), self.Bq[0], [self.gpre])
        self.copy("dve", self.ggla.t[:, :], pg[:, 32:36], self.Bq[0], [self.ggla])
        self.memset("pool", self.waext.t[:], 0.0, [self.waext])
        P.dma("pool", self.waext.t[0:16, :], self.w_alpha2[:, :], [], [self.waext])
        P.dma("pool", self.waext.t[32:33, :], self.b_alpha[:, :], [], [self.waext])
        self.memset("pool", self.glT.t[0:32, :], 0.0, [self.glT])
        self.memset("pool", self.glT.t[32:33, :], 1.0, [self.glT])
        P.dma("pool", self.Bm.t[:], self.bm_in.rearrange("h p j -> p h j"), [], [self.Bm])
        P.dma("pool", self.hm.t[:], self.hmask[:, :], [], [self.hm])

    def load_gpost(self, name):
        src = self.g[name][0:1, :].partition_broadcast(128)
        self.P.dma("sp", self.gpost.t[:], src[:, 0, :], [], [self.gpost])

    def convert_weights(self):
        self.conv_todo = []
        for k in ["w_mk", "w_mv", "w_in", "w_o", "w_mq", "w_mo", "w_g", "w_u", "w_d"]:
            rows = self.w[k].shape[0]
            for r0 in range(0, rows, 256):
                self.conv_todo.append((k, r0, min(rows, r0 + 256)))
        n_now = 16 if self.cfg.pref_t else len(self.conv_todo)
        self.convert_more(n_now)

    def convert_more(self, n, after=()):
        for _ in range(n):
            if not self.conv_todo:
                return
            k, r0, r1 = self.conv_todo.pop(0)
            self.P.dma("pool", self.wb[k][r0:r1, :], self.w[k][r0:r1, :], list(after), [], dram_w=[f"{k}:{r0 // 256}"])

    def warm(self, n):
        for _ in range(n):
            self.mm(self.B[1].t[:, 0:512], self.ident.t[:, :], self.Bm.t[:, 0, 0:512], True, True,
                    [self.ident, self.Bm], self.Bq[1])

    def make_schedule(self):
        c = self.cfg
        W8 = lambda k, c0, n: (k, 0, 8, [(c0, n, 0)])
        halo = [W8("w_in", C_KA, 512), W8("w_in", C_VA, 512)]
        pref = [("w_in", 0, 8, [(C_KB, 256, 0), (C_GL, 16, 256)]), W8("w_in", C_VB, 512)]
        memkv = [W8("w_mk", 0, 512), W8("w_mk", 512, 512), W8("w_mv", 0, 512), W8("w_mv", 512, 512)]
        grp = [W8("w_in", C_QA, 512), W8("w_in", C_KA, 512), W8("w_in", C_VA, 512), W8("w_in", C_QB, 512),
               W8("w_in", C_VB, 512), W8("w_in", C_GL, 528),
               W8("w_o", 0, 512), W8("w_o", 512, 512), W8("w_mq", 0, 512), W8("w_mq", 512, 512),
               W8("w_mo", 0, 512), W8("w_mo", 512, 512)]
        for j in range(6):
            n = 512 if j < 5 else 256
            grp += [W8("w_g", j * 512, n), W8("w_u", j * 512, n)]
        for cc in range(2):
            for kg in range(3):
                nk = 8 if kg < 2 else 6
                grp.append(("w_d", kg * 1024, nk, [(cc * 512, 512, 0)]))
        ng = c.npt // c.gt + 1
        self.sched = memkv + halo + (pref if c.pref_t else []) + grp * ng
        self.r_cur = 0
        self.r_loaded = 0
        self.r_rel = set()

    def _pump(self):
        n = len(self.ring)
        while (self.r_loaded < len(self.sched) and self.r_loaded < self.r_cur + n
               and (self.r_loaded < n or (self.r_loaded - n) in self.r_rel)):
            j = self.r_loaded
            key, row0, nk, pieces = self.sched[j]
            slot = self.ring[j % n]
            for (c0, ncol, s0) in pieces:
                src = self.wb[key][row0:row0 + nk * 128, c0:c0 + ncol].rearrange("(c p) f -> p c f", p=128)
                blks = [f"{key}:{b}" for b in range(row0 // 256, (row0 + nk * 128 + 255) // 256)]
                self.P.dma("sp", slot.t[:, 0:nk, s0:s0 + ncol], src, [], [slot], dram_r=blks)
            self.r_loaded += 1

    def wget(self, expect=None):
        j = self.r_cur
        self.r_cur += 1
        self._pump()
        assert self.r_loaded > j, ("ring stall", j)
        if expect is not None:
            assert self.sched[j][0] == expect, (self.sched[j], expect)
        return j, self.ring[j % len(self.ring)]

    def wrel(self, j):
        self.r_rel.add(j)
        self._pump()

    def init_psum_regions(self):
        self.Aq = []
        for a in self.A:
            lo, hi = a.sub(2)
            self.Aq.append([lo, lo, hi, hi])
        self.Bq = [[b, b, b, b] for b in self.B]

    def Ahalf(self, i):
        a, h = i // 2, i % 2
        return self.A[a].t[:, h * 512:(h + 1) * 512], self.Aq[a][2 * h:2 * h + 2]

    def ev_eng(self):
        return ("act", "dve")[self.rot("ev", 2)]

    def evac(self, out, in_, reads, writes, scale=None, eng=None):
        eng = eng or self.ev_eng()
        if scale is None:
            self.copy(eng, out, in_, reads, writes)
        elif eng == "act":
            self.act(out, in_, AF.Copy, reads, writes, scale=scale)
        else:
            self.ts("dve", out, in_, scale, ALU.mult, reads, writes)

    def prenorm(self, X, n, gi, dst, cs):
        self.prenorm_many([(X, n, cs)], gi, dst)

    def prenorm_many(self, items, gi, dst):
        self.prenorm_B(items, gi, dst, self.prenorm_A(items))

    def prenorm_A(self, items):
        return self.prenorm_A2(items, self.prenorm_A1(items))

    def prenorm_A1(self, items):
        sts = [self.st[self.rot("st", 4)] for _ in items]
        xbs = [self.xsb[self.rot("xsb", 4)] for _ in items]
        for (X, n, cs), st, xb in zip(items, sts, xbs):
            self.act(xb.t[:n, :], X.t[:n, :], AF.Square, [X, st], [xb, st], accum_out=st.t[:n, 0:1])
        for (X, n, cs), st in zip(items, sts):
            self.act(st.t[:n, 1:2], st.t[:n, 0:1], AF.Ln, [st, self.epsc], [st], scale=1.0 / D, bias=self.epsc.t[:n, 0:1])
        for (X, n, cs), st in zip(items, sts):
            self.act(st.t[:n, 1:2], st.t[:n, 1:2], AF.Exp, [st], [st], scale=-0.5)
        return (sts, xbs)

    def prenorm_A2(self, items, sx):
        sts, xbs = sx
        for (X, n, cs), st, xb in zip(items, sts, xbs):
            self.ts("dve", xb.t[:n, :], X.t[:n, :], st.t[:n, 1:2], ALU.mult, [X, st], [xb])
        return xbs

    def prenorm_B(self, items, gi, dst, xbs):
        for (X, n, cs), xb in zip(items, xbs):
            bi = 2 + self.rot("trb", 2)
            ptv = self.B[bi].t[:].bitcast(BF16).rearrange("p (c t) -> p c t", c=8)
            for c in range(8):
                self.tr(ptv[:, c, 0:n], xb.t[:n, c * 128:(c + 1) * 128], n, [xb], self.Bq[bi])
            if gi is None:
                self.evac(dst.t[:, :, cs], ptv[:, :, 0:n], self.Bq[bi], [dst])
            else:
                gb = self.gpre.t[:, gi, :].unsqueeze(2).to_broadcast([128, 8, n])
                self.tt("dve", dst.t[:, :, cs], ptv[:, :, 0:n], gb, ALU.mult, self.Bq[bi] + [self.gpre], [dst])

    def norm_transition(self, tiles, gi):
        items = [(self.X[t.i], t.n, t.cs) for t in tiles]
        pairs = [(tiles[i:i + 2], items[i:i + 2]) for i in range(0, len(tiles), 2)]
        for tp, _ in pairs:
            self.postnorm_many(tp)
        xbs = [self.prenorm_A(ip) for _, ip in pairs]
        for (_, ip), xb in zip(pairs, xbs):
            self.prenorm_B(ip, gi, self.hT, xb)

    def postnorm_many(self, tiles):
        sts = [self.st[self.rot("st", 4)] for _ in tiles]
        for t, st in zip(tiles, sts):
            Y = self.Y[t.i]
            jb = self.xsb[self.rot("xsb", 4)]
            self.act(jb.t[:t.n, :], Y.t[:t.n, :], AF.Square, [Y, st], [jb, st], accum_out=st.t[:t.n, 0:1])
        for t, st in zip(tiles, sts):
            self.act(st.t[:t.n, 1:2], st.t[:t.n, 0:1], AF.Ln, [st, self.epsc], [st], scale=1.0 / D, bias=self.epsc.t[:t.n, 0:1])
        for t, st in zip(tiles, sts):
            self.act(st.t[:t.n, 1:2], st.t[:t.n, 1:2], AF.Exp, [st], [st], scale=-0.5)
        for ix, (t, st) in enumerate(zip(tiles, sts)):
            Y = self.Y[t.i]
            eng = "pool" if ix == len(tiles) - 1 and len(tiles) > 2 else "dve"
            self.tt(eng, Y.t[:t.n, :], Y.t[:t.n, :], self.gpost.t[:t.n, :], ALU.mult, [Y, self.gpost], [Y])
        for t, st in zip(tiles, sts):
            X, Y = self.X[t.i], self.Y[t.i]
            self.stt("dve", X.t[:t.n, :], Y.t[:t.n, :], st.t[:t.n, 1:2], X.t[:t.n, :], ALU.mult, ALU.add, [Y, st, X], [X])

    def proj_fm(self, slot, scol0, m, srcT, ncols, evac_fn):
        bi = self.rot("fmb", 2)
        ps = self.B[bi].t[:m, 0:ncols]
        for k in range(8):
            self.mm(ps, slot.t[:, k, scol0:scol0 + m], srcT.t[:, k, 0:ncols], k == 0, k == 7,
                    [slot, srcT], self.Bq[bi])
        evac_fn(ps, self.Bq[bi])

    def proj_tm(self, slot, scol0, ncol, srcT, t, evac_fn, nk=8):
        ap, bufs = self.Ahalf(self.rot("tmb", 4))
        ps = ap[:t.n, 0:ncol]
        for k in range(nk):
            self.mm(ps, srcT.t[:, k, t.cs], slot.t[:, k, scol0:scol0 + ncol], k == 0, k == nk - 1,
                    [slot, srcT], bufs)
        evac_fn(ps, bufs)

    def stage_win(self, tiles, last, kaT_cur, va_cur):
        P = self.P
        sample = tiles[0].kind == "s"
        NTc = sum(t.n for t in tiles)
        hT, QT = self.hT, self.QT
        j, slot = self.wget("w_in")
        for c in range(4):
            self.proj_fm(slot, c * 128, 128, hT, NTc,
                         lambda ps, pb, c=c: self.evac(QT.t[:, c, 0:NTc], ps, pb, [QT], scale=0.125))
        self.wrel(j)
        j, slot = self.wget("w_in")
        kdst = self.kaTs if sample else kaT_cur
        for c in range(4):
            self.proj_fm(slot, c * 128, 128, hT, NTc,
                         lambda ps, pb, c=c: self.evac(kdst.t[:, c, 0:NTc], ps, pb, [kdst]))
        if last:
            for t in tiles:
                Y = self.Y[t.i]
                self.proj_tm(slot, 0, 512, hT, t,
                             lambda ps, pb, t=t, Y=Y: self.evac(Y.t[:t.n, 0:512], ps, pb, [Y]))
        self.wrel(j)
        j, slot = self.wget("w_in")
        for t in tiles:
            vdst = self.vas[t.i].t[:t.n, :] if sample else va_cur.t[:t.n, t.i, :]
            vbuf = self.vas[t.i] if sample else va_cur
            Y = self.Y[t.i]

            def ev(ps, pb, t=t, vdst=vdst, vbuf=vbuf, Y=Y):
                self.evac(vdst, ps, pb, [vbuf])
                if last:
                    self.evac(Y.t[:t.n, 512:1024], ps, pb, [Y])
            self.proj_tm(slot, 0, 512, hT, t, ev)
        self.wrel(j)
        if last:
            for t in tiles:
                Y = self.Y[t.i]
                if sample:
                    dk, dv = self.sak[t.c0:t.c0 + t.n, :], self.sav[t.c0:t.c0 + t.n, :]
                else:
                    dk, dv = self.pak[t.c0:t.c0 + t.n, :], self.pav[t.c0:t.c0 + t.n, :]
                P.dma("pool", dk, Y.t[:t.n, 0:512], [Y], [], out=True)
                P.dma("pool", dv, Y.t[:t.n, 512:1024], [Y], [], out=True)
        j, slot = self.wget("w_in")
        for c in range(4):
            sc = 0.125 if c < 2 else None
            self.proj_fm(slot, c * 128, 128, hT, NTc,
                         lambda ps, pb, c=c, sc=sc: self.evac(QT.t[:, 4 + c, 0:NTc], ps, pb, [QT], scale=sc))
        for t in tiles:
            self.proj_tm(slot, 256, 256, hT, t,
                         lambda ps, pb, t=t: self.evac(self.kb.t[:t.n, t.i, :], ps, pb, [self.kb]))
        self.wrel(j)
        j, slot = self.wget("w_in")
        for t in tiles:
            self.proj_tm(slot, 0, 512, hT, t,
                         lambda ps, pb, t=t: self.evac(self.vb.t[:t.n, t.i, :], ps, pb, [self.vb]))
        self.wrel(j)
        j, slot = self.wget("w_in")
        self.proj_fm(slot, 0, 16, hT, NTc,
                     lambda ps, pb: self.evac(self.glT.t[0:16, 0:NTc], ps, pb, [self.glT]))
        for c in range(4):
            self.proj_fm(slot, 16 + c * 128, 128, hT, NTc,
                         lambda ps, pb, c=c: self.act(self.rT.t[:, c, 0:NTc], ps, AF.Silu, pb, [self.rT]))
        self.wrel(j)
        self.softplus_tiles(tiles)

    def softplus_tiles(self, tiles, glT=None, sp=None, sp_ap=None):
        glT = glT or self.glT
        sp = sp or self.sp
        spt = sp_ap if sp_ap is not None else sp.t
        glt = glT.t if glT.t is not None else self.glT1_ap
        pss = []
        for t in tiles:
            ap, bufs = self.Ahalf(self.rot("tmb", 4))
            ps = ap[:t.n, 0:256]
            self.mm(ps, glt[0:33, t.cs], self.waext.t[0:33, :], True, True, [glT, self.waext], bufs)
            pss.append((ps, bufs))
        for t, (ps, bufs) in zip(tiles, pss):
            self.act(spt[:t.n, t.i, :], ps, AF.Exp, bufs, [sp], scale=-1.0)
        for t in tiles:
            dst = spt[:t.n, t.i, :]
            self.act(dst, dst, AF.Ln, [sp, self.onesf], [sp], bias=self.onesf.t[:t.n, 0:1])

    def gla_stages(self, t):
        C = t.n
        sp, kb, vb, QT = self.sp, self.kb, self.vb, self.QT
        B0, B1 = self.B[0].t, self.B[1].t
        q0, q1 = self.Bq[0], self.Bq[1]

        def g0():
            suf = B0[:C, 256:512]
            self.mm(suf, self.SLf.t[:C, :C], sp.t[:C, t.i, :], True, True, [self.SLf, sp], q0)
            csts = []
            for p in range(2):
                cst = B0[:, p * 128:p * 128 + C]
                self.mm(cst, sp.t[:C, t.i, p * 128:(p + 1) * 128], self.Uf.t[:C, :C], True, True, [sp, self.Uf], q0)
                csts.append(cst)
            self.act(self.kdw.t[:C, :], suf, AF.Exp, q0, [self.kdw], scale=-1.0 / 16)
            for p in range(2):
                self.act(self.Ep[p].t[:, :C], csts[p], AF.Exp, q0, [self.Ep[p]], scale=-1.0 / 16)
                self.act(self.Em[p].t[:, :C], csts[p], AF.Exp, q0, [self.Em[p]], scale=1.0 / 16)
            self.tt("dve", self.kdec.t[:C, :], kb.t[:C, t.i, :], self.kdw.t[:C, :], ALU.mult, [kb, self.kdw], [self.kdec])
            for p in range(2):
                self.tt("dve", self.qtl[p].t[:, :C], QT.t[:, 4 + p, t.cs], self.Ep[p].t[:, :C], ALU.mult,
                        [QT, self.Ep[p]], [self.qtl[p]])
                for hh in range(2):
                    r = slice(hh * 64, hh * 64 + 64)
                    self.tt("dve", self.ktl[p][hh].t[r, :C], QT.t[r, 6 + p, t.cs], self.Em[p].t[r, :C], ALU.mult,
                            [QT, self.Em[p]], [self.ktl[p][hh]])
                self.copy("dve", self.dec[p].t[:, 0:1], self.Ep[p].t[:, C - 1:C], [self.Ep[p]], [self.dec[p]])

        def g1():
            for h in range(4):
                p, r = h // 2, slice((h % 2) * 64, (h % 2) * 64 + 64)
                self.mm(B1[:C, h * 128:h * 128 + C], self.ktl[p][h % 2].t[:, :C], self.qtl[p].t[:, :C], True, True,
                        [self.ktl[p][h % 2], self.qtl[p]], q1)
            for h in range(4):
                self.tt("dve", self.ATm[h].t[:C, :C], B1[:C, h * 128:h * 128 + C], self.Ub.t[:C, :C], ALU.mult,
                        q1 + [self.Ub], [self.ATm[h]])

        def g2():
            for h in range(4):
                p, r = h // 2, slice((h % 2) * 64, (h % 2) * 64 + 64)
                oT = B0[:, h * 128:h * 128 + C]
                self.mm(oT, vb.t[:C, t.i, h * 128:(h + 1) * 128], self.ATm[h].t[:C, :C], True, False, [vb, self.ATm[h]], q0)
                self.mm(oT, self.Sbf[p][h % 2].t[:, :], self.qtl[p].t[:, :C], False, True, [self.Sbf[p][h % 2], self.qtl[p]], q0)
            for h in range(4):
                self.act(self.sq[h].t[:, :C], B0[:, h * 128:h * 128 + C], AF.Square, q0, [self.sq[h]])

        def g3():
            for h in range(4):
                self.mm(B1[:, h * 128:h * 128 + C], self.onesb.t[:, :], self.sq[h].t[:, :C], True, True,
                        [self.onesb, self.sq[h]], q1)
            for h in range(4):
                self.act(self.rs[h].t[:, :C], B1[:, h * 128:h * 128 + C], AF.Ln, q1 + [self.epsc], [self.rs[h]], scale=1.0 / 128,
                         bias=self.epsc.t[:, 0:1])
            for h in range(4):
                self.act(self.rs[h].t[:, :C], self.rs[h].t[:, :C], AF.Exp, [self.rs[h]], [self.rs[h]], scale=-0.5)
            for h in range(4):
                self.stt("dve", self.t1[h].t[:, :C], B0[:, h * 128:h * 128 + C], self.ggla.t[:, h:h + 1],
                         self.rs[h].t[:, :C], ALU.mult, ALU.mult, q0 + [self.ggla, self.rs[h]], [self.t1[h]])
            for h in range(4):
                self.tt("dve", self.mixT.t[:, 4 + h, t.cs], self.t1[h].t[:, :C], self.rT.t[:, h, t.cs], ALU.mult,
                        [self.t1[h], self.rT], [self.mixT])

        def g4():
            for p in range(2):
                ds = B0[:, p * 256:(p + 1) * 256]
                self.mm(ds, self.kdec.t[:C, p * 128:(p + 1) * 128], vb.t[:C, t.i, p * 256:(p + 1) * 256], True, True,
                        [self.kdec, vb], q0)
            for p in range(2):
                S = self.S[p]
                for hh in range(2):
                    r = slice(hh * 64, hh * 64 + 64)
                    self.stt("dve", S.t[r, :], S.t[r, :], self.dec[p].t[r, 0:1],
                             B0[r, p * 256 + hh * 128:p * 256 + (hh + 1) * 128],
                             ALU.mult, ALU.add, [S, self.dec[p]] + q0, [S])
                self.copy_Sbf(p)

        return [g0, g1, g2, g3, g4]

    def prefix_sufs(self, tiles, bset):
        kb_ap, vb_ap, sp_ap, kbB, vbB, spB = bset
        regs = []
        for t in tiles:
            ap, bufs = self.Ahalf(t.i)
            self.mm(ap[:, 0:256], self.SLf.t[:, :], sp_ap[:, t.i, :], True, True, [self.SLf, spB], bufs)
            for p in range(2):
                self.mm(ap[:, 256 + p:257 + p], sp_ap[:, t.i, p * 128:(p + 1) * 128], self.onesf.t[:, 0:1], True, True,
                        [spB, self.onesf], bufs)
            regs.append((ap, bufs))
        for t, (ap, bufs) in zip(tiles, regs):
            Y = self.Y[t.i]
            self.act(Y.t[:, 0:256], ap[:, 0:256], AF.Exp, bufs, [Y], scale=-1.0 / 16)
            self.act(self.dec8.t[:, 2 * t.i:2 * t.i + 2], ap[:, 256:258], AF.Exp, bufs, [self.dec8], scale=-1.0 / 16)
        for t in tiles:
            Y = self.Y[t.i]
            self.tt("dve", self.kdec4[t.i].t[:, :], kb_ap[:, t.i, :], Y.t[:, 0:256], ALU.mult, [kbB, Y], [self.kdec4[t.i]])

    def prefix_ds(self, tiles, bset):
        kb_ap, vb_ap, sp_ap, kbB, vbB, spB = bset
        for t in tiles:
            kd = self.kdec4[t.i]
            for p in range(2):
                bi = self.rot("pfx", 2)
                ds = self.B[bi].t[:, 0:256]
                self.mm(ds, kd.t[:, p * 128:(p + 1) * 128], vb_ap[:, t.i, p * 256:(p + 1) * 256], True, True,
                        [kd, vbB], self.Bq[bi])
                self.stt("dve", self.S2[p], self.S2[p], self.dec8.t[:, 2 * t.i + p:2 * t.i + p + 1], ds,
                         ALU.mult, ALU.add, [self.S2b[p], self.dec8] + self.Bq[bi], [self.S2b[p]])

    def attn_S(self, nq, h, qT_ap, qbufs, ksegs, bias_ap, extra_bias):
        ai = self.rot("attA", 2)
        A, Aq = self.A[ai].t, self.Aq[ai]
        nk = sum(n for _, n, _ in ksegs)
        sbufs = Aq[0:3]
        ops = []
        for c0 in range(0, nk, 512):
            c1 = min(nk, c0 + 512)
            ops.append((c0, c1, self.ident.t[:nq, :nq], bias_ap[:, c0:c1], [self.ident, self.Bm]))
        if extra_bias is not None:
            l_ap, r_ap, n_e, ebufs = extra_bias
            ops.append((0, n_e, l_ap, r_ap, ebufs))
        col = 0
        for (k_ap, n, kbufs) in ksegs:
            o = 0
            while o < n:
                e = min(n, o + (512 - (col + o) % 512))
                ops.append((col + o, col + e, qT_ap, k_ap[:, o:e], qbufs + kbufs))
                o = e
            col += n
        for bank in range(2):
            bops = [x for x in ops if x[0] // 512 == bank]
            for ix, (c0, c1, l_ap, r_ap, rb) in enumerate(bops):
                self.mm(A[:nq, c0:c1], l_ap, r_ap, ix == 0, ix == len(bops) - 1, rb, sbufs)
        bi = self.rot("attP", 2)
        Pb, rs8, ri8 = self.Pb[bi], self.rs8[bi], self.ri8[bi]
        self.act(Pb.t[:nq, 0:nk], A[:nq, 0:nk], AF.Exp, sbufs + [rs8], [Pb, rs8], accum_out=rs8.t[:nq, 0:1])
        self.P.op("dve", lambda e: e.reciprocal(ri8.t[:nq, 0:1], rs8.t[:nq, 0:1]), [rs8], [ri8])
        self.ts("dve", Pb.t[:nq, 0:nk], Pb.t[:nq, 0:nk], ri8.t[:nq, 0:1], ALU.mult, [Pb, ri8], [Pb])
        return {"ai": ai, "bi": bi, "nk": nk, "h": h, "nq": nq}

    def attn_rest(self, ctx, vblocks, out_ap, out_buf):
        self.attn_T(ctx, vblocks)
        self.attn_PV(ctx, vblocks, out_ap, out_buf)

    def attn_T(self, ctx, vblocks):
        nq, bi = ctx["nq"], ctx["bi"]
        Pn, PT = self.Pb[bi], self.PT[bi]
        ti = 2 + self.rot("trb", 2)
        ptv = self.B[ti].t[:].bitcast(BF16).rearrange("p (c t) -> p c t", c=8)
        col = 0
        for bix, (_, n, _) in enumerate(vblocks):
            self.tr(ptv[:n, bix, 0:nq], Pn.t[:nq, col:col + n], nq, [Pn], self.Bq[ti])
            col += n
        nb = len(vblocks)
        nmax = max(n for _, n, _ in vblocks)
        self.evac(PT.t[:nmax, 0:nb, 0:nq], ptv[:nmax, 0:nb, 0:nq], self.Bq[ti], [PT])

    def attn_PV(self, ctx, vblocks, out_ap, out_buf):
        nq, h, ai, bi = ctx["nq"], ctx["h"], ctx["ai"], ctx["bi"]
        A, Aq = self.A[ai].t, self.Aq[ai]
        PT = self.PT[bi]
        r = slice((h % 2) * 64, (h % 2) * 64 + 64)
        nb = len(vblocks)
        o_ps = A[:, 768:768 + nq]
        for bix, (v_ap, n, vbufs) in enumerate(vblocks):
            self.mm(o_ps, v_ap, PT.t[:n, bix, 0:nq], bix == 0, bix == nb - 1, vbufs + [PT], [Aq[3]])
        self.evac(out_ap, A[r, 768:768 + nq], [Aq[3]], [out_buf])

    def mix_tile(self, t, head_args):
        import os
        km = int(os.environ.get("KM", "0"))
        stages = self.gla_stages(t)
        if km == 1:
            stages = []
        args = [head_args(h) for h in range(8)]
        if km == 2:
            for h in range(8):
                ctx = self.attn_S(*args[h][0])
                self.attn_rest(ctx, args[h][1], args[h][2], self.mixT)
                if h < len(stages):
                    stages[h]()
            return
        if km >= 4:
            stages = stages[:km - 3]
        if km >= 3:
            for h in range(8):
                ctx = self.attn_S(*args[h][0])
                self.attn_rest(ctx, args[h][1], args[h][2], self.mixT)
            for st in stages:
                st()
            return
        ctx = self.attn_S(*args[0][0])
        pend = None
        for h in range(8):
            nxt = self.attn_S(*args[h + 1][0]) if h + 1 < 8 else None
            self.attn_T(ctx, args[h][1])
            if pend is not None:
                self.attn_PV(pend[0], pend[1], pend[2], self.mixT)
            pend = (ctx, args[h][1], args[h][2])
            if h < len(stages):
                stages[h]()
            ctx = nxt
        self.attn_PV(pend[0], pend[1], pend[2], self.mixT)

    def proj_to_Y(self, srcT, wkey, tiles, kgroups=(8,)):
        for cc in range(2):
            accs = [self.Ahalf(t.i) for t in tiles]
            nkg = len(kgroups)
            kbase = 0
            for gi, nk in enumerate(kgroups):
                j, slot = self.wget(wkey)
                for t, (ap, bufs) in zip(tiles, accs):
                    for k in range(nk):
                        self.mm(ap[:t.n, :], srcT.t[:, kbase + k, t.cs], slot.t[:, k, 0:512],
                                gi == 0 and k == 0, gi == nkg - 1 and k == nk - 1, [srcT, slot], bufs)
                self.wrel(j)
                kbase += nk
            for t, (ap, bufs) in zip(tiles, accs):
                Y = self.Y[t.i]
                self.evac(Y.t[:t.n, cc * 512:(cc + 1) * 512], ap[:t.n, :], bufs, [Y])
        self.warm(self.cfg.warm_y)

    def mem_S(self, t, KT):
        n = t.n
        QT = self.QT
        ai = self.rot("attA", 2)
        A, Aq = self.A[ai].t, self.Aq[ai]
        for h in range(4):
            for kk in range(2):
                self.mm(A[:n, h * 256:(h + 1) * 256], QT.t[:, 2 * h + kk, t.cs], KT.t[:, 2 * h + kk, :],
                        kk == 0, kk == 1, [QT, KT], [Aq[h]])
        bi = self.rot("attP", 2)
        Pb, rs8, ri8 = self.Pb[bi], self.rs8[bi], self.ri8[bi]
        for h in range(4):
            self.act(Pb.t[:n, h * 256:(h + 1) * 256], A[:n, h * 256:(h + 1) * 256], AF.Exp, [Aq[h], rs8], [Pb, rs8],
                     scale=1.0 / 16, accum_out=rs8.t[:n, h:h + 1])
        self.P.op("dve", lambda e: e.reciprocal(ri8.t[:n, 0:4], rs8.t[:n, 0:4]), [rs8], [ri8])
        rb = ri8.t[:n, 0:4].unsqueeze(2).to_broadcast([n, 4, 256])
        pv = Pb.t[:n, :].rearrange("p (h k) -> p h k", h=4)
        self.tt("dve", pv, pv, rb, ALU.mult, [Pb, ri8], [Pb])
        return {"bi": bi, "t": t}

    def mem_T(self, ctx):
        t, bi = ctx["t"], ctx["bi"]
        n = t.n
        Pn, PT = self.Pb[bi], self.PT[bi]
        ti = 2 + self.rot("trb", 2)
        ptv = self.B[ti].t[:].bitcast(BF16).rearrange("p (c t) -> p c t", c=8)
        for h in range(4):
            for kb in range(2):
                c0 = h * 256 + kb * 128
                self.tr(ptv[:, 2 * h + kb, 0:n], Pn.t[:n, c0:c0 + 128], n, [Pn], self.Bq[ti])
        self.evac(PT.t[:, :, 0:n], ptv[:, :, 0:n], self.Bq[ti], [PT])

    def mem_PV(self, ctx, V):
        t, bi = ctx["t"], ctx["bi"]
        n = t.n
        PT = self.PT[bi]
        for h in range(4):
            for dc in range(2):
                c = 2 * h + dc
                bk = c // 4
                out = self.B[bk].t[:, (c % 4) * 128:(c % 4) * 128 + n]
                for kb in range(2):
                    self.mm(out, V.t[:, kb, h * 256 + dc * 128:h * 256 + dc * 128 + 128],
                            PT.t[:, 2 * h + kb, 0:n], kb == 0, kb == 1, [V, PT], self.Bq[bk])
        for bk in range(2):
            src = self.B[bk].t[:, :].rearrange("p (c t) -> p c t", c=4)[:, :, 0:n]
            self.evac(self.mixT.t[:, 4 * bk:4 * bk + 4, t.cs], src, self.Bq[bk], [self.mixT])

    def mem_attn_group(self, tiles, kv_of):
        ctx = self.mem_S(tiles[0], kv_of(tiles[0])[0])
        pend = None
        for ix, t in enumerate(tiles):
            nxt = self.mem_S(tiles[ix + 1], kv_of(tiles[ix + 1])[0]) if ix + 1 < len(tiles) else None
            self.mem_T(ctx)
            if pend is not None:
                self.mem_PV(pend[0], pend[1])
            pend = (ctx, kv_of(t)[1])
            ctx = nxt
        self.mem_PV(pend[0], pend[1])

    def stage_ffn(self, tiles):
        NTc = sum(t.n for t in tiles)
        hT = self.hT
        for jp in range(6):
            nch = 4 if jp < 5 else 2
            jg, sg = self.wget("w_g")
            gates = []
            for cc in range(nch):
                a, gb = self.Ahalf(cc)
                gps = a[:, 0:NTc]
                for k in range(8):
                    self.mm(gps, sg.t[:, k, cc * 128:(cc + 1) * 128], hT.t[:, k, 0:NTc], k == 0, k == 7, [sg, hT], gb)
                gates.append((gps, gb))
            self.wrel(jg)
            ju, su = self.wget("w_u")
            for cc in range(nch):
                f = jp * 4 + cc
                gps, gb = gates[cc]
                si = self.silu[self.rot("silu", 2)]
                self.act(si.t[:, 0:NTc], gps, AF.Silu, gb, [si])
                bi = self.rot("ffu", 2)
                ups = self.B[bi].t[:, 0:NTc]
                for k in range(8):
                    self.mm(ups, su.t[:, k, cc * 128:(cc + 1) * 128], hT.t[:, k, 0:NTc], k == 0, k == 7, [su, hT], self.Bq[bi])
                self.tt("dve", self.hidT.t[:, f, 0:NTc], si.t[:, 0:NTc], ups, ALU.mult, [si] + self.Bq[bi], [self.hidT])
            self.wrel(ju)
        self.load_gpost("g_post_ffn")
        self.proj_to_Y(self.hidT, "w_d", tiles, kgroups=(8, 8, 6))

    def group(self, tiles, first, last, g):
        P, cfg = self.P, self.cfg
        sample = tiles[0].kind == "s"
        NTc = sum(t.n for t in tiles)
        cur, prev = g % 2, (g + 1) % 2
        kaT_cur, va_cur = self.kaT[cur], self.va[cur]
        kaT_prev, va_prev = self.kaT[prev], self.va[prev]
        P.cur_stage = f"g{g}:load+norm"
        for t in tiles:
            src = self.xs[t.c0:t.c0 + t.n, :] if sample else self.xp[(g * cfg.gt + t.i) * 128:(g * cfg.gt + t.i + 1) * 128, :]
            P.dma("sp", self.X[t.i].t[:t.n, :], src, [], [self.X[t.i]])
        self.warm(self.cfg.warm_g)
        items = [(self.X[t.i], t.n, t.cs) for t in tiles]
        prs = [items[i:i + 2] for i in range(0, len(items), 2)]
        xbs = [self.prenorm_A(ip) for ip in prs]
        for ip, xb in zip(prs, xbs):
            self.prenorm_B(ip, 0, self.hT, xb)
        P.cur_stage = f"g{g}:win"
        self.stage_win(tiles, last, kaT_cur, va_cur)
        P.cur_stage = f"g{g}:mix"
        import os
        kg = int(os.environ.get("KG", "9"))
        if kg < 2:
            return
        for t in tiles:
            if sample:
                b = t.i
                self.load_state(b)

                def head_args(h, t=t, b=b):
                    p, hh = h // 2, h % 2
                    r = slice(hh * 64, hh * 64 + 64)
                    ckT, cv = self.ckT[b], self.cv[b]
                    ksegs = [(ckT.t[r, p, 0:512], 512, [ckT]), (self.kaTs.t[r, p, t.cs], 32, [self.kaTs])]
                    vbl = [(cv.t[:, kb, p * 128:(p + 1) * 128], 128, [cv]) for kb in range(4)]
                    vbl.append((self.vas[b].t[0:32, p * 128:(p + 1) * 128], 32, [self.vas[b]]))
                    return ((32, h, self.QT.t[r, p, t.cs], [self.QT], ksegs, self.Bm.t[0:32, h, 0:544], None),
                            vbl, self.mixT.t[r, p, t.cs])
                self.mix_tile(t, head_args)
                self.store_state(self.sglao[b, :, :])
            else:
                def head_args(h, t=t):
                    i = t.i
                    p, hh = h // 2, h % 2
                    r = slice(hh * 64, hh * 64 + 64)
                    ksegs, vbl = [], []
                    ksegs.append((kaT_prev.t[r, p, i * 128:512], (4 - i) * 128, [kaT_prev]))
                    ksegs.append((kaT_cur.t[r, p, 0:(i + 1) * 128], (i + 1) * 128, [kaT_cur]))
                    for kb in range(5):
                        if kb < 4 - i:
                            vbl.append((va_prev.t[:, i + kb, p * 128:(p + 1) * 128], 128, [va_prev]))
                        else:
                            vbl.append((va_cur.t[:, kb - (4 - i), p * 128:(p + 1) * 128], 128, [va_cur]))
                    eb = None
                    if first:
                        ne = (4 - i) * 128
                        eb = (self.onesb.t[0:1, 0:128], self.hm.t[0:1, i * 128:512], ne, [self.onesb, self.hm])
                    return ((128, h, self.QT.t[r, p, t.cs], [self.QT], ksegs, self.Bm.t[:, h, :], eb),
                            vbl, self.mixT.t[r, p, t.cs])
                self.mix_tile(t, head_args)
        if last and not sample:
            self.store_state(self.pgla[:, :])
        if kg < 3:
            return
        P.cur_stage = f"g{g}:wo"
        self.load_gpost("g_post_mix")
        self.proj_to_Y(self.mixT, "w_o", tiles)
        self.norm_transition(tiles, 1)
        P.cur_stage = f"g{g}:mem"
        for cc in range(2):
            j, slot = self.wget("w_mq")
            for c in range(4):
                self.proj_fm(slot, c * 128, 128, self.hT, NTc,
                             lambda ps, pb, c=c, cc=cc: self.evac(self.QT.t[:, cc * 4 + c, 0:NTc], ps, pb, [self.QT]))
            self.wrel(j)
        if sample:
            self.mem_attn_group(tiles, lambda t: (self.memKTs[t.i], self.memVs[t.i]))
        else:
            self.mem_attn_group(tiles, lambda t: (self.memKT, self.memV))
        self.load_gpost("g_post_mem")
        self.proj_to_Y(self.mixT, "w_mo", tiles)
        self.norm_transition(tiles, 2)
        P.cur_stage = f"g{g}:ffn"
        self.stage_ffn(tiles)
        P.cur_stage = f"g{g}:end"
        for i0 in range(0, len(tiles), 2):
            tp = tiles[i0:i0 + 2]
            self.postnorm_many(tp)
            for t in tp:
                if sample:
                    dst = self.ys[t.c0:t.c0 + t.n, :]
                else:
                    dst = self.yp[(g * cfg.gt + t.i) * 128:(g * cfg.gt + t.i + 1) * 128, :]
                P.dma("pool", dst, self.X[t.i].t[:t.n, :], [self.X[t.i]], [], out=True)

    def store_state(self, dst):
        for p in range(2):
            self.P.dma("pool", dst[p * 128:(p + 1) * 128, :], self.S[p].t[:, :], [self.S[p]], [], out=True)

    def load_state(self, b):
        for p in range(2):
            self.P.dma("sp", self.S[p].t[:, :], self.sgla[b, p * 128:(p + 1) * 128, :], [], [self.S[p]])
            self.copy_Sbf(p)

    def copy_Sbf(self, p):
        for hh in range(2):
            r = slice(hh * 64, hh * 64 + 64)
            self.copy("act", self.Sbf[p][hh].t[r, :], self.S[p].t[r, :], [self.S[p]], [self.Sbf[p][hh]])

    def mem_kv_prompt(self):
        import os
        P = self.P
        P.cur_stage = "memkv"
        sub = int(os.environ.get("KSUB", "9"))
        tiles = [Tile(i, 128, i * 128, "m") for i in range(2)]
        for t in tiles:
            P.dma("sp", self.X[t.i].t[:, :], self.memp[t.c0:t.c0 + 128, :], [], [self.X[t.i]])
        self.prenorm_many([(self.X[t.i], 128, t.cs) for t in tiles], 3, self.hT)
        if sub < 2:
            return
        for which in range(2):
            for cc in range(2):
                j, slot = self.wget("w_mk" if which == 0 else "w_mv")
                kvar = int(os.environ.get("KVAR", "9"))
                if kvar == 1:
                    self.copy("dve", self.Y[0].t[:, 0:512], slot.t[:, 0, 0:512], [slot], [self.Y[0]])
                    self.wrel(j)
                    continue
                for t in tiles:
                    if kvar == 2 and (which, cc, t.i) != (0, 0, 0):
                        continue
                    if kvar == 3 and (which, cc) != (0, 0):
                        continue
                    if kvar == 4 and which != 0:
                        continue
                    Y = self.Y[2 * which + t.i]

                    def ev(ps, pb, t=t, Y=Y, cc=cc, which=which):
                        e1 = self.ev_eng()
                        self.evac(Y.t[:, cc * 512:(cc + 1) * 512], ps, pb, [Y], eng=e1)
                        if which == 1 and kvar != 5:
                            self.evac(self.memV.t[:, t.i, cc * 512:(cc + 1) * 512], ps, pb, [self.memV],
                                      eng=(e1 if kvar == 7 else None))
                    self.proj_tm(slot, 0, 512, self.hT, t, ev)
                if which == 0 and sub >= 3:
                    for c in range(4):
                        self.proj_fm(slot, c * 128, 128, self.hT, 256,
                                     lambda ps, pb, c=c, cc=cc: self.evac(self.memKT.t[:, cc * 4 + c, :], ps, pb,
                                                                          [self.memKT]))
                self.wrel(j)
        if sub < 5:
            return
        for t in tiles:
            P.dma("pool", self.pmk[t.c0:t.c0 + 128, :], self.Y[t.i].t[:, :], [self.Y[t.i]], [], out=True)
            P.dma("pool", self.pmv[t.c0:t.c0 + 128, :], self.Y[2 + t.i].t[:, :], [self.Y[2 + t.i]], [], out=True)

    def halo_pass(self):
        P = self.P
        P.cur_stage = "halo"
        tiles = [Tile(i, 128, i * 128, "h") for i in range(4)]
        for t in tiles:
            P.dma("sp", self.X[t.i].t[:, :], self.xh[t.c0:t.c0 + 128, :], [], [self.X[t.i]])
        self.prenorm_many([(self.X[t.i], 128, t.cs) for t in tiles], 0, self.hT)
        kdst, vdst = self.kaT[1], self.va[1]
        j, slot = self.wget("w_in")
        for c in range(4):
            self.proj_fm(slot, c * 128, 128, self.hT, 512,
                         lambda ps, pb, c=c: self.evac(kdst.t[:, c, 0:512], ps, pb, [kdst]))
        self.wrel(j)
        j, slot = self.wget("w_in")
        for t in tiles:
            self.proj_tm(slot, 0, 512, self.hT, t,
                         lambda ps, pb, t=t: self.evac(vdst.t[:, t.i, :], ps, pb, [vdst]))
        self.wrel(j)

    def prefix_pass(self):
        P, cfg = self.P, self.cfg
        P.cur_stage = "prefix"
        if cfg.pref_t == 0:
            return
        ja, sa = self.wget("w_in")
        jb, sb_ = self.wget("w_in")
        for kc in range(8):
            self.ts("pool", sa.t[:, kc, 0:272], sa.t[:, kc, 0:272], self.gpre.t[:, 0, kc:kc + 1], ALU.mult,
                    [sa, self.gpre], [sa])
            self.ts("dve", sb_.t[:, kc, 0:512], sb_.t[:, kc, 0:512], self.gpre.t[:, 0, kc:kc + 1], ALU.mult,
                    [sb_, self.gpre], [sb_])
        S2 = [self.Y[p].t[:, 512:768] for p in range(2)]
        S2b = [Buf(f"S2_{p}") for p in range(2)]
        for p in range(2):
            self.memset("pool", S2[p], 0.0, [S2b[p], self.Y[p]])
        self.S2, self.S2b = S2, S2b
        G = cfg.pref_t // cfg.gt
        tiles = [Tile(i, 128, i * 128, "x") for i in range(cfg.gt)]
        items = [(self.X[t.i], 128, t.cs) for t in tiles]
        h = self.hidT.t
        kb1 = h[:, 0:2, :].rearrange("p a b -> p (a b)").rearrange("p (t f) -> p t f", f=256)
        vb1 = h[:, 2:6, :]
        sp1 = h[:, 6:10, :].bitcast(F32)
        glT1 = h[0:33, 10, :]
        B1 = [Buf("kb1"), Buf("vb1"), Buf("sp1"), Buf("glT1")]
        self.memset("pool", h[0:32, 10, :], 0.0, [B1[3]])
        self.memset("pool", h[32:33, 10, :], 1.0, [B1[3]])
        sets = [
            (self.kb.t, self.vb.t, self.sp.t, self.kb, self.vb, self.sp, self.glT.t, self.glT),
            (kb1, vb1, sp1, B1[0], B1[1], B1[2], glT1, B1[3]),
        ]

        def load(g):
            for t in tiles:
                r0 = (g * cfg.gt + t.i) * 128
                P.dma("sp", self.X[t.i].t[:, :], self.xpre[r0:r0 + 128, :], [], [self.X[t.i]])

        def proj(g):
            kb_ap, vb_ap, sp_ap, kbB, vbB, spB, glt, glB = sets[g % 2]
            self.proj_fm(sa, 256, 16, self.hT, self.NT,
                         lambda ps, pb: self.evac(glt[0:16, 0:self.NT], ps, pb, [glB]))
            for t in tiles:
                self.proj_tm(sa, 0, 256, self.hT, t,
                             lambda ps, pb, t=t: self.evac(kb_ap[:, t.i, :], ps, pb, [kbB]))
                self.proj_tm(sb_, 0, 512, self.hT, t,
                             lambda ps, pb, t=t: self.evac(vb_ap[:, t.i, :], ps, pb, [vbB]))
            pss = []
            for t in tiles:
                ap, bufs = self.Ahalf(self.rot("tmb", 4))
                ps = ap[:, 0:256]
                self.mm(ps, glt[0:33, t.cs], self.waext.t[0:33, :], True, True, [glB, self.waext], bufs)
                pss.append((ps, bufs))
            for t, (ps, bufs) in zip(tiles, pss):
                self.act(sp_ap[:, t.i, :], ps, AF.Exp, bufs, [spB], scale=-1.0)
            for t in tiles:
                self.act(sp_ap[:, t.i, :], sp_ap[:, t.i, :], AF.Ln, [spB, self.onesf], [spB], bias=self.onesf.t[:, 0:1])

        def bset(g):
            kb_ap, vb_ap, sp_ap, kbB, vbB, spB, glt, glB = sets[g % 2]
            return (kb_ap, vb_ap, sp_ap, kbB, vbB, spB)

        load(0)
        xbs = self.prenorm_A(items)
        self.prenorm_B(items, None, self.hT, xbs)
        for g in range(G):
            if g + 1 < G:
                load(g + 1)
                sx = self.prenorm_A1(items)
            proj(g)
            if g + 1 < G:
                xbs = self.prenorm_A2(items, sx)
            self.convert_more(6, after=[sets[g % 2][3]])
            if g >= 1:
                self.prefix_sufs(tiles, bset(g - 1))
            if g + 1 < G:
                self.prenorm_B(items, None, self.hT, xbs)
            if g >= 1:
                self.prefix_ds(tiles, bset(g - 1))
        self.prefix_sufs(tiles, bset(G - 1))
        self.prefix_ds(tiles, bset(G - 1))
        self.convert_more(len(self.conv_todo))
        for p in range(2):
            for hh in range(2):
                r = slice(hh * 64, hh * 64 + 64)
                self.copy("dve", self.S[p].t[r, :], self.S2[p][r, hh * 128:(hh + 1) * 128], [self.S2b[p]], [self.S[p]])
            self.copy_Sbf(p)
        self.wrel(ja)
        self.wrel(jb)

    def sample_setup(self):
        P = self.P
        P.cur_stage = "ssetup"
        ctok_b, cmtok_b = self.Y[0], self.Y[1]
        ctok = ctok_b.t[:, :].bitcast(BF16).rearrange("p (k f) -> p k f", k=4)
        cmtok = cmtok_b.t[:, :].bitcast(BF16).rearrange("p (k f) -> p k f", k=2)
        for b in range(2):
            ckT, cv = self.ckT[b], self.cv[b]
            P.dma("pool", ctok[:, :, :], self.cak[b].rearrange("(k p) f -> p k f", p=128), [], [ctok_b])
            P.dma("pool", cv.t[:, :, :], self.cav[b].rearrange("(k p) f -> p k f", p=128), [], [cv])
            for half in range(2):
                ti = 2 + self.rot("trb", 2)
                ptv = self.B[ti].t[:].bitcast(BF16).rearrange("p (c t) -> p c t", c=8)
                for pl in range(2):
                    p = 2 * half + pl
                    for kb in range(4):
                        self.tr(ptv[:, pl * 4 + kb, :], ctok[:, kb, p * 128:(p + 1) * 128], 128,
                                [ctok_b], self.Bq[ti])
                dst = ckT.t[:, 2 * half:2 * half + 2, :].rearrange("p a (k t) -> p (a k) t", t=128)
                self.evac(dst, ptv[:, :, :], self.Bq[ti], [ckT])
            P.dma("pool", cmtok[:, :, :], self.cmk[b].rearrange("(k p) f -> p k f", p=128), [], [cmtok_b])
            P.dma("pool", self.memVs[b].t[:, :, :], self.cmv[b].rearrange("(k p) f -> p k f", p=128), [],
                  [self.memVs[b]])
            for half in range(2):
                ti = 2 + self.rot("trb", 2)
                ptv = self.B[ti].t[:].bitcast(BF16).rearrange("p (c t) -> p c t", c=8)
                for cl in range(4):
                    c = 4 * half + cl
                    for kb in range(2):
                        self.tr(ptv[:, cl * 2 + kb, :], cmtok[:, kb, c * 128:(c + 1) * 128], 128,
                                [cmtok_b], self.Bq[ti])
                dst = self.memKTs[b].t[:, 4 * half:4 * half + 4, :].rearrange("p c (k t) -> p (c k) t", t=128)
                self.evac(dst, ptv[:, :, :], self.Bq[ti], [self.memKTs[b]])


def _bias_toeplitz(table):
    i = np.arange(128)[:, None]
    j = np.arange(640)[None, :]
    idx = np.clip(512 + i - j, -256, 256) + 256
    bm = table[:, idx]
    d = (j // 64) - (i // 64)
    valid = (d >= 0) & (d <= 8)
    return np.where(valid[None], bm, np.float32(NEG)).astype(np.float32)


_CACHE = {}


def _get_nc(cfg_key):
    if cfg_key not in _CACHE:
        _CACHE[cfg_key] = Builder(Cfg(*cfg_key)).build()
    return _CACHE[cfg_key]


def kernel(x_prompt, x_sample, mem_prompt, cache_a_k, cache_a_v, state_gla, cache_mem_k, cache_mem_v,
           g_pre_mix, w_in, rel_bias, w_alpha2, b_alpha, g_gla_out, w_o, g_post_mix,
           g_pre_mem, g_mem, w_mq, w_mk, w_mv, w_mo, g_post_mem,
           g_pre_ffn, w_ffn_gate, w_ffn_up, w_ffn_down, g_post_ffn, _ring=3, _warm_y=0, _warm_g=0):
    f = lambda a: np.ascontiguousarray(np.asarray(a, dtype=np.float32))
    x_prompt, x_sample, mem_prompt = f(x_prompt), f(x_sample), f(mem_prompt)
    nb, seq, _ = x_prompt.shape
    ncore = 8
    qn = ncore // nb
    tpc = seq // qn
    npt = tpc // 128
    pref_t = (qn - 1) * npt
    cfg_key = (npt, pref_t, 4, _ring, _warm_y, _warm_g)
    nc = _get_nc(cfg_key)
    bm = _bias_toeplitz(f(rel_bias)[0])
    shared = {
        "bm": bm,
        "g_pre_mix": f(g_pre_mix), "g_post_mix": f(g_post_mix), "g_pre_mem": f(g_pre_mem), "g_mem": f(g_mem),
        "g_post_mem": f(g_post_mem), "g_pre_ffn": f(g_pre_ffn), "g_post_ffn": f(g_post_ffn),
        "g_gla_out": f(g_gla_out), "w_alpha2": f(w_alpha2)[0], "b_alpha": f(b_alpha),
        "w_in": f(w_in)[0], "w_o": f(w_o)[0], "w_mq": f(w_mq)[0], "w_mk": f(w_mk)[0], "w_mv": f(w_mv)[0],
        "w_mo": f(w_mo)[0], "w_ffn_gate": f(w_ffn_gate)[0], "w_ffn_up": f(w_ffn_up)[0],
        "w_ffn_down": f(w_ffn_down)[0],
    }
    cak, cav = f(cache_a_k)[0].reshape(16, 512, 512), f(cache_a_v)[0].reshape(16, 512, 512)
    sg = f(state_gla)[0].reshape(16, 256, 128)
    cmk, cmv = f(cache_mem_k)[0].reshape(16, 256, 1024), f(cache_mem_v)[0].reshape(16, 256, 1024)
    in_maps = []
    for c in range(ncore):
        s, q = c // qn, c % qn
        t0 = q * tpc
        xpre = np.zeros((max(pref_t, 1) * 128, D), np.float32)
        if q > 0:
            xpre[pref_t * 128 - t0:] = x_prompt[s, 0:t0]
        xh = np.zeros((512, D), np.float32)
        if q > 0:
            xh[:] = x_prompt[s, t0 - 512:t0]
        hmask = np.full((1, 512), NEG if q == 0 else 0.0, np.float32)
        m = dict(shared)
        m.update({
            "xp": x_prompt[s, t0:t0 + tpc], "xh": xh, "xpre": xpre,
            "hmask": hmask,
            "xs": x_sample[2 * c:2 * c + 2].reshape(64, D), "memp": mem_prompt[s],
            "cak": cak[2 * c:2 * c + 2], "cav": cav[2 * c:2 * c + 2], "sgla": sg[2 * c:2 * c + 2],
            "cmk": cmk[2 * c:2 * c + 2], "cmv": cmv[2 * c:2 * c + 2],
        })
        in_maps.append({k: np.ascontiguousarray(v) for k, v in m.items()})
    import os as _os
    if _os.environ.get("KONE"):
        res = run_bass_kernel_spmd(nc, in_maps[:1], core_ids=[0])
        R = [res.results[0]] * ncore
    else:
        res = run_bass_kernel_spmd(nc, in_maps, core_ids=list(range(ncore)))
        R = res.results
    yp = np.stack([np.concatenate([R[s * qn + q]["yp"] for q in range(qn)], 0) for s in range(nb)])
    ys = np.concatenate([R[c]["ys"] for c in range(ncore)], 0).reshape(16, 32, D)
    last = [s * qn + qn - 1 for s in range(nb)]
    pak = np.stack([R[c]["pak"] for c in last]).reshape(1, nb, 512, 8, 64)
    pav = np.stack([R[c]["pav"] for c in last]).reshape(1, nb, 512, 8, 64)
    pgla = np.stack([R[c]["pgla"] for c in last]).reshape(1, nb, 4, 64, 128)
    firstc = [s * qn for s in range(nb)]
    pmk = np.stack([R[c]["pmk"] for c in firstc]).reshape(1, nb, 256, 4, 256)
    pmv = np.stack([R[c]["pmv"] for c in firstc]).reshape(1, nb, 256, 4, 256)
    sak = np.concatenate([R[c]["sak"] for c in range(ncore)], 0).reshape(1, 16, 32, 8, 64)
    sav = np.concatenate([R[c]["sav"] for c in range(ncore)], 0).reshape(1, 16, 32, 8, 64)
    sgl = np.concatenate([R[c]["sglao"] for c in range(ncore)], 0).reshape(1, 16, 4, 64, 128)
    outs = (yp, ys, pak, pav, pgla, pmk, pmv, sak, sav, sgl)
    return tuple(np.ascontiguousarray(o.astype(np.float32)) for o in outs)
```

```python
import numpy as np
import ml_dtypes
from contextlib import ExitStack
import concourse.bass as bass
import concourse.mybir as mybir
from concourse.bass_utils import run_bass_kernel_spmd

F32 = mybir.dt.float32
BF16 = mybir.dt.bfloat16
AF = mybir.ActivationFunctionType
ALU = mybir.AluOpType
AX = mybir.AxisListType

COMPUTE = ("pe", "act", "dve", "pool")
QUEUES = ("sp", "act", "pool")


class Buf:
    __slots__ = ("name", "t", "w", "r", "excl")

    def __init__(self, name, t=None, excl=False):
        self.name = name
        self.t = t
        self.w = None
        self.r = {}
        self.excl = excl

    def sub(self, n):
        return [Buf(f"{self.name}.{i}", self.t, self.excl) for i in range(n)]


class Prog:
    ENGS = ("pe", "act", "dve", "pool", "sp")

    def __init__(self, nc, es, dma_pool=10):
        self.nc = nc
        self.es = es
        self.ops = {e: [] for e in self.ENGS}
        self.seen_e = {e: {} for e in self.ENGS}
        self.seen_d = {e: {} for e in self.ENGS}
        self.dpool = {q: [(q, i) for i in range(dma_pool)] for q in ("sp", "pool", "act")}
        self.dnext = {q: 0 for q in self.dpool}
        self.dcount = {}
        self.out_tokens = {}
        self.dram = {}
        self.n_sb = 0

    def sb(self, name, shape, dtype):
        t = self.es.enter_context(self.nc.sbuf_tensor(name, list(shape), dtype))
        return Buf(name, t)

    def ps(self, name, shape, dtype):
        t = self.es.enter_context(self.nc.psum_tensor(name, list(shape), dtype))
        return Buf(name, t, excl=True)

    def dbuf(self, name):
        if name not in self.dram:
            self.dram[name] = Buf("dram:" + name)
        return self.dram[name]

    def _deps(self, eng, reads, writes, is_dma=False):
        raw, other = [], []
        for b in reads:
            if b.w is not None:
                raw.append(b.w)
            if b.excl:
                other.extend(b.r.values())
        for b in writes:
            if b.w is not None:
                other.append(b.w)
            other.extend(b.r.values())
        waits = []
        for tok, is_raw in [(t, True) for t in raw] + [(t, False) for t in other]:
            if tok[0] == "e":
                _, pe, idx = tok
                if pe == eng and not is_raw and not is_dma:
                    continue
                if pe == eng and eng == "pe":
                    continue
                if self.seen_e[eng].get(pe, -1) >= idx:
                    continue
                self.seen_e[eng][pe] = idx
                self.ops[pe][idx]["signal"] = True
                waits.append(tok)
            else:
                _, key, val = tok
                if self.seen_d[eng].get(key, 0) >= val:
                    continue
                self.seen_d[eng][key] = val
                waits.append(tok)
        return waits

    def _mark(self, tok, reads, writes):
        for b in reads:
            k = (tok[0], tok[1])
            b.r[k] = tok
        for b in writes:
            b.w = tok
            b.r = {}

    def op(self, eng, fn, reads=(), writes=()):
        waits = self._deps(eng, reads, writes)
        idx = len(self.ops[eng])
        self.ops[eng].append({"fn": fn, "waits": waits, "signal": False, "dma": None,
                              "stage": getattr(self, "cur_stage", "")})
        tok = ("e", eng, idx)
        self._mark(tok, [b for b in reads if b not in writes], writes)
        return tok

    def dma(self, q, out_ap, in_ap, reads=(), writes=(), out=False, dram_r=(), dram_w=(), **kw):
        reads = list(reads) + [self.dbuf(n) for n in dram_r]
        writes = list(writes) + [self.dbuf(n) for n in dram_w]
        pool = self.dpool[q]
        key = pool[self.dnext[q] % len(pool)]
        self.dnext[q] += 1
        k = self.dcount.get(key, 0)
        waits = self._deps(q, reads, writes, is_dma=True)
        if k > 0 and self.seen_d[q].get(key, 0) < 16 * k:
            self.seen_d[q][key] = 16 * k
            waits.append(("d", key, 16 * k))
        self.dcount[key] = k + 1
        tok = ("d", key, 16 * (k + 1))
        fn = lambda e, o=out_ap, i=in_ap, kw=kw: e.dma_start(out=o, in_=i, **kw)
        self.ops[q].append({"fn": fn, "waits": waits, "signal": False, "dma": key})
        self._mark(tok, reads, writes)
        if out:
            self.out_tokens[key] = tok
        return tok

    def finish(self):
        nc = self.nc
        fin = []
        for key, tok in self.out_tokens.items():
            if self.seen_d["sp"].get(key, 0) < tok[2]:
                fin.append(tok)
        es = self.es
        esem = {e: es.enter_context(nc.semaphore("tl_" + e)) for e in COMPUTE}
        dsem = {}
        for key in self.dcount:
            dsem[key] = es.enter_context(nc.semaphore(f"d_{key[0]}_{key[1]}"))
        cum = {}
        for e in COMPUTE:
            c = 0
            arr = []
            for o in self.ops[e]:
                if o["signal"]:
                    c += 1
                arr.append(c)
            cum[e] = arr
            assert c < 60000, (e, c)
        self.stats = {e: (len(self.ops[e]), cum[e][-1] if e in cum and cum[e] else 0) for e in self.ENGS}

        def emit(eng_name, eng):
            for o in self.ops[eng_name]:
                for tok in o["waits"]:
                    if tok[0] == "e":
                        eng.wait_ge(esem[tok[1]], cum[tok[1]][tok[2]])
                    else:
                        eng.wait_ge(dsem[tok[1]], tok[2])
                inst = o["fn"](eng)
                if o["dma"] is not None:
                    inst.then_inc(dsem[o["dma"]], 16)
                elif o["signal"]:
                    inst.then_inc(esem[eng_name], 1)
            if eng_name == "sp":
                for tok in fin:
                    eng.wait_ge(dsem[tok[1]], tok[2])

        with nc.Block() as block:
            @block.sync
            def _(e):
                emit("sp", e)

            @block.tensor
            def _(e):
                emit("pe", e)

            @block.scalar
            def _(e):
                emit("act", e)

            @block.vector
            def _(e):
                emit("dve", e)

            @block.gpsimd
            def _(e):
                emit("pool", e)


D = 1024
IN_W = 3088
DFF = 2816
NFF = DFF // 128
C_QA, C_KA, C_VA, C_QB, C_KB, C_VB, C_GL, C_R = 0, 512, 1024, 1536, 1792, 2048, 2560, 2576
EPS = 1e-6
NEG = -1e30
SLOTW = 528


class Cfg:
    def __init__(self, npt=32, pref_t=96, gt=4, ring=3, warm_y=0, warm_g=0):
        self.warm_y, self.warm_g = warm_y, warm_g
        self.npt = npt
        self.pref_t = pref_t
        self.gt = gt
        self.ring = ring
        assert npt % gt == 0 and pref_t % gt == 0


class Tile:
    def __init__(self, i, n, c0, kind):
        self.i, self.n, self.c0, self.kind = i, n, c0, kind

    @property
    def cs(self):
        return slice(self.c0, self.c0 + self.n)


class Builder:
    def __init__(self, cfg):
        self.cfg = cfg
        self.nc = bass.Bass("TRN2", target_bir_lowering=False)

    def dram_in(self, name, shape, dt=F32):
        return self.nc.dram_tensor(name, list(shape), dt, kind="ExternalInput").ap()

    def dram_out(self, name, shape, dt=F32):
        return self.nc.dram_tensor(name, list(shape), dt, kind="ExternalOutput").ap()

    def dram_tmp(self, name, shape, dt):
        return self.nc.dram_tensor(name, list(shape), dt).ap()

    def build(self):
        cfg = self.cfg
        with ExitStack() as es:
            self.P = Prog(self.nc, es)
            self.declare_io()
            self.alloc()
            import os
            stop = int(os.environ.get("KSTOP", "99"))
            self.setup_consts()
            if stop >= 2:
                self.convert_weights()
            self.make_schedule()
            if stop >= 3:
                self.mem_kv_prompt()
            if stop >= 4:
                self.halo_pass()
            if stop >= 5:
                self.prefix_pass()
            ng = cfg.npt // cfg.gt
            if stop >= 6:
                for g in range(ng):
                    tiles = [Tile(i, 128, i * 128, "p") for i in range(cfg.gt)]
                    self.group(tiles, first=(g == 0), last=(g == ng - 1), g=g)
            if stop >= 7:
                self.sample_setup()
                tiles = [Tile(i, 32, i * 32, "s") for i in range(2)]
                self.group(tiles, first=False, last=True, g=ng)
            self.P.finish()
        return self.nc

    def declare_io(self):
        c = self.cfg
        di, do = self.dram_in, self.dram_out
        self.xp = di("xp", [c.npt * 128, D])
        self.xh = di("xh", [512, D])
        self.xpre = di("xpre", [max(c.pref_t, 1) * 128, D])
        self.hmask = di("hmask", [1, 512])
        self.xs = di("xs", [64, D])
        self.memp = di("memp", [256, D])
        self.cak = di("cak", [2, 512, 512])
        self.cav = di("cav", [2, 512, 512])
        self.sgla = di("sgla", [2, 256, 128])
        self.cmk = di("cmk", [2, 256, D])
        self.cmv = di("cmv", [2, 256, D])
        self.bm_in = di("bm", [8, 128, 640])
        self.g = {n: di(n, [1, D]) for n in ("g_pre_mix", "g_post_mix", "g_pre_mem", "g_mem", "g_post_mem",
                                            "g_pre_ffn", "g_post_ffn")}
        self.g_gla = di("g_gla_out", [1, 512])
        self.w_alpha2 = di("w_alpha2", [16, 256])
        self.b_alpha = di("b_alpha", [1, 256])
        self.w = {"w_in": di("w_in", [D, IN_W]), "w_o": di("w_o", [D, D]), "w_mq": di("w_mq", [D, D]),
                  "w_mk": di("w_mk", [D, D]), "w_mv": di("w_mv", [D, D]), "w_mo": di("w_mo", [D, D]),
                  "w_g": di("w_ffn_gate", [D, DFF]), "w_u": di("w_ffn_up", [D, DFF]),
                  "w_d": di("w_ffn_down", [DFF, D])}
        self.wb = {k: self.dram_out(k + "_bf", list(v.shape), BF16) for k, v in self.w.items()}
        self.yp = do("yp", [c.npt * 128, D])
        self.ys = do("ys", [64, D])
        self.pak = do("pak", [512, 512])
        self.pav = do("pav", [512, 512])
        self.pgla = do("pgla", [256, 128])
        self.pmk = do("pmk", [256, D])
        self.pmv = do("pmv", [256, D])
        self.sak = do("sak", [64, 512])
        self.sav = do("sav", [64, 512])
        self.sglao = do("sglao", [2, 256, 128])

    def alloc(self):
        P, c = self.P, self.cfg
        sb, ps = P.sb, P.ps
        gt = c.gt
        NT = gt * 128
        self.NT = NT
        self.X = [sb(f"X{i}", [128, D], F32) for i in range(gt)]
        self.Y = [sb(f"Y{i}", [128, D], F32) for i in range(gt)]
        self.hT = sb("hT", [128, 8, NT], BF16)
        self.QT = sb("QT", [128, 8, NT], BF16)
        self.kaT = [sb(f"kaT{i}", [128, 4, NT], BF16) for i in range(2)]
        self.va = [sb(f"va{i}", [128, gt, 512], BF16) for i in range(2)]
        self.kb = sb("kb", [128, gt, 256], BF16)
        self.vb = sb("vb", [128, gt, 512], BF16)
        self.sp = sb("sp", [128, gt, 256], F32)
        self.glT = sb("glT", [33, NT], BF16)
        self.rT = sb("rT", [128, 4, NT], BF16)
        self.mixT = sb("mixT", [128, 8, NT], BF16)
        self.hidT = sb("hidT", [128, NFF, NT], BF16)
        self.Bm = sb("Bm", [128, 8, 640], BF16)
        self.gpost = sb("gpost", [128, D], F32)
        self.gpre = sb("gpre", [128, 4, 8], F32)
        self.ggla = sb("ggla", [128, 4], F32)
        self.waext = sb("waext", [33, 256], BF16)
        self.hm = sb("hm", [1, 512], BF16)
        self.memKT = sb("memKT", [128, 8, 256], BF16)
        self.memV = sb("memV", [128, 2, D], BF16)
        self.ident = sb("ident", [128, 128], BF16)
        self.Uf = sb("Uf", [128, 128], F32)
        self.SLf = sb("SLf", [128, 128], F32)
        self.Ub = sb("Ub", [128, 128], BF16)
        self.onesb = sb("onesb", [128, 128], BF16)
        self.onesf = sb("onesf", [128, 1], F32)
        self.S = [sb(f"S{p}", [128, 128], F32) for p in range(2)]
        self.Sbf = [[sb(f"Sbf{p}{hh}", [128, 128], BF16) for hh in range(2)] for p in range(2)]
        self.xsb = [sb(f"xsb{i}", [128, D], BF16) for i in range(4)]
        self.st = [sb(f"st{i}", [128, 8], F32) for i in range(4)]
        self.Pb = [sb(f"Pb{i}", [128, 1024], BF16) for i in range(2)]
        self.Pn = self.Pb
        self.PT = [sb(f"PT{i}", [128, 8, 128], BF16) for i in range(2)]
        self.rs8 = [sb(f"rs8{i}", [128, 8], F32) for i in range(2)]
        self.ri8 = [sb(f"ri8{i}", [128, 8], F32) for i in range(2)]
        self.Ep = [sb(f"Ep{p}", [128, 128], F32) for p in range(2)]
        self.Em = [sb(f"Em{p}", [128, 128], F32) for p in range(2)]
        self.qtl = [sb(f"qtl{p}", [128, 128], BF16) for p in range(2)]
        self.ktl = [[sb(f"ktl{p}{hh}", [128, 128], BF16) for hh in range(2)] for p in range(2)]
        self.kdw = sb("kdw", [128, 256], F32)
        self.kdec = sb("kdec", [128, 256], BF16)
        self.kdec4 = [self.kdec] + [sb(f"kdec{i}", [128, 256], BF16) for i in range(1, 4)]
        self.dec8 = sb("dec8", [128, 8], F32)
        self.dec = [sb(f"dec{p}", [128, 1], F32) for p in range(2)]
        self.ATm = [sb(f"ATm{i}", [128, 128], BF16) for i in range(4)]
        self.sq = [sb(f"sq{i}", [128, 128], BF16) for i in range(4)]
        self.rs = [sb(f"rs{i}", [128, 128], F32) for i in range(4)]
        self.t1 = [sb(f"t1{i}", [128, 128], F32) for i in range(4)]
        self.silu = [sb(f"silu{i}", [128, NT], BF16) for i in range(2)]
        self.ring = [sb(f"ring{i}", [128, 8, SLOTW], BF16) for i in range(c.ring)]
        self.ckT = self.kaT
        self.cv = self.va
        self.kaTs = sb("kaTs", [128, 4, 64], BF16)
        self.vas = [sb(f"vas{b}", [32, 512], BF16) for b in range(2)]
        self.memKTs = [self.memKT, sb("memKTs1", [128, 8, 256], BF16)]
        self.memVs = [self.memV, sb("memVs1", [128, 2, D], BF16)]
        self.A = [ps(f"psA{i}", [128, 1024], F32) for i in range(2)]
        self.B = [ps(f"psB{i}", [128, 512], F32) for i in range(4)]
        self.rr = {}
        self.init_psum_regions()

    def rot(self, key, n):
        v = self.rr.get(key, 0)
        self.rr[key] = v + 1
        return v % n

    def mm(self, out, lhsT, rhs, start, stop, reads, writes):
        self.P.op("pe", lambda e: e.matmul(out, lhsT, rhs, start=start, stop=stop), reads, writes)

    def tr(self, out, in_, n, reads, writes):
        idn = self.ident.t[:n, :n]
        self.P.op("pe", lambda e: e.transpose(out, in_, idn), list(reads) + [self.ident], writes)

    def act(self, out, in_, func, reads, writes, **kw):
        self.P.op("act", lambda e: e.activation(out, in_, func, **kw), reads, writes)

    def copy(self, eng, out, in_, reads, writes):
        if eng == "act":
            self.P.op("act", lambda e: e.copy(out, in_), reads, writes)
        else:
            self.P.op(eng, lambda e: e.tensor_copy(out, in_), reads, writes)

    def tt(self, eng, out, a, b, op, reads, writes):
        self.P.op(eng, lambda e: e.tensor_tensor(out, a, b, op), reads, writes)

    def ts(self, eng, out, a, s1, op0, reads, writes, s2=None, op1=None):
        if op1 is None:
            self.P.op(eng, lambda e: e.tensor_scalar(out, a, s1, None, op0), reads, writes)
        else:
            self.P.op(eng, lambda e: e.tensor_scalar(out, a, s1, s2, op0, op1), reads, writes)

    def stt(self, eng, out, a, s, b, op0, op1, reads, writes):
        self.P.op(eng, lambda e: e.scalar_tensor_tensor(out, a, s, b, op0, op1), reads, writes)

    def memset(self, eng, ap, val, writes):
        self.P.op(eng, lambda e: e.memset(ap, val), [], writes)

    def rstd(self, dst, src, inv_n, reads, writes):
        self.act(dst, src, AF.Ln, list(reads) + [self.epsc], writes, scale=inv_n, bias=self.epsc.t[:dst.shape[0], 0:1])
        self.act(dst, dst, AF.Exp, writes, writes, scale=-0.5)

    def setup_consts(self):
        P = self.P
        self.epsc = P.sb("epsc", [128, 1], F32)
        self.memset("pool", self.epsc.t[:], EPS, [self.epsc])
        tmpf = P.sb("tmpf", [128, 128], F32)
        self.memset("pool", tmpf.t[:], 1.0, [tmpf])
        P.op("pool", lambda e: e.affine_select(tmpf.t[:], tmpf.t[:], [[-1, 128]], ALU.is_equal, 0.0,
                                              base=0, channel_multiplier=1), [tmpf], [tmpf])
        self.copy("dve", self.ident.t[:], tmpf.t[:], [tmpf], [self.ident])
        self.memset("pool", self.Uf.t[:], 1.0, [self.Uf])
        P.op("pool", lambda e: e.affine_select(self.Uf.t[:], self.Uf.t[:], [[1, 128]], ALU.is_ge, 0.0,
                                              base=0, channel_multiplier=-1), [self.Uf], [self.Uf])
        self.copy("dve", self.Ub.t[:], self.Uf.t[:], [self.Uf], [self.Ub])
        self.memset("pool", self.SLf.t[:], 1.0, [self.SLf])
        P.op("pool", lambda e: e.affine_select(self.SLf.t[:], self.SLf.t[:], [[-1, 128]], ALU.is_gt, 0.0,
                                              base=0, channel_multiplier=1), [self.SLf], [self.SLf])
        self.memset("pool", self.onesb.t[:], 1.0, [self.onesb])
        self.memset("pool", self.onesf.t[:], 1.0, [self.onesf])
        for p in range(2):
            self.memset("pool", self.S[p].t[:], 0.0, [self.S[p]])
            for hh in range(2):
                self.memset("pool", self.Sbf[p][hh].t[:], 0.0, [self.Sbf[p][hh]])
                self.memset("pool", self.ktl[p][hh].t[:], 0.0, [self.ktl[p][hh]])
        self.identf = tmpf
        g8 = P.sb("g8", [8, 5, 128], F32)
        for i, n in enumerate(("g_pre_mix", "g_pre_mem", "g_pre_ffn", "g_mem")):
            P.dma("sp", g8.t[:, i, :], self.g[n].rearrange("o (c p) -> (o c) p", p=128), [], [g8])
        P.dma("sp", g8.t[0:4, 4, :], self.g_gla.rearrange("o (c p) -> (o c) p", p=128), [], [g8])
        pg = self.B[0].t[:, 0:40]
        for i in range(5):
            nr = 8 if i < 4 else 4
            P.op("pe", lambda e, i=i, nr=nr: e.transpose(pg[:, i * 8:i * 8 + nr], g8.t[0:nr, i, :], tmpf.t[0:nr, 0:nr]),
                 [g8, tmpf], self.Bq[0])
        self.copy("dve", self.gpre.t[:, :, :], pg[:, 0:32].rearrange("p (a c) -> p a c", a=4), self.Bq[0], [self.gpre])
        self.copy("dve", self.ggla.t[:, :], pg[:, 32:36], self.Bq[0], [self.ggla])
        self.memset("pool", self.waext.t[:], 0.0, [self.waext])
        P.dma("pool", self.waext.t[0:16, :], self.w_alpha2[:, :], [], [self.waext])
        P.dma("pool", self.waext.t[32:33, :], self.b_alpha[:, :], [], [self.waext])
        self.memset("pool", self.glT.t[0:32, :], 0.0, [self.glT])
        self.memset("pool", self.glT.t[32:33, :], 1.0, [self.glT])
        P.dma("pool", self.Bm.t[:], self.bm_in.rearrange("h p j -> p h j"), [], [self.Bm])
        P.dma("pool", self.hm.t[:], self.hmask[:, :], [], [self.hm])

    def load_gpost(self, name):
        src = self.g[name][0:1, :].partition_broadcast(128)
        self.P.dma("sp", self.gpost.t[:], src[:, 0, :], [], [self.gpost])

    def convert_weights(self):
        self.conv_todo = []
        for k in ["w_mk", "w_mv", "w_in", "w_o", "w_mq", "w_mo", "w_g", "w_u", "w_d"]:
            rows = self.w[k].shape[0]
            for r0 in range(0, rows, 256):
                self.conv_todo.append((k, r0, min(rows, r0 + 256)))
        n_now = 16 if self.cfg.pref_t else len(self.conv_todo)
        self.convert_more(n_now)

    def convert_more(self, n, after=()):
        for _ in range(n):
            if not self.conv_todo:
                return
            k, r0, r1 = self.conv_todo.pop(0)
            self.P.dma("pool", self.wb[k][r0:r1, :], self.w[k][r0:r1, :], list(after), [], dram_w=[f"{k}:{r0 // 256}"])

    def warm(self, n):
        for _ in range(n):
            self.mm(self.B[1].t[:, 0:512], self.ident.t[:, :], self.Bm.t[:, 0, 0:512], True, True,
                    [self.ident, self.Bm], self.Bq[1])

    def make_schedule(self):
        c = self.cfg
        W8 = lambda k, c0, n: (k, 0, 8, [(c0, n, 0)])
        halo = [W8("w_in", C_KA, 512), W8("w_in", C_VA, 512)]
        pref = [("w_in", 0, 8, [(C_KB, 256, 0), (C_GL, 16, 256)]), W8("w_in", C_VB, 512)]
        memkv = [W8("w_mk", 0, 512), W8("w_mk", 512, 512), W8("w_mv", 0, 512), W8("w_mv", 512, 512)]
        grp = [W8("w_in", C_QA, 512), W8("w_in", C_KA, 512), W8("w_in", C_VA, 512), W8("w_in", C_QB, 512),
               W8("w_in", C_VB, 512), W8("w_in", C_GL, 528),
               W8("w_o", 0, 512), W8("w_o", 512, 512), W8("w_mq", 0, 512), W8("w_mq", 512, 512),
               W8("w_mo", 0, 512), W8("w_mo", 512, 512)]
        for j in range(6):
            n = 512 if j < 5 else 256
            grp += [W8("w_g", j * 512, n), W8("w_u", j * 512, n)]
        for cc in range(2):
            for kg in range(3):
                nk = 8 if kg < 2 else 6
                grp.append(("w_d", kg * 1024, nk, [(cc * 512, 512, 0)]))
        ng = c.npt // c.gt + 1
        self.sched = memkv + halo + (pref if c.pref_t else []) + grp * ng
        self.r_cur = 0
        self.r_loaded = 0
        self.r_rel = set()

    def _pump(self):
        n = len(self.ring)
        while (self.r_loaded < len(self.sched) and self.r_loaded < self.r_cur + n
               and (self.r_loaded < n or (self.r_loaded - n) in self.r_rel)):
            j = self.r_loaded
            key, row0, nk, pieces = self.sched[j]
            slot = self.ring[j % n]
            for (c0, ncol, s0) in pieces:
                src = self.wb[key][row0:row0 + nk * 128, c0:c0 + ncol].rearrange("(c p) f -> p c f", p=128)
                blks = [f"{key}:{b}" for b in range(row0 // 256, (row0 + nk * 128 + 255) // 256)]
                self.P.dma("sp", slot.t[:, 0:nk, s0:s0 + ncol], src, [], [slot], dram_r=blks)
            self.r_loaded += 1

    def wget(self, expect=None):
        j = self.r_cur
        self.r_cur += 1
        self._pump()
        assert self.r_loaded > j, ("ring stall", j)
        if expect is not None:
            assert self.sched[j][0] == expect, (self.sched[j], expect)
        return j, self.ring[j % len(self.ring)]

    def wrel(self, j):
        self.r_rel.add(j)
        self._pump()

    def init_psum_regions(self):
        self.Aq = []
        for a in self.A:
            lo, hi = a.sub(2)
            self.Aq.append([lo, lo, hi, hi])
        self.Bq = [[b, b, b, b] for b in self.B]

    def Ahalf(self, i):
        a, h = i // 2, i % 2
        return self.A[a].t[:, h * 512:(h + 1) * 512], self.Aq[a][2 * h:2 * h + 2]

    def ev_eng(self):
        return ("act", "dve")[self.rot("ev", 2)]

    def evac(self, out, in_, reads, writes, scale=None, eng=None):
        eng = eng or self.ev_eng()
        if scale is None:
            self.copy(eng, out, in_, reads, writes)
        elif eng == "act":
            self.act(out, in_, AF.Copy, reads, writes, scale=scale)
        else:
            self.ts("dve", out, in_, scale, ALU.mult, reads, writes)

    def prenorm(self, X, n, gi, dst, cs):
        self.prenorm_many([(X, n, cs)], gi, dst)

    def prenorm_many(self, items, gi, dst):
        self.prenorm_B(items, gi, dst, self.prenorm_A(items))

    def prenorm_A(self, items):
        return self.prenorm_A2(items, self.prenorm_A1(items))

    def prenorm_A1(self, items):
        sts = [self.st[self.rot("st", 4)] for _ in items]
        xbs = [self.xsb[self.rot("xsb", 4)] for _ in items]
        for (X, n, cs), st, xb in zip(items, sts, xbs):
            self.act(xb.t[:n, :], X.t[:n, :], AF.Square, [X, st], [xb, st], accum_out=st.t[:n, 0:1])
        for (X, n, cs), st in zip(items, sts):
            self.act(st.t[:n, 1:2], st.t[:n, 0:1], AF.Ln, [st, self.epsc], [st], scale=1.0 / D, bias=self.epsc.t[:n, 0:1])
        for (X, n, cs), st in zip(items, sts):
            self.act(st.t[:n, 1:2], st.t[:n, 1:2], AF.Exp, [st], [st], scale=-0.5)
        return (sts, xbs)

    def prenorm_A2(self, items, sx):
        sts, xbs = sx
        for (X, n, cs), st, xb in zip(items, sts, xbs):
            self.ts("dve", xb.t[:n, :], X.t[:n, :], st.t[:n, 1:2], ALU.mult, [X, st], [xb])
        return xbs

    def prenorm_B(self, items, gi, dst, xbs):
        for (X, n, cs), xb in zip(items, xbs):
            bi = 2 + self.rot("trb", 2)
            ptv = self.B[bi].t[:].bitcast(BF16).rearrange("p (c t) -> p c t", c=8)
            for c in range(8):
                self.tr(ptv[:, c, 0:n], xb.t[:n, c * 128:(c + 1) * 128], n, [xb], self.Bq[bi])
            if gi is None:
                self.evac(dst.t[:, :, cs], ptv[:, :, 0:n], self.Bq[bi], [dst])
            else:
                gb = self.gpre.t[:, gi, :].unsqueeze(2).to_broadcast([128, 8, n])
                self.tt("dve", dst.t[:, :, cs], ptv[:, :, 0:n], gb, ALU.mult, self.Bq[bi] + [self.gpre], [dst])

    def norm_transition(self, tiles, gi):
        items = [(self.X[t.i], t.n, t.cs) for t in tiles]
        pairs = [(tiles[i:i + 2], items[i:i + 2]) for i in range(0, len(tiles), 2)]
        for tp, _ in pairs:
            self.postnorm_many(tp)
        xbs = [self.prenorm_A(ip) for _, ip in pairs]
        for (_, ip), xb in zip(pairs, xbs):
            self.prenorm_B(ip, gi, self.hT, xb)

    def postnorm_many(self, tiles):
        sts = [self.st[self.rot("st", 4)] for _ in tiles]
        for t, st in zip(tiles, sts):
            Y = self.Y[t.i]
            jb = self.xsb[self.rot("xsb", 4)]
            self.act(jb.t[:t.n, :], Y.t[:t.n, :], AF.Square, [Y, st], [jb, st], accum_out=st.t[:t.n, 0:1])
        for t, st in zip(tiles, sts):
            self.act(st.t[:t.n, 1:2], st.t[:t.n, 0:1], AF.Ln, [st, self.epsc], [st], scale=1.0 / D, bias=self.epsc.t[:t.n, 0:1])
        for t, st in zip(tiles, sts):
            self.act(st.t[:t.n, 1:2], st.t[:t.n, 1:2], AF.Exp, [st], [st], scale=-0.5)
        for ix, (t, st) in enumerate(zip(tiles, sts)):
            Y = self.Y[t.i]
            eng = "pool" if ix == len(tiles) - 1 and len(tiles) > 2 else "dve"
            self.tt(eng, Y.t[:t.n, :], Y.t[:t.n, :], self.gpost.t[:t.n, :], ALU.mult, [Y, self.gpost], [Y])
        for t, st in zip(tiles, sts):
            X, Y = self.X[t.i], self.Y[t.i]
            self.stt("dve", X.t[:t.n, :], Y.t[:t.n, :], st.t[:t.n, 1:2], X.t[:t.n, :], ALU.mult, ALU.add, [Y, st, X], [X])

    def proj_fm(self, slot, scol0, m, srcT, ncols, evac_fn):
        bi = self.rot("fmb", 2)
        ps = self.B[bi].t[:m, 0:ncols]
        for k in range(8):
            self.mm(ps, slot.t[:, k, scol0:scol0 + m], srcT.t[:, k, 0:ncols], k == 0, k == 7,
                    [slot, srcT], self.Bq[bi])
        evac_fn(ps, self.Bq[bi])

    def proj_tm(self, slot, scol0, ncol, srcT, t, evac_fn, nk=8):
        ap, bufs = self.Ahalf(self.rot("tmb", 4))
        ps = ap[:t.n, 0:ncol]
        for k in range(nk):
            self.mm(ps, srcT.t[:, k, t.cs], slot.t[:, k, scol0:scol0 + ncol], k == 0, k == nk - 1,
                    [slot, srcT], bufs)
        evac_fn(ps, bufs)

    def stage_win(self, tiles, last, kaT_cur, va_cur):
        P = self.P
        sample = tiles[0].kind == "s"
        NTc = sum(t.n for t in tiles)
        hT, QT = self.hT, self.QT
        j, slot = self.wget("w_in")
        for c in range(4):
            self.proj_fm(slot, c * 128, 128, hT, NTc,
                         lambda ps, pb, c=c: self.evac(QT.t[:, c, 0:NTc], ps, pb, [QT], scale=0.125))
        self.wrel(j)
        j, slot = self.wget("w_in")
        kdst = self.kaTs if sample else kaT_cur
        for c in range(4):
            self.proj_fm(slot, c * 128, 128, hT, NTc,
                         lambda ps, pb, c=c: self.evac(kdst.t[:, c, 0:NTc], ps, pb, [kdst]))
        if last:
            for t in tiles:
                Y = self.Y[t.i]
                self.proj_tm(slot, 0, 512, hT, t,
                             lambda ps, pb, t=t, Y=Y: self.evac(Y.t[:t.n, 0:512], ps, pb, [Y]))
        self.wrel(j)
        j, slot = self.wget("w_in")
        for t in tiles:
            vdst = self.vas[t.i].t[:t.n, :] if sample else va_cur.t[:t.n, t.i, :]
            vbuf = self.vas[t.i] if sample else va_cur
            Y = self.Y[t.i]

            def ev(ps, pb, t=t, vdst=vdst, vbuf=vbuf, Y=Y):
                self.evac(vdst, ps, pb, [vbuf])
                if last:
                    self.evac(Y.t[:t.n, 512:1024], ps, pb, [Y])
            self.proj_tm(slot, 0, 512, hT, t, ev)
        self.wrel(j)
        if last:
            for t in tiles:
                Y = self.Y[t.i]
                if sample:
                    dk, dv = self.sak[t.c0:t.c0 + t.n, :], self.sav[t.c0:t.c0 + t.n, :]
                else:
                    dk, dv = self.pak[t.c0:t.c0 + t.n, :], self.pav[t.c0:t.c0 + t.n, :]
                P.dma("pool", dk, Y.t[:t.n, 0:512], [Y], [], out=True)
                P.dma("pool", dv, Y.t[:t.n, 512:1024], [Y], [], out=True)
        j, slot = self.wget("w_in")
        for c in range(4):
            sc = 0.125 if c < 2 else None
            self.proj_fm(slot, c * 128, 128, hT, NTc,
                         lambda ps, pb, c=c, sc=sc: self.evac(QT.t[:, 4 + c, 0:NTc], ps, pb, [QT], scale=sc))
        for t in tiles:
            self.proj_tm(slot, 256, 256, hT, t,
                         lambda ps, pb, t=t: self.evac(self.kb.t[:t.n, t.i, :], ps, pb, [self.kb]))
        self.wrel(j)
        j, slot = self.wget("w_in")
        for t in tiles:
            self.proj_tm(slot, 0, 512, hT, t,
                         lambda ps, pb, t=t: self.evac(self.vb.t[:t.n, t.i, :], ps, pb, [self.vb]))
        self.wrel(j)
        j, slot = self.wget("w_in")
        self.proj_fm(slot, 0, 16, hT, NTc,
                     lambda ps, pb: self.evac(self.glT.t[0:16, 0:NTc], ps, pb, [self.glT]))
        for c in range(4):
            self.proj_fm(slot, 16 + c * 128, 128, hT, NTc,
                         lambda ps, pb, c=c: self.act(self.rT.t[:, c, 0:NTc], ps, AF.Silu, pb, [self.rT]))
        self.wrel(j)
        self.softplus_tiles(tiles)

    def softplus_tiles(self, tiles, glT=None, sp=None, sp_ap=None):
        glT = glT or self.glT
        sp = sp or self.sp
        spt = sp_ap if sp_ap is not None else sp.t
        glt = glT.t if glT.t is not None else self.glT1_ap
        pss = []
        for t in tiles:
            ap, bufs = self.Ahalf(self.rot("tmb", 4))
            ps = ap[:t.n, 0:256]
            self.mm(ps, glt[0:33, t.cs], self.waext.t[0:33, :], True, True, [glT, self.waext], bufs)
            pss.append((ps, bufs))
        for t, (ps, bufs) in zip(tiles, pss):
            self.act(spt[:t.n, t.i, :], ps, AF.Exp, bufs, [sp], scale=-1.0)
        for t in tiles:
            dst = spt[:t.n, t.i, :]
            self.act(dst, dst, AF.Ln, [sp, self.onesf], [sp], bias=self.onesf.t[:t.n, 0:1])

    def gla_stages(self, t):
        C = t.n
        sp, kb, vb, QT = self.sp, self.kb, self.vb, self.QT
        B0, B1 = self.B[0].t, self.B[1].t
        q0, q1 = self.Bq[0], self.Bq[1]

        def g0():
            suf = B0[:C, 256:512]
            self.mm(suf, self.SLf.t[:C, :C], sp.t[:C, t.i, :], True, True, [self.SLf, sp], q0)
            csts = []
            for p in range(2):
                cst = B0[:, p * 128:p * 128 + C]
                self.mm(cst, sp.t[:C, t.i, p * 128:(p + 1) * 128], self.Uf.t[:C, :C], True, True, [sp, self.Uf], q0)
                csts.append(cst)
            self.act(self.kdw.t[:C, :], suf, AF.Exp, q0, [self.kdw], scale=-1.0 / 16)
            for p in range(2):
                self.act(self.Ep[p].t[:, :C], csts[p], AF.Exp, q0, [self.Ep[p]], scale=-1.0 / 16)
                self.act(self.Em[p].t[:, :C], csts[p], AF.Exp, q0, [self.Em[p]], scale=1.0 / 16)
            self.tt("dve", self.kdec.t[:C, :], kb.t[:C, t.i, :], self.kdw.t[:C, :], ALU.mult, [kb, self.kdw], [self.kdec])
            for p in range(2):
                self.tt("dve", self.qtl[p].t[:, :C], QT.t[:, 4 + p, t.cs], self.Ep[p].t[:, :C], ALU.mult,
                        [QT, self.Ep[p]], [self.qtl[p]])
                for hh in range(2):
                    r = slice(hh * 64, hh * 64 + 64)
                    self.tt("dve", self.ktl[p][hh].t[r, :C], QT.t[r, 6 + p, t.cs], self.Em[p].t[r, :C], ALU.mult,
                            [QT, self.Em[p]], [self.ktl[p][hh]])
                self.copy("dve", self.dec[p].t[:, 0:1], self.Ep[p].t[:, C - 1:C], [self.Ep[p]], [self.dec[p]])

        def g1():
            for h in range(4):
                p, r = h // 2, slice((h % 2) * 64, (h % 2) * 64 + 64)
                self.mm(B1[:C, h * 128:h * 128 + C], self.ktl[p][h % 2].t[:, :C], self.qtl[p].t[:, :C], True, True,
                        [self.ktl[p][h % 2], self.qtl[p]], q1)
            for h in range(4):
                self.tt("dve", self.ATm[h].t[:C, :C], B1[:C, h * 128:h * 128 + C], self.Ub.t[:C, :C], ALU.mult,
                        q1 + [self.Ub], [self.ATm[h]])

        def g2():
            for h in range(4):
                p, r = h // 2, slice((h % 2) * 64, (h % 2) * 64 + 64)
                oT = B0[:, h * 128:h * 128 + C]
                self.mm(oT, vb.t[:C, t.i, h * 128:(h + 1) * 128], self.ATm[h].t[:C, :C], True, False, [vb, self.ATm[h]], q0)
                self.mm(oT, self.Sbf[p][h % 2].t[:, :], self.qtl[p].t[:, :C], False, True, [self.Sbf[p][h % 2], self.qtl[p]], q0)
            for h in range(4):
                self.act(self.sq[h].t[:, :C], B0[:, h * 128:h * 128 + C], AF.Square, q0, [self.sq[h]])

        def g3():
            for h in range(4):
                self.mm(B1[:, h * 128:h * 128 + C], self.onesb.t[:, :], self.sq[h].t[:, :C], True, True,
                        [self.onesb, self.sq[h]], q1)
            for h in range(4):
                self.act(self.rs[h].t[:, :C], B1[:, h * 128:h * 128 + C], AF.Ln, q1 + [self.epsc], [self.rs[h]], scale=1.0 / 128,
                         bias=self.epsc.t[:, 0:1])
            for h in range(4):
                self.act(self.rs[h].t[:, :C], self.rs[h].t[:, :C], AF.Exp, [self.rs[h]], [self.rs[h]], scale=-0.5)
            for h in range(4):
                self.stt("dve", self.t1[h].t[:, :C], B0[:, h * 128:h * 128 + C], self.ggla.t[:, h:h + 1],
                         self.rs[h].t[:, :C], ALU.mult, ALU.mult, q0 + [self.ggla, self.rs[h]], [self.t1[h]])
            for h in range(4):
                self.tt("dve", self.mixT.t[:, 4 + h, t.cs], self.t1[h].t[:, :C], self.rT.t[:, h, t.cs], ALU.mult,
                        [self.t1[h], self.rT], [self.mixT])

        def g4():
            for p in range(2):
                ds = B0[:, p * 256:(p + 1) * 256]
                self.mm(ds, self.kdec.t[:C, p * 128:(p + 1) * 128], vb.t[:C, t.i, p * 256:(p + 1) * 256], True, True,
                        [self.kdec, vb], q0)
            for p in range(2):
                S = self.S[p]
                for hh in range(2):
                    r = slice(hh * 64, hh * 64 + 64)
                    self.stt("dve", S.t[r, :], S.t[r, :], self.dec[p].t[r, 0:1],
                             B0[r, p * 256 + hh * 128:p * 256 + (hh + 1) * 128],
                             ALU.mult, ALU.add, [S, self.dec[p]] + q0, [S])
                self.copy_Sbf(p)

        return [g0, g1, g2, g3, g4]

    def prefix_sufs(self, tiles, bset):
        kb_ap, vb_ap, sp_ap, kbB, vbB, spB = bset
        regs = []
        for t in tiles:
            ap, bufs = self.Ahalf(t.i)
            self.mm(ap[:, 0:256], self.SLf.t[:, :], sp_ap[:, t.i, :], True, True, [self.SLf, spB], bufs)
            for p in range(2):
                self.mm(ap[:, 256 + p:257 + p], sp_ap[:, t.i, p * 128:(p + 1) * 128], self.onesf.t[:, 0:1], True, True,
                        [spB, self.onesf], bufs)
            regs.append((ap, bufs))
        for t, (ap, bufs) in zip(tiles, regs):
            Y = self.Y[t.i]
            self.act(Y.t[:, 0:256], ap[:, 0:256], AF.Exp, bufs, [Y], scale=-1.0 / 16)
            self.act(self.dec8.t[:, 2 * t.i:2 * t.i + 2], ap[:, 256:258], AF.Exp, bufs, [self.dec8], scale=-1.0 / 16)
        for t in tiles:
            Y = self.Y[t.i]
            self.tt("dve", self.kdec4[t.i].t[:, :], kb_ap[:, t.i, :], Y.t[:, 0:256], ALU.mult, [kbB, Y], [self.kdec4[t.i]])

    def prefix_ds(self, tiles, bset):
        kb_ap, vb_ap, sp_ap, kbB, vbB, spB = bset
        for t in tiles:
            kd = self.kdec4[t.i]
            for p in range(2):
                bi = self.rot("pfx", 2)
                ds = self.B[bi].t[:, 0:256]
                self.mm(ds, kd.t[:, p * 128:(p + 1) * 128], vb_ap[:, t.i, p * 256:(p + 1) * 256], True, True,
                        [kd, vbB], self.Bq[bi])
                self.stt("dve", self.S2[p], self.S2[p], self.dec8.t[:, 2 * t.i + p:2 * t.i + p + 1], ds,
                         ALU.mult, ALU.add, [self.S2b[p], self.dec8] + self.Bq[bi], [self.S2b[p]])

    def attn_S(self, nq, h, qT_ap, qbufs, ksegs, bias_ap, extra_bias):
        ai = self.rot("attA", 2)
        A, Aq = self.A[ai].t, self.Aq[ai]
        nk = sum(n for _, n, _ in ksegs)
        sbufs = Aq[0:3]
        ops = []
        for c0 in range(0, nk, 512):
            c1 = min(nk, c0 + 512)
            ops.append((c0, c1, self.ident.t[:nq, :nq], bias_ap[:, c0:c1], [self.ident, self.Bm]))
        if extra_bias is not None:
            l_ap, r_ap, n_e, ebufs = extra_bias
            ops.append((0, n_e, l_ap, r_ap, ebufs))
        col = 0
        for (k_ap, n, kbufs) in ksegs:
            o = 0
            while o < n:
                e = min(n, o + (512 - (col + o) % 512))
                ops.append((col + o, col + e, qT_ap, k_ap[:, o:e], qbufs + kbufs))
                o = e
            col += n
        for bank in range(2):
            bops = [x for x in ops if x[0] // 512 == bank]
            for ix, (c0, c1, l_ap, r_ap, rb) in enumerate(bops):
                self.mm(A[:nq, c0:c1], l_ap, r_ap, ix == 0, ix == len(bops) - 1, rb, sbufs)
        bi = self.rot("attP", 2)
        Pb, rs8, ri8 = self.Pb[bi], self.rs8[bi], self.ri8[bi]
        self.act(Pb.t[:nq, 0:nk], A[:nq, 0:nk], AF.Exp, sbufs + [rs8], [Pb, rs8], accum_out=rs8.t[:nq, 0:1])
        self.P.op("dve", lambda e: e.reciprocal(ri8.t[:nq, 0:1], rs8.t[:nq, 0:1]), [rs8], [ri8])
        self.ts("dve", Pb.t[:nq, 0:nk], Pb.t[:nq, 0:nk], ri8.t[:nq, 0:1], ALU.mult, [Pb, ri8], [Pb])
        return {"ai": ai, "bi": bi, "nk": nk, "h": h, "nq": nq}

    def attn_rest(self, ctx, vblocks, out_ap, out_buf):
        self.attn_T(ctx, vblocks)
        self.attn_PV(ctx, vblocks, out_ap, out_buf)

    def attn_T(self, ctx, vblocks):
        nq, bi = ctx["nq"], ctx["bi"]
        Pn, PT = self.Pb[bi], self.PT[bi]
        ti = 2 + self.rot("trb", 2)
        ptv = self.B[ti].t[:].bitcast(BF16).rearrange("p (c t) -> p c t", c=8)
        col = 0
        for bix, (_, n, _) in enumerate(vblocks):
            self.tr(ptv[:n, bix, 0:nq], Pn.t[:nq, col:col + n], nq, [Pn], self.Bq[ti])
            col += n
        nb = len(vblocks)
        nmax = max(n for _, n, _ in vblocks)
        self.evac(PT.t[:nmax, 0:nb, 0:nq], ptv[:nmax, 0:nb, 0:nq], self.Bq[ti], [PT], eng="dve")

    def attn_PV(self, ctx, vblocks, out_ap, out_buf):
        nq, h, ai, bi = ctx["nq"], ctx["h"], ctx["ai"], ctx["bi"]
        A, Aq = self.A[ai].t, self.Aq[ai]
        PT = self.PT[bi]
        r = slice((h % 2) * 64, (h % 2) * 64 + 64)
        nb = len(vblocks)
        o_ps = A[:, 768:768 + nq]
        for bix, (v_ap, n, vbufs) in enumerate(vblocks):
            self.mm(o_ps, v_ap, PT.t[:n, bix, 0:nq], bix == 0, bix == nb - 1, vbufs + [PT], [Aq[3]])
        self.evac(out_ap, A[r, 768:768 + nq], [Aq[3]], [out_buf])

    def mix_tile(self, t, head_args):
        import os
        km = int(os.environ.get("KM", "0"))
        stages = self.gla_stages(t)
        if km == 1:
            stages = []
        args = [head_args(h) for h in range(8)]
        if km == 2:
            for h in range(8):
                ctx = self.attn_S(*args[h][0])
                self.attn_rest(ctx, args[h][1], args[h][2], self.mixT)
                if h < len(stages):
                    stages[h]()
            return
        if km >= 4:
            stages = stages[:km - 3]
        if km >= 3:
            for h in range(8):
                ctx = self.attn_S(*args[h][0])
                self.attn_rest(ctx, args[h][1], args[h][2], self.mixT)
            for st in stages:
                st()
            return
        ctx = self.attn_S(*args[0][0])
        pend = None
        for h in range(8):
            nxt = self.attn_S(*args[h + 1][0]) if h + 1 < 8 else None
            self.attn_T(ctx, args[h][1])
            if pend is not None:
                self.attn_PV(pend[0], pend[1], pend[2], self.mixT)
            pend = (ctx, args[h][1], args[h][2])
            if h < len(stages):
                stages[h]()
            ctx = nxt
        self.attn_PV(pend[0], pend[1], pend[2], self.mixT)

    def proj_to_Y(self, srcT, wkey, tiles, kgroups=(8,)):
        for cc in range(2):
            accs = [self.Ahalf(t.i) for t in tiles]
            nkg = len(kgroups)
            kbase = 0
            for gi, nk in enumerate(kgroups):
                j, slot = self.wget(wkey)
                for t, (ap, bufs) in zip(tiles, accs):
                    for k in range(nk):
                        self.mm(ap[:t.n, :], srcT.t[:, kbase + k, t.cs], slot.t[:, k, 0:512],
                                gi == 0 and k == 0, gi == nkg - 1 and k == nk - 1, [srcT, slot], bufs)
                self.wrel(j)
                kbase += nk
            for t, (ap, bufs) in zip(tiles, accs):
                Y = self.Y[t.i]
                self.evac(Y.t[:t.n, cc * 512:(cc + 1) * 512], ap[:t.n, :], bufs, [Y])
        self.warm(self.cfg.warm_y)

    def mem_S(self, t, KT):
        n = t.n
        QT = self.QT
        ai = self.rot("attA", 2)
        A, Aq = self.A[ai].t, self.Aq[ai]
        for h in range(4):
            for kk in range(2):
                self.mm(A[:n, h * 256:(h + 1) * 256], QT.t[:, 2 * h + kk, t.cs], KT.t[:, 2 * h + kk, :],
                        kk == 0, kk == 1, [QT, KT], [Aq[h]])
        bi = self.rot("attP", 2)
        Pb, rs8, ri8 = self.Pb[bi], self.rs8[bi], self.ri8[bi]
        for h in range(4):
            self.act(Pb.t[:n, h * 256:(h + 1) * 256], A[:n, h * 256:(h + 1) * 256], AF.Exp, [Aq[h], rs8], [Pb, rs8],
                     scale=1.0 / 16, accum_out=rs8.t[:n, h:h + 1])
        self.P.op("dve", lambda e: e.reciprocal(ri8.t[:n, 0:4], rs8.t[:n, 0:4]), [rs8], [ri8])
        rb = ri8.t[:n, 0:4].unsqueeze(2).to_broadcast([n, 4, 256])
        pv = Pb.t[:n, :].rearrange("p (h k) -> p h k", h=4)
        self.tt("dve", pv, pv, rb, ALU.mult, [Pb, ri8], [Pb])
        return {"bi": bi, "t": t}

    def mem_T(self, ctx):
        t, bi = ctx["t"], ctx["bi"]
        n = t.n
        Pn, PT = self.Pb[bi], self.PT[bi]
        ti = 2 + self.rot("trb", 2)
        ptv = self.B[ti].t[:].bitcast(BF16).rearrange("p (c t) -> p c t", c=8)
        for h in range(4):
            for kb in range(2):
                c0 = h * 256 + kb * 128
                self.tr(ptv[:, 2 * h + kb, 0:n], Pn.t[:n, c0:c0 + 128], n, [Pn], self.Bq[ti])
        self.evac(PT.t[:, :, 0:n], ptv[:, :, 0:n], self.Bq[ti], [PT])

    def mem_PV(self, ctx, V):
        t, bi = ctx["t"], ctx["bi"]
        n = t.n
        PT = self.PT[bi]
        for h in range(4):
            for dc in range(2):
                c = 2 * h + dc
                bk = c // 4
                out = self.B[bk].t[:, (c % 4) * 128:(c % 4) * 128 + n]
                for kb in range(2):
                    self.mm(out, V.t[:, kb, h * 256 + dc * 128:h * 256 + dc * 128 + 128],
                            PT.t[:, 2 * h + kb, 0:n], kb == 0, kb == 1, [V, PT], self.Bq[bk])
        for bk in range(2):
            src = self.B[bk].t[:, :].rearrange("p (c t) -> p c t", c=4)[:, :, 0:n]
            self.evac(self.mixT.t[:, 4 * bk:4 * bk + 4, t.cs], src, self.Bq[bk], [self.mixT])

    def mem_attn_group(self, tiles, kv_of):
        ctx = self.mem_S(tiles[0], kv_of(tiles[0])[0])
        pend = None
        for ix, t in enumerate(tiles):
            nxt = self.mem_S(tiles[ix + 1], kv_of(tiles[ix + 1])[0]) if ix + 1 < len(tiles) else None
            self.mem_T(ctx)
            if pend is not None:
                self.mem_PV(pend[0], pend[1])
            pend = (ctx, kv_of(t)[1])
            ctx = nxt
        self.mem_PV(pend[0], pend[1])

    def stage_ffn(self, tiles):
        NTc = sum(t.n for t in tiles)
        hT = self.hT
        for jp in range(6):
            nch = 4 if jp < 5 else 2
            jg, sg = self.wget("w_g")
            gates = []
            for cc in range(nch):
                a, gb = self.Ahalf(cc)
                gps = a[:, 0:NTc]
                for k in range(8):
                    self.mm(gps, sg.t[:, k, cc * 128:(cc + 1) * 128], hT.t[:, k, 0:NTc], k == 0, k == 7, [sg, hT], gb)
                gates.append((gps, gb))
            self.wrel(jg)
            ju, su = self.wget("w_u")
            for cc in range(nch):
                f = jp * 4 + cc
                gps, gb = gates[cc]
                si = self.silu[self.rot("silu", 2)]
                self.act(si.t[:, 0:NTc], gps, AF.Silu, gb, [si])
                bi = self.rot("ffu", 2)
                ups = self.B[bi].t[:, 0:NTc]
                for k in range(8):
                    self.mm(ups, su.t[:, k, cc * 128:(cc + 1) * 128], hT.t[:, k, 0:NTc], k == 0, k == 7, [su, hT], self.Bq[bi])
                self.tt("dve", self.hidT.t[:, f, 0:NTc], si.t[:, 0:NTc], ups, ALU.mult, [si] + self.Bq[bi], [self.hidT])
            self.wrel(ju)
        self.load_gpost("g_post_ffn")
        self.proj_to_Y(self.hidT, "w_d", tiles, kgroups=(8, 8, 6))

    def group(self, tiles, first, last, g):
        P, cfg = self.P, self.cfg
        sample = tiles[0].kind == "s"
        NTc = sum(t.n for t in tiles)
        cur, prev = g % 2, (g + 1) % 2
        kaT_cur, va_cur = self.kaT[cur], self.va[cur]
        kaT_prev, va_prev = self.kaT[prev], self.va[prev]
        P.cur_stage = f"g{g}:load+norm"
        for t in tiles:
            src = self.xs[t.c0:t.c0 + t.n, :] if sample else self.xp[(g * cfg.gt + t.i) * 128:(g * cfg.gt + t.i + 1) * 128, :]
            P.dma("sp", self.X[t.i].t[:t.n, :], src, [], [self.X[t.i]])
        self.warm(self.cfg.warm_g)
        items = [(self.X[t.i], t.n, t.cs) for t in tiles]
        prs = [items[i:i + 2] for i in range(0, len(items), 2)]
        xbs = [self.prenorm_A(ip) for ip in prs]
        for ip, xb in zip(prs, xbs):
            self.prenorm_B(ip, 0, self.hT, xb)
        P.cur_stage = f"g{g}:win"
        self.stage_win(tiles, last, kaT_cur, va_cur)
        P.cur_stage = f"g{g}:mix"
        import os
        kg = int(os.environ.get("KG", "9"))
        if kg < 2:
            return
        for t in tiles:
            if sample:
                b = t.i
                self.load_state(b)

                def head_args(h, t=t, b=b):
                    p, hh = h // 2, h % 2
                    r = slice(hh * 64, hh * 64 + 64)
                    ckT, cv = self.ckT[b], self.cv[b]
                    ksegs = [(ckT.t[r, p, 0:512], 512, [ckT]), (self.kaTs.t[r, p, t.cs], 32, [self.kaTs])]
                    vbl = [(cv.t[:, kb, p * 128:(p + 1) * 128], 128, [cv]) for kb in range(4)]
                    vbl.append((self.vas[b].t[0:32, p * 128:(p + 1) * 128], 32, [self.vas[b]]))
                    return ((32, h, self.QT.t[r, p, t.cs], [self.QT], ksegs, self.Bm.t[0:32, h, 0:544], None),
                            vbl, self.mixT.t[r, p, t.cs])
                self.mix_tile(t, head_args)
                self.store_state(self.sglao[b, :, :])
            else:
                def head_args(h, t=t):
                    i = t.i
                    p, hh = h // 2, h % 2
                    r = slice(hh * 64, hh * 64 + 64)
                    ksegs, vbl = [], []
                    ksegs.append((kaT_prev.t[r, p, i * 128:512], (4 - i) * 128, [kaT_prev]))
                    ksegs.append((kaT_cur.t[r, p, 0:(i + 1) * 128], (i + 1) * 128, [kaT_cur]))
                    for kb in range(5):
                        if kb < 4 - i:
                            vbl.append((va_prev.t[:, i + kb, p * 128:(p + 1) * 128], 128, [va_prev]))
                        else:
                            vbl.append((va_cur.t[:, kb - (4 - i), p * 128:(p + 1) * 128], 128, [va_cur]))
                    eb = None
                    if first:
                        ne = (4 - i) * 128
                        eb = (self.onesb.t[0:1, 0:128], self.hm.t[0:1, i * 128:512], ne, [self.onesb, self.hm])
                    return ((128, h, self.QT.t[r, p, t.cs], [self.QT], ksegs, self.Bm.t[:, h, :], eb),
                            vbl, self.mixT.t[r, p, t.cs])
                self.mix_tile(t, head_args)
        if last and not sample:
            self.store_state(self.pgla[:, :])
        if kg < 3:
            return
        P.cur_stage = f"g{g}:wo"
        self.load_gpost("g_post_mix")
        self.proj_to_Y(self.mixT, "w_o", tiles)
        self.norm_transition(tiles, 1)
        P.cur_stage = f"g{g}:mem"
        for cc in range(2):
            j, slot = self.wget("w_mq")
            for c in range(4):
                self.proj_fm(slot, c * 128, 128, self.hT, NTc,
                             lambda ps, pb, c=c, cc=cc: self.evac(self.QT.t[:, cc * 4 + c, 0:NTc], ps, pb, [self.QT]))
            self.wrel(j)
        if sample:
            self.mem_attn_group(tiles, lambda t: (self.memKTs[t.i], self.memVs[t.i]))
        else:
            self.mem_attn_group(tiles, lambda t: (self.memKT, self.memV))
        self.load_gpost("g_post_mem")
        self.proj_to_Y(self.mixT, "w_mo", tiles)
        self.norm_transition(tiles, 2)
        P.cur_stage = f"g{g}:ffn"
        self.stage_ffn(tiles)
        P.cur_stage = f"g{g}:end"
        for i0 in range(0, len(tiles), 2):
            tp = tiles[i0:i0 + 2]
            self.postnorm_many(tp)
            for t in tp:
                if sample:
                    dst = self.ys[t.c0:t.c0 + t.n, :]
                else:
                    dst = self.yp[(g * cfg.gt + t.i) * 128:(g * cfg.gt + t.i + 1) * 128, :]
                P.dma("pool", dst, self.X[t.i].t[:t.n, :], [self.X[t.i]], [], out=True)

    def store_state(self, dst):
        for p in range(2):
            self.P.dma("pool", dst[p * 128:(p + 1) * 128, :], self.S[p].t[:, :], [self.S[p]], [], out=True)

    def load_state(self, b):
        for p in range(2):
            self.P.dma("sp", self.S[p].t[:, :], self.sgla[b, p * 128:(p + 1) * 128, :], [], [self.S[p]])
            self.copy_Sbf(p)

    def copy_Sbf(self, p):
        for hh in range(2):
            r = slice(hh * 64, hh * 64 + 64)
            self.copy("act", self.Sbf[p][hh].t[r, :], self.S[p].t[r, :], [self.S[p]], [self.Sbf[p][hh]])

    def mem_kv_prompt(self):
        import os
        P = self.P
        P.cur_stage = "memkv"
        sub = int(os.environ.get("KSUB", "9"))
        tiles = [Tile(i, 128, i * 128, "m") for i in range(2)]
        for t in tiles:
            P.dma("sp", self.X[t.i].t[:, :], self.memp[t.c0:t.c0 + 128, :], [], [self.X[t.i]])
        self.prenorm_many([(self.X[t.i], 128, t.cs) for t in tiles], 3, self.hT)
        if sub < 2:
            return
        for which in range(2):
            for cc in range(2):
                j, slot = self.wget("w_mk" if which == 0 else "w_mv")
                kvar = int(os.environ.get("KVAR", "9"))
                if kvar == 1:
                    self.copy("dve", self.Y[0].t[:, 0:512], slot.t[:, 0, 0:512], [slot], [self.Y[0]])
                    self.wrel(j)
                    continue
                for t in tiles:
                    if kvar == 2 and (which, cc, t.i) != (0, 0, 0):
                        continue
                    if kvar == 3 and (which, cc) != (0, 0):
                        continue
                    if kvar == 4 and which != 0:
                        continue
                    Y = self.Y[2 * which + t.i]

                    def ev(ps, pb, t=t, Y=Y, cc=cc, which=which):
                        e1 = self.ev_eng()
                        self.evac(Y.t[:, cc * 512:(cc + 1) * 512], ps, pb, [Y], eng=e1)
                        if which == 1 and kvar != 5:
                            self.evac(self.memV.t[:, t.i, cc * 512:(cc + 1) * 512], ps, pb, [self.memV],
                                      eng=(e1 if kvar == 7 else None))
                    self.proj_tm(slot, 0, 512, self.hT, t, ev)
                if which == 0 and sub >= 3:
                    for c in range(4):
                        self.proj_fm(slot, c * 128, 128, self.hT, 256,
                                     lambda ps, pb, c=c, cc=cc: self.evac(self.memKT.t[:, cc * 4 + c, :], ps, pb,
                                                                          [self.memKT]))
                self.wrel(j)
        if sub < 5:
            return
        for t in tiles:
            P.dma("pool", self.pmk[t.c0:t.c0 + 128, :], self.Y[t.i].t[:, :], [self.Y[t.i]], [], out=True)
            P.dma("pool", self.pmv[t.c0:t.c0 + 128, :], self.Y[2 + t.i].t[:, :], [self.Y[2 + t.i]], [], out=True)

    def halo_pass(self):
        P = self.P
        P.cur_stage = "halo"
        tiles = [Tile(i, 128, i * 128, "h") for i in range(4)]
        for t in tiles:
            P.dma("sp", self.X[t.i].t[:, :], self.xh[t.c0:t.c0 + 128, :], [], [self.X[t.i]])
        self.prenorm_many([(self.X[t.i], 128, t.cs) for t in tiles], 0, self.hT)
        kdst, vdst = self.kaT[1], self.va[1]
        j, slot = self.wget("w_in")
        for c in range(4):
            self.proj_fm(slot, c * 128, 128, self.hT, 512,
                         lambda ps, pb, c=c: self.evac(kdst.t[:, c, 0:512], ps, pb, [kdst]))
        self.wrel(j)
        j, slot = self.wget("w_in")
        for t in tiles:
            self.proj_tm(slot, 0, 512, self.hT, t,
                         lambda ps, pb, t=t: self.evac(vdst.t[:, t.i, :], ps, pb, [vdst]))
        self.wrel(j)

    def prefix_pass(self):
        P, cfg = self.P, self.cfg
        P.cur_stage = "prefix"
        if cfg.pref_t == 0:
            return
        ja, sa = self.wget("w_in")
        jb, sb_ = self.wget("w_in")
        for kc in range(8):
            self.ts("pool", sa.t[:, kc, 0:272], sa.t[:, kc, 0:272], self.gpre.t[:, 0, kc:kc + 1], ALU.mult,
                    [sa, self.gpre], [sa])
            self.ts("dve", sb_.t[:, kc, 0:512], sb_.t[:, kc, 0:512], self.gpre.t[:, 0, kc:kc + 1], ALU.mult,
                    [sb_, self.gpre], [sb_])
        S2 = [self.Y[p].t[:, 512:768] for p in range(2)]
        S2b = [Buf(f"S2_{p}") for p in range(2)]
        for p in range(2):
            self.memset("pool", S2[p], 0.0, [S2b[p], self.Y[p]])
        self.S2, self.S2b = S2, S2b
        G = cfg.pref_t // cfg.gt
        tiles = [Tile(i, 128, i * 128, "x") for i in range(cfg.gt)]
        items = [(self.X[t.i], 128, t.cs) for t in tiles]
        h = self.hidT.t
        kb1 = h[:, 0:2, :].rearrange("p a b -> p (a b)").rearrange("p (t f) -> p t f", f=256)
        vb1 = h[:, 2:6, :]
        sp1 = h[:, 6:10, :].bitcast(F32)
        glT1 = h[0:33, 10, :]
        B1 = [Buf("kb1"), Buf("vb1"), Buf("sp1"), Buf("glT1")]
        self.memset("pool", h[0:32, 10, :], 0.0, [B1[3]])
        self.memset("pool", h[32:33, 10, :], 1.0, [B1[3]])
        sets = [
            (self.kb.t, self.vb.t, self.sp.t, self.kb, self.vb, self.sp, self.glT.t, self.glT),
            (kb1, vb1, sp1, B1[0], B1[1], B1[2], glT1, B1[3]),
        ]

        def load(g):
            for t in tiles:
                r0 = (g * cfg.gt + t.i) * 128
                P.dma("sp", self.X[t.i].t[:, :], self.xpre[r0:r0 + 128, :], [], [self.X[t.i]])

        def proj(g):
            kb_ap, vb_ap, sp_ap, kbB, vbB, spB, glt, glB = sets[g % 2]
            self.proj_fm(sa, 256, 16, self.hT, self.NT,
                         lambda ps, pb: self.evac(glt[0:16, 0:self.NT], ps, pb, [glB]))
            for t in tiles:
                self.proj_tm(sa, 0, 256, self.hT, t,
                             lambda ps, pb, t=t: self.evac(kb_ap[:, t.i, :], ps, pb, [kbB]))
                self.proj_tm(sb_, 0, 512, self.hT, t,
                             lambda ps, pb, t=t: self.evac(vb_ap[:, t.i, :], ps, pb, [vbB]))
            pss = []
            for t in tiles:
                ap, bufs = self.Ahalf(self.rot("tmb", 4))
                ps = ap[:, 0:256]
                self.mm(ps, glt[0:33, t.cs], self.waext.t[0:33, :], True, True, [glB, self.waext], bufs)
                pss.append((ps, bufs))
            for t, (ps, bufs) in zip(tiles, pss):
                self.act(sp_ap[:, t.i, :], ps, AF.Exp, bufs, [spB], scale=-1.0)
            for t in tiles:
                self.act(sp_ap[:, t.i, :], sp_ap[:, t.i, :], AF.Ln, [spB, self.onesf], [spB], bias=self.onesf.t[:, 0:1])

        def bset(g):
            kb_ap, vb_ap, sp_ap, kbB, vbB, spB, glt, glB = sets[g % 2]
            return (kb_ap, vb_ap, sp_ap, kbB, vbB, spB)

        load(0)
        xbs = self.prenorm_A(items)
        self.prenorm_B(items, None, self.hT, xbs)
        for g in range(G):
            if g + 1 < G:
                load(g + 1)
                sx = self.prenorm_A1(items)
            proj(g)
            if g + 1 < G:
                xbs = self.prenorm_A2(items, sx)
            self.convert_more(3, after=[sets[g % 2][3]])
            if g >= 1:
                self.prefix_sufs(tiles, bset(g - 1))
            if g + 1 < G:
                self.prenorm_B(items, None, self.hT, xbs)
            if g >= 1:
                self.prefix_ds(tiles, bset(g - 1))
        self.prefix_sufs(tiles, bset(G - 1))
        self.prefix_ds(tiles, bset(G - 1))
        self.convert_more(len(self.conv_todo))
        for p in range(2):
            for hh in range(2):
                r = slice(hh * 64, hh * 64 + 64)
                self.copy("dve", self.S[p].t[r, :], self.S2[p][r, hh * 128:(hh + 1) * 128], [self.S2b[p]], [self.S[p]])
            self.copy_Sbf(p)
        self.wrel(ja)
        self.wrel(jb)

    def sample_setup(self):
        P = self.P
        P.cur_stage = "ssetup"
        ctok_b, cmtok_b = self.Y[0], self.Y[1]
        ctok = ctok_b.t[:, :].bitcast(BF16).rearrange("p (k f) -> p k f", k=4)
        cmtok = cmtok_b.t[:, :].bitcast(BF16).rearrange("p (k f) -> p k f", k=2)
        for b in range(2):
            ckT, cv = self.ckT[b], self.cv[b]
            P.dma("pool", ctok[:, :, :], self.cak[b].rearrange("(k p) f -> p k f", p=128), [], [ctok_b])
            P.dma("pool", cv.t[:, :, :], self.cav[b].rearrange("(k p) f -> p k f", p=128), [], [cv])
            for half in range(2):
                ti = 2 + self.rot("trb", 2)
                ptv = self.B[ti].t[:].bitcast(BF16).rearrange("p (c t) -> p c t", c=8)
                for pl in range(2):
                    p = 2 * half + pl
                    for kb in range(4):
                        self.tr(ptv[:, pl * 4 + kb, :], ctok[:, kb, p * 128:(p + 1) * 128], 128,
                                [ctok_b], self.Bq[ti])
                dst = ckT.t[:, 2 * half:2 * half + 2, :].rearrange("p a (k t) -> p (a k) t", t=128)
                self.evac(dst, ptv[:, :, :], self.Bq[ti], [ckT])
            P.dma("pool", cmtok[:, :, :], self.cmk[b].rearrange("(k p) f -> p k f", p=128), [], [cmtok_b])
            P.dma("pool", self.memVs[b].t[:, :, :], self.cmv[b].rearrange("(k p) f -> p k f", p=128), [],
                  [self.memVs[b]])
            for half in range(2):
                ti = 2 + self.rot("trb", 2)
                ptv = self.B[ti].t[:].bitcast(BF16).rearrange("p (c t) -> p c t", c=8)
                for cl in range(4):
                    c = 4 * half + cl
                    for kb in range(2):
                        self.tr(ptv[:, cl * 2 + kb, :], cmtok[:, kb, c * 128:(c + 1) * 128], 128,
                                [cmtok_b], self.Bq[ti])
                dst = self.memKTs[b].t[:, 4 * half:4 * half + 4, :].rearrange("p c (k t) -> p (c k) t", t=128)
                self.evac(dst, ptv[:, :, :], self.Bq[ti], [self.memKTs[b]])


def _bias_toeplitz(table):
    i = np.arange(128)[:, None]
    j = np.arange(640)[None, :]
    idx = np.clip(512 + i - j, -256, 256) + 256
    bm = table[:, idx]
    d = (j // 64) - (i // 64)
    valid = (d >= 0) & (d <= 8)
    return np.where(valid[None], bm, np.float32(NEG)).astype(np.float32)


_CACHE = {}


def _get_nc(cfg_key):
    if cfg_key not in _CACHE:
        _CACHE[cfg_key] = Builder(Cfg(*cfg_key)).build()
    return _CACHE[cfg_key]


def kernel(x_prompt, x_sample, mem_prompt, cache_a_k, cache_a_v, state_gla, cache_mem_k, cache_mem_v,
           g_pre_mix, w_in, rel_bias, w_alpha2, b_alpha, g_gla_out, w_o, g_post_mix,
           g_pre_mem, g_mem, w_mq, w_mk, w_mv, w_mo, g_post_mem,
           g_pre_ffn, w_ffn_gate, w_ffn_up, w_ffn_down, g_post_ffn, _ring=3, _warm_y=0, _warm_g=0):
    f = lambda a: np.ascontiguousarray(np.asarray(a, dtype=np.float32))
    x_prompt, x_sample, mem_prompt = f(x_prompt), f(x_sample), f(mem_prompt)
    nb, seq, _ = x_prompt.shape
    ncore = 8
    qn = ncore // nb
    tpc = seq // qn
    npt = tpc // 128
    pref_t = (qn - 1) * npt
    cfg_key = (npt, pref_t, 4, _ring, _warm_y, _warm_g)
    nc = _get_nc(cfg_key)
    bm = _bias_toeplitz(f(rel_bias)[0])
    shared = {
        "bm": bm,
        "g_pre_mix": f(g_pre_mix), "g_post_mix": f(g_post_mix), "g_pre_mem": f(g_pre_mem), "g_mem": f(g_mem),
        "g_post_mem": f(g_post_mem), "g_pre_ffn": f(g_pre_ffn), "g_post_ffn": f(g_post_ffn),
        "g_gla_out": f(g_gla_out), "w_alpha2": f(w_alpha2)[0], "b_alpha": f(b_alpha),
        "w_in": f(w_in)[0], "w_o": f(w_o)[0], "w_mq": f(w_mq)[0], "w_mk": f(w_mk)[0], "w_mv": f(w_mv)[0],
        "w_mo": f(w_mo)[0], "w_ffn_gate": f(w_ffn_gate)[0], "w_ffn_up": f(w_ffn_up)[0],
        "w_ffn_down": f(w_ffn_down)[0],
    }
    cak, cav = f(cache_a_k)[0].reshape(16, 512, 512), f(cache_a_v)[0].reshape(16, 512, 512)
    sg = f(state_gla)[0].reshape(16, 256, 128)
    cmk, cmv = f(cache_mem_k)[0].reshape(16, 256, 1024), f(cache_mem_v)[0].reshape(16, 256, 1024)
    in_maps = []
    for c in range(ncore):
        s, q = c // qn, c % qn
        t0 = q * tpc
        xpre = np.zeros((max(pref_t, 1) * 128, D), np.float32)
        if q > 0:
            xpre[pref_t * 128 - t0:] = x_prompt[s, 0:t0]
        xh = np.zeros((512, D), np.float32)
        if q > 0:
            xh[:] = x_prompt[s, t0 - 512:t0]
        hmask = np.full((1, 512), NEG if q == 0 else 0.0, np.float32)
        m = dict(shared)
        m.update({
            "xp": x_prompt[s, t0:t0 + tpc], "xh": xh, "xpre": xpre,
            "hmask": hmask,
            "xs": x_sample[2 * c:2 * c + 2].reshape(64, D), "memp": mem_prompt[s],
            "cak": cak[2 * c:2 * c + 2], "cav": cav[2 * c:2 * c + 2], "sgla": sg[2 * c:2 * c + 2],
            "cmk": cmk[2 * c:2 * c + 2], "cmv": cmv[2 * c:2 * c + 2],
        })
        in_maps.append({k: np.ascontiguousarray(v) for k, v in m.items()})
    import os as _os
    if _os.environ.get("KONE"):
        res = run_bass_kernel_spmd(nc, in_maps[:1], core_ids=[0])
        R = [res.results[0]] * ncore
    else:
        res = run_bass_kernel_spmd(nc, in_maps, core_ids=list(range(ncore)))
        R = res.results
    yp = np.stack([np.concatenate([R[s * qn + q]["yp"] for q in range(qn)], 0) for s in range(nb)])
    ys = np.concatenate([R[c]["ys"] for c in range(ncore)], 0).reshape(16, 32, D)
    last = [s * qn + qn - 1 for s in range(nb)]
    pak = np.stack([R[c]["pak"] for c in last]).reshape(1, nb, 512, 8, 64)
    pav = np.stack([R[c]["pav"] for c in last]).reshape(1, nb, 512, 8, 64)
    pgla = np.stack([R[c]["pgla"] for c in last]).reshape(1, nb, 4, 64, 128)
    firstc = [s * qn for s in range(nb)]
    pmk = np.stack([R[c]["pmk"] for c in firstc]).reshape(1, nb, 256, 4, 256)
    pmv = np.stack([R[c]["pmv"] for c in firstc]).reshape(1, nb, 256, 4, 256)
    sak = np.concatenate([R[c]["sak"] for c in range(ncore)], 0).reshape(1, 16, 32, 8, 64)
    sav = np.concatenate([R[c]["sav"] for c in range(ncore)], 0).reshape(1, 16, 32, 8, 64)
    sgl = np.concatenate([R[c]["sglao"] for c in range(ncore)], 0).reshape(1, 16, 4, 64, 128)
    outs = (yp, ys, pak, pav, pgla, pmk, pmv, sak, sav, sgl)
    return tuple(np.ascontiguousarray(o.astype(np.float32)) for o in outs)
```

```python
import numpy as np
import ml_dtypes
from contextlib import ExitStack
import concourse.bass as bass
import concourse.mybir as mybir
from concourse.bass_utils import run_bass_kernel_spmd

F32 = mybir.dt.float32
BF16 = mybir.dt.bfloat16
AF = mybir.ActivationFunctionType
ALU = mybir.AluOpType
AX = mybir.AxisListType

COMPUTE = ("pe", "act", "dve", "pool")
QUEUES = ("sp", "act", "pool")


class Buf:
    __slots__ = ("name", "t", "w", "r", "excl")

    def __init__(self, name, t=None, excl=False):
        self.name = name
        self.t = t
        self.w = None
        self.r = {}
        self.excl = excl

    def sub(self, n):
        return [Buf(f"{self.name}.{i}", self.t, self.excl) for i in range(n)]


class Prog:
    ENGS = ("pe", "act", "dve", "pool", "sp")

    def __init__(self, nc, es, dma_pool=10):
        self.nc = nc
        self.es = es
        self.ops = {e: [] for e in self.ENGS}
        self.seen_e = {e: {} for e in self.ENGS}
        self.seen_d = {e: {} for e in self.ENGS}
        self.dpool = {q: [(q, i) for i in range(dma_pool)] for q in ("sp", "pool", "act")}
        self.dnext = {q: 0 for q in self.dpool}
        self.dcount = {}
        self.out_tokens = {}
        self.dram = {}
        self.n_sb = 0

    def sb(self, name, shape, dtype):
        t = self.es.enter_context(self.nc.sbuf_tensor(name, list(shape), dtype))
        return Buf(name, t)

    def ps(self, name, shape, dtype):
        t = self.es.enter_context(self.nc.psum_tensor(name, list(shape), dtype))
        return Buf(name, t, excl=True)

    def dbuf(self, name):
        if name not in self.dram:
            self.dram[name] = Buf("dram:" + name)
        return self.dram[name]

    def _deps(self, eng, reads, writes, is_dma=False):
        raw, other = [], []
        for b in reads:
            if b.w is not None:
                raw.append(b.w)
            if b.excl:
                other.extend(b.r.values())
        for b in writes:
            if b.w is not None:
                other.append(b.w)
            other.extend(b.r.values())
        waits = []
        for tok, is_raw in [(t, True) for t in raw] + [(t, False) for t in other]:
            if tok[0] == "e":
                _, pe, idx = tok
                if pe == eng and not is_raw and not is_dma:
                    continue
                if pe == eng and eng == "pe":
                    continue
                if self.seen_e[eng].get(pe, -1) >= idx:
                    continue
                self.seen_e[eng][pe] = idx
                self.ops[pe][idx]["signal"] = True
                waits.append(tok)
            else:
                _, key, val = tok
                if self.seen_d[eng].get(key, 0) >= val:
                    continue
                self.seen_d[eng][key] = val
                waits.append(tok)
        return waits

    def _mark(self, tok, reads, writes):
        for b in reads:
            k = (tok[0], tok[1])
            b.r[k] = tok
        for b in writes:
            b.w = tok
            b.r = {}

    def op(self, eng, fn, reads=(), writes=()):
        waits = self._deps(eng, reads, writes)
        idx = len(self.ops[eng])
        self.ops[eng].append({"fn": fn, "waits": waits, "signal": False, "dma": None,
                              "stage": getattr(self, "cur_stage", "")})
        tok = ("e", eng, idx)
        self._mark(tok, [b for b in reads if b not in writes], writes)
        return tok

    def dma(self, q, out_ap, in_ap, reads=(), writes=(), out=False, dram_r=(), dram_w=(), **kw):
        reads = list(reads) + [self.dbuf(n) for n in dram_r]
        writes = list(writes) + [self.dbuf(n) for n in dram_w]
        pool = self.dpool[q]
        key = pool[self.dnext[q] % len(pool)]
        self.dnext[q] += 1
        k = self.dcount.get(key, 0)
        waits = self._deps(q, reads, writes, is_dma=True)
        if k > 0 and self.seen_d[q].get(key, 0) < 16 * k:
            self.seen_d[q][key] = 16 * k
            waits.append(("d", key, 16 * k))
        self.dcount[key] = k + 1
        tok = ("d", key, 16 * (k + 1))
        fn = lambda e, o=out_ap, i=in_ap, kw=kw: e.dma_start(out=o, in_=i, **kw)
        self.ops[q].append({"fn": fn, "waits": waits, "signal": False, "dma": key})
        self._mark(tok, reads, writes)
        if out:
            self.out_tokens[key] = tok
        return tok

    def finish(self):
        nc = self.nc
        fin = []
        for key, tok in self.out_tokens.items():
            if self.seen_d["sp"].get(key, 0) < tok[2]:
                fin.append(tok)
        es = self.es
        esem = {e: es.enter_context(nc.semaphore("tl_" + e)) for e in COMPUTE}
        dsem = {}
        for key in self.dcount:
            dsem[key] = es.enter_context(nc.semaphore(f"d_{key[0]}_{key[1]}"))
        cum = {}
        for e in COMPUTE:
            c = 0
            arr = []
            for o in self.ops[e]:
                if o["signal"]:
                    c += 1
                arr.append(c)
            cum[e] = arr
            assert c < 60000, (e, c)
        self.stats = {e: (len(self.ops[e]), cum[e][-1] if e in cum and cum[e] else 0) for e in self.ENGS}

        def emit(eng_name, eng):
            for o in self.ops[eng_name]:
                for tok in o["waits"]:
                    if tok[0] == "e":
                        eng.wait_ge(esem[tok[1]], cum[tok[1]][tok[2]])
                    else:
                        eng.wait_ge(dsem[tok[1]], tok[2])
                inst = o["fn"](eng)
                if o["dma"] is not None:
                    inst.then_inc(dsem[o["dma"]], 16)
                elif o["signal"]:
                    inst.then_inc(esem[eng_name], 1)
            if eng_name == "sp":
                for tok in fin:
                    eng.wait_ge(dsem[tok[1]], tok[2])

        with nc.Block() as block:
            @block.sync
            def _(e):
                emit("sp", e)

            @block.tensor
            def _(e):
                emit("pe", e)

            @block.scalar
            def _(e):
                emit("act", e)

            @block.vector
            def _(e):
                emit("dve", e)

            @block.gpsimd
            def _(e):
                emit("pool", e)


D = 1024
IN_W = 3088
DFF = 2816
NFF = DFF // 128
C_QA, C_KA, C_VA, C_QB, C_KB, C_VB, C_GL, C_R = 0, 512, 1024, 1536, 1792, 2048, 2560, 2576
EPS = 1e-6
NEG = -1e30
SLOTW = 528


class Cfg:
    def __init__(self, npt=32, pref_t=96, gt=4, ring=3, warm_y=0, warm_g=0):
        self.warm_y, self.warm_g = warm_y, warm_g
        self.npt = npt
        self.pref_t = pref_t
        self.gt = gt
        self.ring = ring
        assert npt % gt == 0 and pref_t % gt == 0


class Tile:
    def __init__(self, i, n, c0, kind):
        self.i, self.n, self.c0, self.kind = i, n, c0, kind

    @property
    def cs(self):
        return slice(self.c0, self.c0 + self.n)


class Builder:
    def __init__(self, cfg):
        self.cfg = cfg
        self.nc = bass.Bass("TRN2", target_bir_lowering=False)

    def dram_in(self, name, shape, dt=F32):
        return self.nc.dram_tensor(name, list(shape), dt, kind="ExternalInput").ap()

    def dram_out(self, name, shape, dt=F32):
        return self.nc.dram_tensor(name, list(shape), dt, kind="ExternalOutput").ap()

    def dram_tmp(self, name, shape, dt):
        return self.nc.dram_tensor(name, list(shape), dt).ap()

    def build(self):
        cfg = self.cfg
        with ExitStack() as es:
            self.P = Prog(self.nc, es)
            self.declare_io()
            self.alloc()
            import os
            stop = int(os.environ.get("KSTOP", "99"))
            self.setup_consts()
            if stop >= 2:
                self.convert_weights()
            self.make_schedule()
            if stop >= 3:
                self.mem_kv_prompt()
            if stop >= 4:
                self.halo_pass()
            if stop >= 5:
                self.prefix_pass()
            ng = cfg.npt // cfg.gt
            if stop >= 6:
                for g in range(ng):
                    tiles = [Tile(i, 128, i * 128, "p") for i in range(cfg.gt)]
                    self.group(tiles, first=(g == 0), last=(g == ng - 1), g=g)
            if stop >= 7:
                self.sample_setup()
                tiles = [Tile(i, 32, i * 32, "s") for i in range(2)]
                self.group(tiles, first=False, last=True, g=ng)
            self.P.finish()
        return self.nc

    def declare_io(self):
        c = self.cfg
        di, do = self.dram_in, self.dram_out
        self.xp = di("xp", [c.npt * 128, D])
        self.xh = di("xh", [512, D])
        self.xpre = di("xpre", [max(c.pref_t, 1) * 128, D])
        self.hmask = di("hmask", [1, 512])
        self.xs = di("xs", [64, D])
        self.memp = di("memp", [256, D])
        self.cak = di("cak", [2, 512, 512])
        self.cav = di("cav", [2, 512, 512])
        self.sgla = di("sgla", [2, 256, 128])
        self.cmk = di("cmk", [2, 256, D])
        self.cmv = di("cmv", [2, 256, D])
        self.bm_in = di("bm", [8, 128, 640])
        self.g = {n: di(n, [1, D]) for n in ("g_pre_mix", "g_post_mix", "g_pre_mem", "g_mem", "g_post_mem",
                                            "g_pre_ffn", "g_post_ffn")}
        self.g_gla = di("g_gla_out", [1, 512])
        self.w_alpha2 = di("w_alpha2", [16, 256])
        self.b_alpha = di("b_alpha", [1, 256])
        self.w = {"w_in": di("w_in", [D, IN_W]), "w_o": di("w_o", [D, D]), "w_mq": di("w_mq", [D, D]),
                  "w_mk": di("w_mk", [D, D]), "w_mv": di("w_mv", [D, D]), "w_mo": di("w_mo", [D, D]),
                  "w_g": di("w_ffn_gate", [D, DFF]), "w_u": di("w_ffn_up", [D, DFF]),
                  "w_d": di("w_ffn_down", [DFF, D])}
        self.wb = {k: self.dram_out(k + "_bf", list(v.shape), BF16) for k, v in self.w.items()}
        self.yp = do("yp", [c.npt * 128, D])
        self.ys = do("ys", [64, D])
        self.pak = do("pak", [512, 512])
        self.pav = do("pav", [512, 512])
        self.pgla = do("pgla", [256, 128])
        self.pmk = do("pmk", [256, D])
        self.pmv = do("pmv", [256, D])
        self.sak = do("sak", [64, 512])
        self.sav = do("sav", [64, 512])
        self.sglao = do("sglao", [2, 256, 128])

    def alloc(self):
        P, c = self.P, self.cfg
        sb, ps = P.sb, P.ps
        gt = c.gt
        NT = gt * 128
        self.NT = NT
        self.X = [sb(f"X{i}", [128, D], F32) for i in range(gt)]
        self.Y = [sb(f"Y{i}", [128, D], F32) for i in range(gt)]
        self.hT = sb("hT", [128, 8, NT], BF16)
        self.QT = sb("QT", [128, 8, NT], BF16)
        self.kaT = [sb(f"kaT{i}", [128, 4, NT], BF16) for i in range(2)]
        self.va = [sb(f"va{i}", [128, gt, 512], BF16) for i in range(2)]
        self.kb = sb("kb", [128, gt, 256], BF16)
        self.vb = sb("vb", [128, gt, 512], BF16)
        self.sp = sb("sp", [128, gt, 256], F32)
        self.glT = sb("glT", [33, NT], BF16)
        self.rT = sb("rT", [128, 4, NT], BF16)
        self.mixT = sb("mixT", [128, 8, NT], BF16)
        self.hidT = sb("hidT", [128, NFF, NT], BF16)
        self.Bm = sb("Bm", [128, 8, 640], BF16)
        self.gpost = sb("gpost", [128, D], F32)
        self.gpre = sb("gpre", [128, 4, 8], F32)
        self.ggla = sb("ggla", [128, 4], F32)
        self.waext = sb("waext", [33, 256], BF16)
        self.hm = sb("hm", [1, 512], BF16)
        self.memKT = sb("memKT", [128, 8, 256], BF16)
        self.memV = sb("memV", [128, 2, D], BF16)
        self.ident = sb("ident", [128, 128], BF16)
        self.Uf = sb("Uf", [128, 128], F32)
        self.SLf = sb("SLf", [128, 128], F32)
        self.Ub = sb("Ub", [128, 128], BF16)
        self.onesb = sb("onesb", [128, 128], BF16)
        self.onesf = sb("onesf", [128, 1], F32)
        self.S = [sb(f"S{p}", [128, 128], F32) for p in range(2)]
        self.Sbf = [[sb(f"Sbf{p}{hh}", [128, 128], BF16) for hh in range(2)] for p in range(2)]
        self.xsb = [sb(f"xsb{i}", [128, D], BF16) for i in range(4)]
        self.st = [sb(f"st{i}", [128, 8], F32) for i in range(4)]
        self.Pb = [sb(f"Pb{i}", [128, 1024], BF16) for i in range(2)]
        self.Pn = self.Pb
        self.PT = [sb(f"PT{i}", [128, 8, 128], BF16) for i in range(2)]
        self.rs8 = [sb(f"rs8{i}", [128, 8], F32) for i in range(2)]
        self.ri8 = [sb(f"ri8{i}", [128, 8], F32) for i in range(2)]
        self.Ep = [sb(f"Ep{p}", [128, 128], F32) for p in range(2)]
        self.Em = [sb(f"Em{p}", [128, 128], F32) for p in range(2)]
        self.qtl = [sb(f"qtl{p}", [128, 128], BF16) for p in range(2)]
        self.ktl = [[sb(f"ktl{p}{hh}", [128, 128], BF16) for hh in range(2)] for p in range(2)]
        self.kdw = sb("kdw", [128, 256], F32)
        self.kdec = sb("kdec", [128, 256], BF16)
        self.kdec4 = [self.kdec] + [sb(f"kdec{i}", [128, 256], BF16) for i in range(1, 4)]
        self.dec8 = sb("dec8", [128, 8], F32)
        self.dec = [sb(f"dec{p}", [128, 1], F32) for p in range(2)]
        self.ATm = [sb(f"ATm{i}", [128, 128], BF16) for i in range(4)]
        self.sq = [sb(f"sq{i}", [128, 128], BF16) for i in range(4)]
        self.rs = [sb(f"rs{i}", [128, 128], F32) for i in range(4)]
        self.t1 = [sb(f"t1{i}", [128, 128], F32) for i in range(4)]
        self.silu = [sb(f"silu{i}", [128, NT], BF16) for i in range(2)]
        self.ring = [sb(f"ring{i}", [128, 8, SLOTW], BF16) for i in range(c.ring)]
        self.ckT = self.kaT
        self.cv = self.va
        self.kaTs = sb("kaTs", [128, 4, 64], BF16)
        self.vas = [sb(f"vas{b}", [32, 512], BF16) for b in range(2)]
        self.memKTs = [self.memKT, sb("memKTs1", [128, 8, 256], BF16)]
        self.memVs = [self.memV, sb("memVs1", [128, 2, D], BF16)]
        self.A = [ps(f"psA{i}", [128, 1024], F32) for i in range(2)]
        self.B = [ps(f"psB{i}", [128, 512], F32) for i in range(4)]
        self.rr = {}
        self.init_psum_regions()

    def rot(self, key, n):
        v = self.rr.get(key, 0)
        self.rr[key] = v + 1
        return v % n

    def mm(self, out, lhsT, rhs, start, stop, reads, writes):
        self.P.op("pe", lambda e: e.matmul(out, lhsT, rhs, start=start, stop=stop), reads, writes)

    def tr(self, out, in_, n, reads, writes):
        idn = self.ident.t[:n, :n]
        self.P.op("pe", lambda e: e.transpose(out, in_, idn), list(reads) + [self.ident], writes)

    def act(self, out, in_, func, reads, writes, **kw):
        self.P.op("act", lambda e: e.activation(out, in_, func, **kw), reads, writes)

    def copy(self, eng, out, in_, reads, writes):
        if eng == "act":
            self.P.op("act", lambda e: e.copy(out, in_), reads, writes)
        else:
            self.P.op(eng, lambda e: e.tensor_copy(out, in_), reads, writes)

    def tt(self, eng, out, a, b, op, reads, writes):
        self.P.op(eng, lambda e: e.tensor_tensor(out, a, b, op), reads, writes)

    def ts(self, eng, out, a, s1, op0, reads, writes, s2=None, op1=None):
        if op1 is None:
            self.P.op(eng, lambda e: e.tensor_scalar(out, a, s1, None, op0), reads, writes)
        else:
            self.P.op(eng, lambda e: e.tensor_scalar(out, a, s1, s2, op0, op1), reads, writes)

    def stt(self, eng, out, a, s, b, op0, op1, reads, writes):
        self.P.op(eng, lambda e: e.scalar_tensor_tensor(out, a, s, b, op0, op1), reads, writes)

    def memset(self, eng, ap, val, writes):
        self.P.op(eng, lambda e: e.memset(ap, val), [], writes)

    def rstd(self, dst, src, inv_n, reads, writes):
        self.act(dst, src, AF.Ln, list(reads) + [self.epsc], writes, scale=inv_n, bias=self.epsc.t[:dst.shape[0], 0:1])
        self.act(dst, dst, AF.Exp, writes, writes, scale=-0.5)

    def setup_consts(self):
        P = self.P
        self.epsc = P.sb("epsc", [128, 1], F32)
        self.memset("pool", self.epsc.t[:], EPS, [self.epsc])
        tmpf = P.sb("tmpf", [128, 128], F32)
        self.memset("pool", tmpf.t[:], 1.0, [tmpf])
        P.op("pool", lambda e: e.affine_select(tmpf.t[:], tmpf.t[:], [[-1, 128]], ALU.is_equal, 0.0,
                                              base=0, channel_multiplier=1), [tmpf], [tmpf])
        self.copy("dve", self.ident.t[:], tmpf.t[:], [tmpf], [self.ident])
        self.memset("pool", self.Uf.t[:], 1.0, [self.Uf])
        P.op("pool", lambda e: e.affine_select(self.Uf.t[:], self.Uf.t[:], [[1, 128]], ALU.is_ge, 0.0,
                                              base=0, channel_multiplier=-1), [self.Uf], [self.Uf])
        self.copy("dve", self.Ub.t[:], self.Uf.t[:], [self.Uf], [self.Ub])
        self.memset("pool", self.SLf.t[:], 1.0, [self.SLf])
        P.op("pool", lambda e: e.affine_select(self.SLf.t[:], self.SLf.t[:], [[-1, 128]], ALU.is_gt, 0.0,
                                              base=0, channel_multiplier=1), [self.SLf], [self.SLf])
        self.memset("pool", self.onesb.t[:], 1.0, [self.onesb])
        self.memset("pool", self.onesf.t[:], 1.0, [self.onesf])
        for p in range(2):
            self.memset("pool", self.S[p].t[:], 0.0, [self.S[p]])
            for hh in range(2):
                self.memset("pool", self.Sbf[p][hh].t[:], 0.0, [self.Sbf[p][hh]])
                self.memset("pool", self.ktl[p][hh].t[:], 0.0, [self.ktl[p][hh]])
        self.identf = tmpf
        g8 = P.sb("g8", [8, 5, 128], F32)
        for i, n in enumerate(("g_pre_mix", "g_pre_mem", "g_pre_ffn", "g_mem")):
            P.dma("sp", g8.t[:, i, :], self.g[n].rearrange("o (c p) -> (o c) p", p=128), [], [g8])
        P.dma("sp", g8.t[0:4, 4, :], self.g_gla.rearrange("o (c p) -> (o c) p", p=128), [], [g8])
        pg = self.B[0].t[:, 0:40]
        for i in range(5):
            nr = 8 if i < 4 else 4
            P.op("pe", lambda e, i=i, nr=nr: e.transpose(pg[:, i * 8:i * 8 + nr], g8.t[0:nr, i, :], tmpf.t[0:nr, 0:nr]),
                 [g8, tmpf], self.Bq[0])
        self.copy("dve", self.gpre.t[:, :, :], pg[:, 0:32].rearrange("p (a c) -> p a c", a=4), self.Bq[0], [self.gpre])
        self.copy("dve", self.ggla.t[:, :], pg[:, 32:36], self.Bq[0], [self.ggla])
        self.memset("pool", self.waext.t[:], 0.0, [self.waext])
        P.dma("pool", self.waext.t[0:16, :], self.w_alpha2[:, :], [], [self.waext])
        P.dma("pool", self.waext.t[32:33, :], self.b_alpha[:, :], [], [self.waext])
        self.memset("pool", self.glT.t[0:32, :], 0.0, [self.glT])
        self.memset("pool", self.glT.t[32:33, :], 1.0, [self.glT])
        P.dma("pool", self.Bm.t[:], self.bm_in.rearrange("h p j -> p h j"), [], [self.Bm])
        P.dma("pool", self.hm.t[:], self.hmask[:, :], [], [self.hm])

    def load_gpost(self, name):
        src = self.g[name][0:1, :].partition_broadcast(128)
        self.P.dma("sp", self.gpost.t[:], src[:, 0, :], [], [self.gpost])

    def convert_weights(self):
        self.conv_todo = []
        for k in ["w_mk", "w_mv", "w_in", "w_o", "w_mq", "w_mo", "w_g", "w_u", "w_d"]:
            rows = self.w[k].shape[0]
            for r0 in range(0, rows, 256):
                self.conv_todo.append((k, r0, min(rows, r0 + 256)))
        n_now = 16 if self.cfg.pref_t else len(self.conv_todo)
        self.convert_more(n_now)

    def convert_more(self, n, after=()):
        for _ in range(n):
            if not self.conv_todo:
                return
            k, r0, r1 = self.conv_todo.pop(0)
            self.P.dma("pool", self.wb[k][r0:r1, :], self.w[k][r0:r1, :], list(after), [], dram_w=[f"{k}:{r0 // 256}"])

    def warm(self, n):
        for _ in range(n):
            self.mm(self.B[1].t[:, 0:512], self.ident.t[:, :], self.Bm.t[:, 0, 0:512], True, True,
                    [self.ident, self.Bm], self.Bq[1])

    def make_schedule(self):
        c = self.cfg
        W8 = lambda k, c0, n: (k, 0, 8, [(c0, n, 0)])
        halo = [W8("w_in", C_KA, 512), W8("w_in", C_VA, 512)]
        pref = [("w_in", 0, 8, [(C_KB, 256, 0), (C_GL, 16, 256)]), W8("w_in", C_VB, 512)]
        memkv = [W8("w_mk", 0, 512), W8("w_mk", 512, 512), W8("w_mv", 0, 512), W8("w_mv", 512, 512)]
        grp = [W8("w_in", C_QA, 512), W8("w_in", C_KA, 512), W8("w_in", C_VA, 512), W8("w_in", C_QB, 512),
               W8("w_in", C_VB, 512), W8("w_in", C_GL, 528),
               W8("w_o", 0, 512), W8("w_o", 512, 512), W8("w_mq", 0, 512), W8("w_mq", 512, 512),
               W8("w_mo", 0, 512), W8("w_mo", 512, 512)]
        for j in range(6):
            n = 512 if j < 5 else 256
            grp += [W8("w_g", j * 512, n), W8("w_u", j * 512, n)]
        for cc in range(2):
            for kg in range(3):
                nk = 8 if kg < 2 else 6
                grp.append(("w_d", kg * 1024, nk, [(cc * 512, 512, 0)]))
        ng = c.npt // c.gt + 1
        self.sched = memkv + halo + (pref if c.pref_t else []) + grp * ng
        self.r_cur = 0
        self.r_loaded = 0
        self.r_rel = set()

    def _pump(self):
        n = len(self.ring)
        while (self.r_loaded < len(self.sched) and self.r_loaded < self.r_cur + n
               and (self.r_loaded < n or (self.r_loaded - n) in self.r_rel)):
            j = self.r_loaded
            key, row0, nk, pieces = self.sched[j]
            slot = self.ring[j % n]
            for (c0, ncol, s0) in pieces:
                src = self.wb[key][row0:row0 + nk * 128, c0:c0 + ncol].rearrange("(c p) f -> p c f", p=128)
                blks = [f"{key}:{b}" for b in range(row0 // 256, (row0 + nk * 128 + 255) // 256)]
                self.P.dma("sp", slot.t[:, 0:nk, s0:s0 + ncol], src, [], [slot], dram_r=blks)
            self.r_loaded += 1

    def wget(self, expect=None):
        j = self.r_cur
        self.r_cur += 1
        self._pump()
        assert self.r_loaded > j, ("ring stall", j)
        if expect is not None:
            assert self.sched[j][0] == expect, (self.sched[j], expect)
        return j, self.ring[j % len(self.ring)]

    def wrel(self, j):
        self.r_rel.add(j)
        self._pump()

    def init_psum_regions(self):
        self.Aq = []
        for a in self.A:
            lo, hi = a.sub(2)
            self.Aq.append([lo, lo, hi, hi])
        self.Bq = [[b, b, b, b] for b in self.B]

    def Ahalf(self, i):
        a, h = i // 2, i % 2
        return self.A[a].t[:, h * 512:(h + 1) * 512], self.Aq[a][2 * h:2 * h + 2]

    def ev_eng(self):
        return ("act", "dve")[self.rot("ev", 2)]

    def evac(self, out, in_, reads, writes, scale=None, eng=None):
        eng = eng or self.ev_eng()
        if scale is None:
            self.copy(eng, out, in_, reads, writes)
        elif eng == "act":
            self.act(out, in_, AF.Copy, reads, writes, scale=scale)
        else:
            self.ts("dve", out, in_, scale, ALU.mult, reads, writes)

    def prenorm(self, X, n, gi, dst, cs):
        self.prenorm_many([(X, n, cs)], gi, dst)

    def prenorm_many(self, items, gi, dst):
        self.prenorm_B(items, gi, dst, self.prenorm_A(items))

    def prenorm_A(self, items):
        return self.prenorm_A2(items, self.prenorm_A1(items))

    def prenorm_A1(self, items):
        sts = [self.st[self.rot("st", 4)] for _ in items]
        xbs = [self.xsb[self.rot("xsb", 4)] for _ in items]
        for (X, n, cs), st, xb in zip(items, sts, xbs):
            self.act(xb.t[:n, :], X.t[:n, :], AF.Square, [X, st], [xb, st], accum_out=st.t[:n, 0:1])
        for (X, n, cs), st in zip(items, sts):
            self.act(st.t[:n, 1:2], st.t[:n, 0:1], AF.Ln, [st, self.epsc], [st], scale=1.0 / D, bias=self.epsc.t[:n, 0:1])
        for (X, n, cs), st in zip(items, sts):
            self.act(st.t[:n, 1:2], st.t[:n, 1:2], AF.Exp, [st], [st], scale=-0.5)
        return (sts, xbs)

    def prenorm_A2(self, items, sx):
        sts, xbs = sx
        for (X, n, cs), st, xb in zip(items, sts, xbs):
            self.ts("dve", xb.t[:n, :], X.t[:n, :], st.t[:n, 1:2], ALU.mult, [X, st], [xb])
        return xbs

    def prenorm_B(self, items, gi, dst, xbs):
        for (X, n, cs), xb in zip(items, xbs):
            bi = 2 + self.rot("trb", 2)
            ptv = self.B[bi].t[:].bitcast(BF16).rearrange("p (c t) -> p c t", c=8)
            for c in range(8):
                self.tr(ptv[:, c, 0:n], xb.t[:n, c * 128:(c + 1) * 128], n, [xb], self.Bq[bi])
            if gi is None:
                self.evac(dst.t[:, :, cs], ptv[:, :, 0:n], self.Bq[bi], [dst])
            else:
                gb = self.gpre.t[:, gi, :].unsqueeze(2).to_broadcast([128, 8, n])
                self.tt("dve", dst.t[:, :, cs], ptv[:, :, 0:n], gb, ALU.mult, self.Bq[bi] + [self.gpre], [dst])

    def norm_transition(self, tiles, gi):
        items = [(self.X[t.i], t.n, t.cs) for t in tiles]
        pairs = [(tiles[i:i + 2], items[i:i + 2]) for i in range(0, len(tiles), 2)]
        for tp, _ in pairs:
            self.postnorm_many(tp)
        xbs = [self.prenorm_A(ip) for _, ip in pairs]
        for (_, ip), xb in zip(pairs, xbs):
            self.prenorm_B(ip, gi, self.hT, xb)

    def postnorm_many(self, tiles):
        sts = [self.st[self.rot("st", 4)] for _ in tiles]
        for t, st in zip(tiles, sts):
            Y = self.Y[t.i]
            jb = self.xsb[self.rot("xsb", 4)]
            self.act(jb.t[:t.n, :], Y.t[:t.n, :], AF.Square, [Y, st], [jb, st], accum_out=st.t[:t.n, 0:1])
        for t, st in zip(tiles, sts):
            self.act(st.t[:t.n, 1:2], st.t[:t.n, 0:1], AF.Ln, [st, self.epsc], [st], scale=1.0 / D, bias=self.epsc.t[:t.n, 0:1])
        for t, st in zip(tiles, sts):
            self.act(st.t[:t.n, 1:2], st.t[:t.n, 1:2], AF.Exp, [st], [st], scale=-0.5)
        for ix, (t, st) in enumerate(zip(tiles, sts)):
            Y = self.Y[t.i]
            eng = "pool" if ix == len(tiles) - 1 and len(tiles) > 2 else "dve"
            self.tt(eng, Y.t[:t.n, :], Y.t[:t.n, :], self.gpost.t[:t.n, :], ALU.mult, [Y, self.gpost], [Y])
        for t, st in zip(tiles, sts):
            X, Y = self.X[t.i], self.Y[t.i]
            self.stt("dve", X.t[:t.n, :], Y.t[:t.n, :], st.t[:t.n, 1:2], X.t[:t.n, :], ALU.mult, ALU.add, [Y, st, X], [X])

    def proj_fm(self, slot, scol0, m, srcT, ncols, evac_fn):
        bi = self.rot("fmb", 2)
        ps = self.B[bi].t[:m, 0:ncols]
        for k in range(8):
            self.mm(ps, slot.t[:, k, scol0:scol0 + m], srcT.t[:, k, 0:ncols], k == 0, k == 7,
                    [slot, srcT], self.Bq[bi])
        evac_fn(ps, self.Bq[bi])

    def proj_tm(self, slot, scol0, ncol, srcT, t, evac_fn, nk=8):
        ap, bufs = self.Ahalf(self.rot("tmb", 4))
        ps = ap[:t.n, 0:ncol]
        for k in range(nk):
            self.mm(ps, srcT.t[:, k, t.cs], slot.t[:, k, scol0:scol0 + ncol], k == 0, k == nk - 1,
                    [slot, srcT], bufs)
        evac_fn(ps, bufs)

    def stage_win(self, tiles, last, kaT_cur, va_cur):
        P = self.P
        sample = tiles[0].kind == "s"
        NTc = sum(t.n for t in tiles)
        hT, QT = self.hT, self.QT
        j, slot = self.wget("w_in")
        for c in range(4):
            self.proj_fm(slot, c * 128, 128, hT, NTc,
                         lambda ps, pb, c=c: self.evac(QT.t[:, c, 0:NTc], ps, pb, [QT], scale=0.125))
        self.wrel(j)
        j, slot = self.wget("w_in")
        kdst = self.kaTs if sample else kaT_cur
        for c in range(4):
            self.proj_fm(slot, c * 128, 128, hT, NTc,
                         lambda ps, pb, c=c: self.evac(kdst.t[:, c, 0:NTc], ps, pb, [kdst]))
        if last:
            for t in tiles:
                Y = self.Y[t.i]
                self.proj_tm(slot, 0, 512, hT, t,
                             lambda ps, pb, t=t, Y=Y: self.evac(Y.t[:t.n, 0:512], ps, pb, [Y]))
        self.wrel(j)
        j, slot = self.wget("w_in")
        for t in tiles:
            vdst = self.vas[t.i].t[:t.n, :] if sample else va_cur.t[:t.n, t.i, :]
            vbuf = self.vas[t.i] if sample else va_cur
            Y = self.Y[t.i]

            def ev(ps, pb, t=t, vdst=vdst, vbuf=vbuf, Y=Y):
                self.evac(vdst, ps, pb, [vbuf])
                if last:
                    self.evac(Y.t[:t.n, 512:1024], ps, pb, [Y])
            self.proj_tm(slot, 0, 512, hT, t, ev)
        self.wrel(j)
        if last:
            for t in tiles:
                Y = self.Y[t.i]
                if sample:
                    dk, dv = self.sak[t.c0:t.c0 + t.n, :], self.sav[t.c0:t.c0 + t.n, :]
                else:
                    dk, dv = self.pak[t.c0:t.c0 + t.n, :], self.pav[t.c0:t.c0 + t.n, :]
                P.dma("pool", dk, Y.t[:t.n, 0:512], [Y], [], out=True)
                P.dma("pool", dv, Y.t[:t.n, 512:1024], [Y], [], out=True)
        j, slot = self.wget("w_in")
        for c in range(4):
            sc = 0.125 if c < 2 else None
            self.proj_fm(slot, c * 128, 128, hT, NTc,
                         lambda ps, pb, c=c, sc=sc: self.evac(QT.t[:, 4 + c, 0:NTc], ps, pb, [QT], scale=sc))
        for t in tiles:
            self.proj_tm(slot, 256, 256, hT, t,
                         lambda ps, pb, t=t: self.evac(self.kb.t[:t.n, t.i, :], ps, pb, [self.kb]))
        self.wrel(j)
        j, slot = self.wget("w_in")
        for t in tiles:
            self.proj_tm(slot, 0, 512, hT, t,
                         lambda ps, pb, t=t: self.evac(self.vb.t[:t.n, t.i, :], ps, pb, [self.vb]))
        self.wrel(j)
        j, slot = self.wget("w_in")
        self.proj_fm(slot, 0, 16, hT, NTc,
                     lambda ps, pb: self.evac(self.glT.t[0:16, 0:NTc], ps, pb, [self.glT]))
        for c in range(4):
            self.proj_fm(slot, 16 + c * 128, 128, hT, NTc,
                         lambda ps, pb, c=c: self.act(self.rT.t[:, c, 0:NTc], ps, AF.Silu, pb, [self.rT]))
        self.wrel(j)
        self.softplus_tiles(tiles)

    def softplus_tiles(self, tiles, glT=None, sp=None, sp_ap=None):
        glT = glT or self.glT
        sp = sp or self.sp
        spt = sp_ap if sp_ap is not None else sp.t
        glt = glT.t if glT.t is not None else self.glT1_ap
        pss = []
        for t in tiles:
            ap, bufs = self.Ahalf(self.rot("tmb", 4))
            ps = ap[:t.n, 0:256]
            self.mm(ps, glt[0:33, t.cs], self.waext.t[0:33, :], True, True, [glT, self.waext], bufs)
            pss.append((ps, bufs))
        for t, (ps, bufs) in zip(tiles, pss):
            self.act(spt[:t.n, t.i, :], ps, AF.Exp, bufs, [sp], scale=-1.0)
        for t in tiles:
            dst = spt[:t.n, t.i, :]
            self.act(dst, dst, AF.Ln, [sp, self.onesf], [sp], bias=self.onesf.t[:t.n, 0:1])

    def gla_stages(self, t):
        C = t.n
        sp, kb, vb, QT = self.sp, self.kb, self.vb, self.QT
        B0, B1 = self.B[0].t, self.B[1].t
        q0, q1 = self.Bq[0], self.Bq[1]

        def g0():
            suf = B0[:C, 256:512]
            self.mm(suf, self.SLf.t[:C, :C], sp.t[:C, t.i, :], True, True, [self.SLf, sp], q0)
            csts = []
            for p in range(2):
                cst = B0[:, p * 128:p * 128 + C]
                self.mm(cst, sp.t[:C, t.i, p * 128:(p + 1) * 128], self.Uf.t[:C, :C], True, True, [sp, self.Uf], q0)
                csts.append(cst)
            self.act(self.kdw.t[:C, :], suf, AF.Exp, q0, [self.kdw], scale=-1.0 / 16)
            for p in range(2):
                self.act(self.Ep[p].t[:, :C], csts[p], AF.Exp, q0, [self.Ep[p]], scale=-1.0 / 16)
                self.act(self.Em[p].t[:, :C], csts[p], AF.Exp, q0, [self.Em[p]], scale=1.0 / 16)
            self.tt("dve", self.kdec.t[:C, :], kb.t[:C, t.i, :], self.kdw.t[:C, :], ALU.mult, [kb, self.kdw], [self.kdec])
            for p in range(2):
                self.tt("dve", self.qtl[p].t[:, :C], QT.t[:, 4 + p, t.cs], self.Ep[p].t[:, :C], ALU.mult,
                        [QT, self.Ep[p]], [self.qtl[p]])
                for hh in range(2):
                    r = slice(hh * 64, hh * 64 + 64)
                    self.tt("dve", self.ktl[p][hh].t[r, :C], QT.t[r, 6 + p, t.cs], self.Em[p].t[r, :C], ALU.mult,
                            [QT, self.Em[p]], [self.ktl[p][hh]])
                self.copy("dve", self.dec[p].t[:, 0:1], self.Ep[p].t[:, C - 1:C], [self.Ep[p]], [self.dec[p]])

        def g1():
            for h in range(4):
                p, r = h // 2, slice((h % 2) * 64, (h % 2) * 64 + 64)
                self.mm(B1[:C, h * 128:h * 128 + C], self.ktl[p][h % 2].t[:, :C], self.qtl[p].t[:, :C], True, True,
                        [self.ktl[p][h % 2], self.qtl[p]], q1)
            for h in range(4):
                self.tt("dve", self.ATm[h].t[:C, :C], B1[:C, h * 128:h * 128 + C], self.Ub.t[:C, :C], ALU.mult,
                        q1 + [self.Ub], [self.ATm[h]])

        def g2():
            for h in range(4):
                p, r = h // 2, slice((h % 2) * 64, (h % 2) * 64 + 64)
                oT = B0[:, h * 128:h * 128 + C]
                self.mm(oT, vb.t[:C, t.i, h * 128:(h + 1) * 128], self.ATm[h].t[:C, :C], True, False, [vb, self.ATm[h]], q0)
                self.mm(oT, self.Sbf[p][h % 2].t[:, :], self.qtl[p].t[:, :C], False, True, [self.Sbf[p][h % 2], self.qtl[p]], q0)
            for h in range(4):
                self.act(self.sq[h].t[:, :C], B0[:, h * 128:h * 128 + C], AF.Square, q0, [self.sq[h]])

        def g3():
            for h in range(4):
                self.mm(B1[:, h * 128:h * 128 + C], self.onesb.t[:, :], self.sq[h].t[:, :C], True, True,
                        [self.onesb, self.sq[h]], q1)
            for h in range(4):
                self.act(self.rs[h].t[:, :C], B1[:, h * 128:h * 128 + C], AF.Ln, q1 + [self.epsc], [self.rs[h]], scale=1.0 / 128,
                         bias=self.epsc.t[:, 0:1])
            for h in range(4):
                self.act(self.rs[h].t[:, :C], self.rs[h].t[:, :C], AF.Exp, [self.rs[h]], [self.rs[h]], scale=-0.5)
            for h in range(4):
                self.stt("dve", self.t1[h].t[:, :C], B0[:, h * 128:h * 128 + C], self.ggla.t[:, h:h + 1],
                         self.rs[h].t[:, :C], ALU.mult, ALU.mult, q0 + [self.ggla, self.rs[h]], [self.t1[h]])
            for h in range(4):
                self.tt("dve", self.mixT.t[:, 4 + h, t.cs], self.t1[h].t[:, :C], self.rT.t[:, h, t.cs], ALU.mult,
                        [self.t1[h], self.rT], [self.mixT])

        def g4():
            for p in range(2):
                ds = B0[:, p * 256:(p + 1) * 256]
                self.mm(ds, self.kdec.t[:C, p * 128:(p + 1) * 128], vb.t[:C, t.i, p * 256:(p + 1) * 256], True, True,
                        [self.kdec, vb], q0)
            for p in range(2):
                S = self.S[p]
                for hh in range(2):
                    r = slice(hh * 64, hh * 64 + 64)
                    self.stt("dve", S.t[r, :], S.t[r, :], self.dec[p].t[r, 0:1],
                             B0[r, p * 256 + hh * 128:p * 256 + (hh + 1) * 128],
                             ALU.mult, ALU.add, [S, self.dec[p]] + q0, [S])
                self.copy_Sbf(p)

        return [g0, g1, g2, g3, g4]

    def prefix_sufs(self, tiles, bset):
        kb_ap, vb_ap, sp_ap, kbB, vbB, spB = bset
        regs = []
        for t in tiles:
            ap, bufs = self.Ahalf(t.i)
            self.mm(ap[:, 0:256], self.SLf.t[:, :], sp_ap[:, t.i, :], True, True, [self.SLf, spB], bufs)
            for p in range(2):
                self.mm(ap[:, 256 + p:257 + p], sp_ap[:, t.i, p * 128:(p + 1) * 128], self.onesf.t[:, 0:1], True, True,
                        [spB, self.onesf], bufs)
            regs.append((ap, bufs))
        for t, (ap, bufs) in zip(tiles, regs):
            Y = self.Y[t.i]
            self.act(Y.t[:, 0:256], ap[:, 0:256], AF.Exp, bufs, [Y], scale=-1.0 / 16)
            self.act(self.dec8.t[:, 2 * t.i:2 * t.i + 2], ap[:, 256:258], AF.Exp, bufs, [self.dec8], scale=-1.0 / 16)
        for t in tiles:
            Y = self.Y[t.i]
            self.tt("dve", self.kdec4[t.i].t[:, :], kb_ap[:, t.i, :], Y.t[:, 0:256], ALU.mult, [kbB, Y], [self.kdec4[t.i]])

    def prefix_ds(self, tiles, bset):
        kb_ap, vb_ap, sp_ap, kbB, vbB, spB = bset
        for t in tiles:
            kd = self.kdec4[t.i]
            for p in range(2):
                bi = self.rot("pfx", 2)
                ds = self.B[bi].t[:, 0:256]
                self.mm(ds, kd.t[:, p * 128:(p + 1) * 128], vb_ap[:, t.i, p * 256:(p + 1) * 256], True, True,
                        [kd, vbB], self.Bq[bi])
                self.stt("dve", self.S2[p], self.S2[p], self.dec8.t[:, 2 * t.i + p:2 * t.i + p + 1], ds,
                         ALU.mult, ALU.add, [self.S2b[p], self.dec8] + self.Bq[bi], [self.S2b[p]])

    def attn_S(self, nq, h, qT_ap, qbufs, ksegs, bias_ap, extra_bias):
        ai = self.rot("attA", 2)
        A, Aq = self.A[ai].t, self.Aq[ai]
        nk = sum(n for _, n, _ in ksegs)
        sbufs = Aq[0:3]
        ops = []
        for c0 in range(0, nk, 512):
            c1 = min(nk, c0 + 512)
            ops.append((c0, c1, self.ident.t[:nq, :nq], bias_ap[:, c0:c1], [self.ident, self.Bm]))
        if extra_bias is not None:
            l_ap, r_ap, n_e, ebufs = extra_bias
            ops.append((0, n_e, l_ap, r_ap, ebufs))
        col = 0
        for (k_ap, n, kbufs) in ksegs:
            o = 0
            while o < n:
                e = min(n, o + (512 - (col + o) % 512))
                ops.append((col + o, col + e, qT_ap, k_ap[:, o:e], qbufs + kbufs))
                o = e
            col += n
        for bank in range(2):
            bops = [x for x in ops if x[0] // 512 == bank]
            for ix, (c0, c1, l_ap, r_ap, rb) in enumerate(bops):
                self.mm(A[:nq, c0:c1], l_ap, r_ap, ix == 0, ix == len(bops) - 1, rb, sbufs)
        bi = self.rot("attP", 2)
        Pb, rs8, ri8 = self.Pb[bi], self.rs8[bi], self.ri8[bi]
        self.act(Pb.t[:nq, 0:nk], A[:nq, 0:nk], AF.Exp, sbufs + [rs8], [Pb, rs8], accum_out=rs8.t[:nq, 0:1])
        self.P.op("dve", lambda e: e.reciprocal(ri8.t[:nq, 0:1], rs8.t[:nq, 0:1]), [rs8], [ri8])
        self.ts("dve", Pb.t[:nq, 0:nk], Pb.t[:nq, 0:nk], ri8.t[:nq, 0:1], ALU.mult, [Pb, ri8], [Pb])
        return {"ai": ai, "bi": bi, "nk": nk, "h": h, "nq": nq}

    def attn_rest(self, ctx, vblocks, out_ap, out_buf):
        self.attn_T(ctx, vblocks)
        self.attn_PV(ctx, vblocks, out_ap, out_buf)

    def attn_T(self, ctx, vblocks):
        nq, bi = ctx["nq"], ctx["bi"]
        Pn, PT = self.Pb[bi], self.PT[bi]
        ti = 2 + self.rot("trb", 2)
        ptv = self.B[ti].t[:].bitcast(BF16).rearrange("p (c t) -> p c t", c=8)
        col = 0
        for bix, (_, n, _) in enumerate(vblocks):
            self.tr(ptv[:n, bix, 0:nq], Pn.t[:nq, col:col + n], nq, [Pn], self.Bq[ti])
            col += n
        nb = len(vblocks)
        nmax = max(n for _, n, _ in vblocks)
        self.evac(PT.t[:nmax, 0:nb, 0:nq], ptv[:nmax, 0:nb, 0:nq], self.Bq[ti], [PT], eng="act")

    def attn_PV(self, ctx, vblocks, out_ap, out_buf):
        nq, h, ai, bi = ctx["nq"], ctx["h"], ctx["ai"], ctx["bi"]
        A, Aq = self.A[ai].t, self.Aq[ai]
        PT = self.PT[bi]
        r = slice((h % 2) * 64, (h % 2) * 64 + 64)
        nb = len(vblocks)
        o_ps = A[:, 768:768 + nq]
        for bix, (v_ap, n, vbufs) in enumerate(vblocks):
            self.mm(o_ps, v_ap, PT.t[:n, bix, 0:nq], bix == 0, bix == nb - 1, vbufs + [PT], [Aq[3]])
        self.evac(out_ap, A[r, 768:768 + nq], [Aq[3]], [out_buf])

    def mix_tile(self, t, head_args):
        import os
        km = int(os.environ.get("KM", "0"))
        stages = self.gla_stages(t)
        if km == 1:
            stages = []
        args = [head_args(h) for h in range(8)]
        if km == 2:
            for h in range(8):
                ctx = self.attn_S(*args[h][0])
                self.attn_rest(ctx, args[h][1], args[h][2], self.mixT)
                if h < len(stages):
                    stages[h]()
            return
        if km >= 4:
            stages = stages[:km - 3]
        if km >= 3:
            for h in range(8):
                ctx = self.attn_S(*args[h][0])
                self.attn_rest(ctx, args[h][1], args[h][2], self.mixT)
            for st in stages:
                st()
            return
        ctx = self.attn_S(*args[0][0])
        pend = None
        for h in range(8):
            nxt = self.attn_S(*args[h + 1][0]) if h + 1 < 8 else None
            self.attn_T(ctx, args[h][1])
            if pend is not None:
                self.attn_PV(pend[0], pend[1], pend[2], self.mixT)
            pend = (ctx, args[h][1], args[h][2])
            if h < len(stages):
                stages[h]()
            ctx = nxt
        self.attn_PV(pend[0], pend[1], pend[2], self.mixT)

    def proj_to_Y(self, srcT, wkey, tiles, kgroups=(8,)):
        for cc in range(2):
            accs = [self.Ahalf(t.i) for t in tiles]
            nkg = len(kgroups)
            kbase = 0
            for gi, nk in enumerate(kgroups):
                j, slot = self.wget(wkey)
                for t, (ap, bufs) in zip(tiles, accs):
                    for k in range(nk):
                        self.mm(ap[:t.n, :], srcT.t[:, kbase + k, t.cs], slot.t[:, k, 0:512],
                                gi == 0 and k == 0, gi == nkg - 1 and k == nk - 1, [srcT, slot], bufs)
                self.wrel(j)
                kbase += nk
            for t, (ap, bufs) in zip(tiles, accs):
                Y = self.Y[t.i]
                self.evac(Y.t[:t.n, cc * 512:(cc + 1) * 512], ap[:t.n, :], bufs, [Y])
        self.warm(self.cfg.warm_y)

    def mem_S(self, t, KT):
        n = t.n
        QT = self.QT
        ai = self.rot("attA", 2)
        A, Aq = self.A[ai].t, self.Aq[ai]
        for h in range(4):
            for kk in range(2):
                self.mm(A[:n, h * 256:(h + 1) * 256], QT.t[:, 2 * h + kk, t.cs], KT.t[:, 2 * h + kk, :],
                        kk == 0, kk == 1, [QT, KT], [Aq[h]])
        bi = self.rot("attP", 2)
        Pb, rs8, ri8 = self.Pb[bi], self.rs8[bi], self.ri8[bi]
        for h in range(4):
            self.act(Pb.t[:n, h * 256:(h + 1) * 256], A[:n, h * 256:(h + 1) * 256], AF.Exp, [Aq[h], rs8], [Pb, rs8],
                     scale=1.0 / 16, accum_out=rs8.t[:n, h:h + 1])
        self.P.op("dve", lambda e: e.reciprocal(ri8.t[:n, 0:4], rs8.t[:n, 0:4]), [rs8], [ri8])
        rb = ri8.t[:n, 0:4].unsqueeze(2).to_broadcast([n, 4, 256])
        pv = Pb.t[:n, :].rearrange("p (h k) -> p h k", h=4)
        self.tt("dve", pv, pv, rb, ALU.mult, [Pb, ri8], [Pb])
        return {"bi": bi, "t": t}

    def mem_T(self, ctx):
        t, bi = ctx["t"], ctx["bi"]
        n = t.n
        Pn, PT = self.Pb[bi], self.PT[bi]
        ti = 2 + self.rot("trb", 2)
        ptv = self.B[ti].t[:].bitcast(BF16).rearrange("p (c t) -> p c t", c=8)
        for h in range(4):
            for kb in range(2):
                c0 = h * 256 + kb * 128
                self.tr(ptv[:, 2 * h + kb, 0:n], Pn.t[:n, c0:c0 + 128], n, [Pn], self.Bq[ti])
        self.evac(PT.t[:, :, 0:n], ptv[:, :, 0:n], self.Bq[ti], [PT])

    def mem_PV(self, ctx, V):
        t, bi = ctx["t"], ctx["bi"]
        n = t.n
        PT = self.PT[bi]
        for h in range(4):
            for dc in range(2):
                c = 2 * h + dc
                bk = c // 4
                out = self.B[bk].t[:, (c % 4) * 128:(c % 4) * 128 + n]
                for kb in range(2):
                    self.mm(out, V.t[:, kb, h * 256 + dc * 128:h * 256 + dc * 128 + 128],
                            PT.t[:, 2 * h + kb, 0:n], kb == 0, kb == 1, [V, PT], self.Bq[bk])
        for bk in range(2):
            src = self.B[bk].t[:, :].rearrange("p (c t) -> p c t", c=4)[:, :, 0:n]
            self.evac(self.mixT.t[:, 4 * bk:4 * bk + 4, t.cs], src, self.Bq[bk], [self.mixT])

    def mem_attn_group(self, tiles, kv_of):
        ctx = self.mem_S(tiles[0], kv_of(tiles[0])[0])
        pend = None
        for ix, t in enumerate(tiles):
            nxt = self.mem_S(tiles[ix + 1], kv_of(tiles[ix + 1])[0]) if ix + 1 < len(tiles) else None
            self.mem_T(ctx)
            if pend is not None:
                self.mem_PV(pend[0], pend[1])
            pend = (ctx, kv_of(t)[1])
            ctx = nxt
        self.mem_PV(pend[0], pend[1])

    def stage_ffn(self, tiles):
        NTc = sum(t.n for t in tiles)
        hT = self.hT
        for jp in range(6):
            nch = 4 if jp < 5 else 2
            jg, sg = self.wget("w_g")
            gates = []
            for cc in range(nch):
                a, gb = self.Ahalf(cc)
                gps = a[:, 0:NTc]
                for k in range(8):
                    self.mm(gps, sg.t[:, k, cc * 128:(cc + 1) * 128], hT.t[:, k, 0:NTc], k == 0, k == 7, [sg, hT], gb)
                gates.append((gps, gb))
            self.wrel(jg)
            ju, su = self.wget("w_u")
            for cc in range(nch):
                f = jp * 4 + cc
                gps, gb = gates[cc]
                si = self.silu[self.rot("silu", 2)]
                self.act(si.t[:, 0:NTc], gps, AF.Silu, gb, [si])
                bi = self.rot("ffu", 2)
                ups = self.B[bi].t[:, 0:NTc]
                for k in range(8):
                    self.mm(ups, su.t[:, k, cc * 128:(cc + 1) * 128], hT.t[:, k, 0:NTc], k == 0, k == 7, [su, hT], self.Bq[bi])
                self.tt("dve", self.hidT.t[:, f, 0:NTc], si.t[:, 0:NTc], ups, ALU.mult, [si] + self.Bq[bi], [self.hidT])
            self.wrel(ju)
        self.load_gpost("g_post_ffn")
        self.proj_to_Y(self.hidT, "w_d", tiles, kgroups=(8, 8, 6))

    def group(self, tiles, first, last, g):
        P, cfg = self.P, self.cfg
        sample = tiles[0].kind == "s"
        NTc = sum(t.n for t in tiles)
        cur, prev = g % 2, (g + 1) % 2
        kaT_cur, va_cur = self.kaT[cur], self.va[cur]
        kaT_prev, va_prev = self.kaT[prev], self.va[prev]
        P.cur_stage = f"g{g}:load+norm"
        for t in tiles:
            src = self.xs[t.c0:t.c0 + t.n, :] if sample else self.xp[(g * cfg.gt + t.i) * 128:(g * cfg.gt + t.i + 1) * 128, :]
            P.dma("sp", self.X[t.i].t[:t.n, :], src, [], [self.X[t.i]])
        self.warm(self.cfg.warm_g)
        items = [(self.X[t.i], t.n, t.cs) for t in tiles]
        prs = [items[i:i + 2] for i in range(0, len(items), 2)]
        xbs = [self.prenorm_A(ip) for ip in prs]
        for ip, xb in zip(prs, xbs):
            self.prenorm_B(ip, 0, self.hT, xb)
        P.cur_stage = f"g{g}:win"
        self.stage_win(tiles, last, kaT_cur, va_cur)
        P.cur_stage = f"g{g}:mix"
        import os
        kg = int(os.environ.get("KG", "9"))
        if kg < 2:
            return
        for t in tiles:
            if sample:
                b = t.i
                self.load_state(b)

                def head_args(h, t=t, b=b):
                    p, hh = h // 2, h % 2
                    r = slice(hh * 64, hh * 64 + 64)
                    ckT, cv = self.ckT[b], self.cv[b]
                    ksegs = [(ckT.t[r, p, 0:512], 512, [ckT]), (self.kaTs.t[r, p, t.cs], 32, [self.kaTs])]
                    vbl = [(cv.t[:, kb, p * 128:(p + 1) * 128], 128, [cv]) for kb in range(4)]
                    vbl.append((self.vas[b].t[0:32, p * 128:(p + 1) * 128], 32, [self.vas[b]]))
                    return ((32, h, self.QT.t[r, p, t.cs], [self.QT], ksegs, self.Bm.t[0:32, h, 0:544], None),
                            vbl, self.mixT.t[r, p, t.cs])
                self.mix_tile(t, head_args)
                self.store_state(self.sglao[b, :, :])
            else:
                def head_args(h, t=t):
                    i = t.i
                    p, hh = h // 2, h % 2
                    r = slice(hh * 64, hh * 64 + 64)
                    ksegs, vbl = [], []
                    ksegs.append((kaT_prev.t[r, p, i * 128:512], (4 - i) * 128, [kaT_prev]))
                    ksegs.append((kaT_cur.t[r, p, 0:(i + 1) * 128], (i + 1) * 128, [kaT_cur]))
                    for kb in range(5):
                        if kb < 4 - i:
                            vbl.append((va_prev.t[:, i + kb, p * 128:(p + 1) * 128], 128, [va_prev]))
                        else:
                            vbl.append((va_cur.t[:, kb - (4 - i), p * 128:(p + 1) * 128], 128, [va_cur]))
                    eb = None
                    if first:
                        ne = (4 - i) * 128
                        eb = (self.onesb.t[0:1, 0:128], self.hm.t[0:1, i * 128:512], ne, [self.onesb, self.hm])
                    return ((128, h, self.QT.t[r, p, t.cs], [self.QT], ksegs, self.Bm.t[:, h, :], eb),
                            vbl, self.mixT.t[r, p, t.cs])
                self.mix_tile(t, head_args)
        if last and not sample:
            self.store_state(self.pgla[:, :])
        if kg < 3:
            return
        P.cur_stage = f"g{g}:wo"
        self.load_gpost("g_post_mix")
        self.proj_to_Y(self.mixT, "w_o", tiles)
        self.norm_transition(tiles, 1)
        P.cur_stage = f"g{g}:mem"
        for cc in range(2):
            j, slot = self.wget("w_mq")
            for c in range(4):
                self.proj_fm(slot, c * 128, 128, self.hT, NTc,
                             lambda ps, pb, c=c, cc=cc: self.evac(self.QT.t[:, cc * 4 + c, 0:NTc], ps, pb, [self.QT]))
            self.wrel(j)
        if sample:
            self.mem_attn_group(tiles, lambda t: (self.memKTs[t.i], self.memVs[t.i]))
        else:
            self.mem_attn_group(tiles, lambda t: (self.memKT, self.memV))
        self.load_gpost("g_post_mem")
        self.proj_to_Y(self.mixT, "w_mo", tiles)
        self.norm_transition(tiles, 2)
        P.cur_stage = f"g{g}:ffn"
        self.stage_ffn(tiles)
        P.cur_stage = f"g{g}:end"
        for i0 in range(0, len(tiles), 2):
            tp = tiles[i0:i0 + 2]
            self.postnorm_many(tp)
            for t in tp:
                if sample:
                    dst = self.ys[t.c0:t.c0 + t.n, :]
                else:
                    dst = self.yp[(g * cfg.gt + t.i) * 128:(g * cfg.gt + t.i + 1) * 128, :]
                P.dma("pool", dst, self.X[t.i].t[:t.n, :], [self.X[t.i]], [], out=True)

    def store_state(self, dst):
        for p in range(2):
            self.P.dma("pool", dst[p * 128:(p + 1) * 128, :], self.S[p].t[:, :], [self.S[p]], [], out=True)

    def load_state(self, b):
        for p in range(2):
            self.P.dma("sp", self.S[p].t[:, :], self.sgla[b, p * 128:(p + 1) * 128, :], [], [self.S[p]])
            self.copy_Sbf(p)

    def copy_Sbf(self, p):
        for hh in range(2):
            r = slice(hh * 64, hh * 64 + 64)
            self.copy("act", self.Sbf[p][hh].t[r, :], self.S[p].t[r, :], [self.S[p]], [self.Sbf[p][hh]])

    def mem_kv_prompt(self):
        import os
        P = self.P
        P.cur_stage = "memkv"
        sub = int(os.environ.get("KSUB", "9"))
        tiles = [Tile(i, 128, i * 128, "m") for i in range(2)]
        for t in tiles:
            P.dma("sp", self.X[t.i].t[:, :], self.memp[t.c0:t.c0 + 128, :], [], [self.X[t.i]])
        self.prenorm_many([(self.X[t.i], 128, t.cs) for t in tiles], 3, self.hT)
        if sub < 2:
            return
        for which in range(2):
            for cc in range(2):
                j, slot = self.wget("w_mk" if which == 0 else "w_mv")
                kvar = int(os.environ.get("KVAR", "9"))
                if kvar == 1:
                    self.copy("dve", self.Y[0].t[:, 0:512], slot.t[:, 0, 0:512], [slot], [self.Y[0]])
                    self.wrel(j)
                    continue
                for t in tiles:
                    if kvar == 2 and (which, cc, t.i) != (0, 0, 0):
                        continue
                    if kvar == 3 and (which, cc) != (0, 0):
                        continue
                    if kvar == 4 and which != 0:
                        continue
                    Y = self.Y[2 * which + t.i]

                    def ev(ps, pb, t=t, Y=Y, cc=cc, which=which):
                        e1 = self.ev_eng()
                        self.evac(Y.t[:, cc * 512:(cc + 1) * 512], ps, pb, [Y], eng=e1)
                        if which == 1 and kvar != 5:
                            self.evac(self.memV.t[:, t.i, cc * 512:(cc + 1) * 512], ps, pb, [self.memV],
                                      eng=(e1 if kvar == 7 else None))
                    self.proj_tm(slot, 0, 512, self.hT, t, ev)
                if which == 0 and sub >= 3:
                    for c in range(4):
                        self.proj_fm(slot, c * 128, 128, self.hT, 256,
                                     lambda ps, pb, c=c, cc=cc: self.evac(self.memKT.t[:, cc * 4 + c, :], ps, pb,
                                                                          [self.memKT]))
                self.wrel(j)
        if sub < 5:
            return
        for t in tiles:
            P.dma("pool", self.pmk[t.c0:t.c0 + 128, :], self.Y[t.i].t[:, :], [self.Y[t.i]], [], out=True)
            P.dma("pool", self.pmv[t.c0:t.c0 + 128, :], self.Y[2 + t.i].t[:, :], [self.Y[2 + t.i]], [], out=True)

    def halo_pass(self):
        P = self.P
        P.cur_stage = "halo"
        tiles = [Tile(i, 128, i * 128, "h") for i in range(4)]
        for t in tiles:
            P.dma("sp", self.X[t.i].t[:, :], self.xh[t.c0:t.c0 + 128, :], [], [self.X[t.i]])
        self.prenorm_many([(self.X[t.i], 128, t.cs) for t in tiles], 0, self.hT)
        kdst, vdst = self.kaT[1], self.va[1]
        j, slot = self.wget("w_in")
        for c in range(4):
            self.proj_fm(slot, c * 128, 128, self.hT, 512,
                         lambda ps, pb, c=c: self.evac(kdst.t[:, c, 0:512], ps, pb, [kdst]))
        self.wrel(j)
        j, slot = self.wget("w_in")
        for t in tiles:
            self.proj_tm(slot, 0, 512, self.hT, t,
                         lambda ps, pb, t=t: self.evac(vdst.t[:, t.i, :], ps, pb, [vdst]))
        self.wrel(j)

    def prefix_pass(self):
        P, cfg = self.P, self.cfg
        P.cur_stage = "prefix"
        if cfg.pref_t == 0:
            return
        ja, sa = self.wget("w_in")
        jb, sb_ = self.wget("w_in")
        for kc in range(8):
            self.ts("pool", sa.t[:, kc, 0:272], sa.t[:, kc, 0:272], self.gpre.t[:, 0, kc:kc + 1], ALU.mult,
                    [sa, self.gpre], [sa])
            self.ts("dve", sb_.t[:, kc, 0:512], sb_.t[:, kc, 0:512], self.gpre.t[:, 0, kc:kc + 1], ALU.mult,
                    [sb_, self.gpre], [sb_])
        S2 = [self.Y[p].t[:, 512:768] for p in range(2)]
        S2b = [Buf(f"S2_{p}") for p in range(2)]
        for p in range(2):
            self.memset("pool", S2[p], 0.0, [S2b[p], self.Y[p]])
        self.S2, self.S2b = S2, S2b
        G = cfg.pref_t // cfg.gt
        tiles = [Tile(i, 128, i * 128, "x") for i in range(cfg.gt)]
        items = [(self.X[t.i], 128, t.cs) for t in tiles]
        h = self.hidT.t
        kb1 = h[:, 0:2, :].rearrange("p a b -> p (a b)").rearrange("p (t f) -> p t f", f=256)
        vb1 = h[:, 2:6, :]
        sp1 = h[:, 6:10, :].bitcast(F32)
        glT1 = h[0:33, 10, :]
        B1 = [Buf("kb1"), Buf("vb1"), Buf("sp1"), Buf("glT1")]
        self.memset("pool", h[0:32, 10, :], 0.0, [B1[3]])
        self.memset("pool", h[32:33, 10, :], 1.0, [B1[3]])
        sets = [
            (self.kb.t, self.vb.t, self.sp.t, self.kb, self.vb, self.sp, self.glT.t, self.glT),
            (kb1, vb1, sp1, B1[0], B1[1], B1[2], glT1, B1[3]),
        ]

        def load(g):
            for t in tiles:
                r0 = (g * cfg.gt + t.i) * 128
                P.dma("sp", self.X[t.i].t[:, :], self.xpre[r0:r0 + 128, :], [], [self.X[t.i]])

        def proj(g):
            kb_ap, vb_ap, sp_ap, kbB, vbB, spB, glt, glB = sets[g % 2]
            self.proj_fm(sa, 256, 16, self.hT, self.NT,
                         lambda ps, pb: self.evac(glt[0:16, 0:self.NT], ps, pb, [glB]))
            for t in tiles:
                self.proj_tm(sa, 0, 256, self.hT, t,
                             lambda ps, pb, t=t: self.evac(kb_ap[:, t.i, :], ps, pb, [kbB]))
                self.proj_tm(sb_, 0, 512, self.hT, t,
                             lambda ps, pb, t=t: self.evac(vb_ap[:, t.i, :], ps, pb, [vbB]))
            pss = []
            for t in tiles:
                ap, bufs = self.Ahalf(self.rot("tmb", 4))
                ps = ap[:, 0:256]
                self.mm(ps, glt[0:33, t.cs], self.waext.t[0:33, :], True, True, [glB, self.waext], bufs)
                pss.append((ps, bufs))
            for t, (ps, bufs) in zip(tiles, pss):
                self.act(sp_ap[:, t.i, :], ps, AF.Exp, bufs, [spB], scale=-1.0)
            for t in tiles:
                self.act(sp_ap[:, t.i, :], sp_ap[:, t.i, :], AF.Ln, [spB, self.onesf], [spB], bias=self.onesf.t[:, 0:1])

        def bset(g):
            kb_ap, vb_ap, sp_ap, kbB, vbB, spB, glt, glB = sets[g % 2]
            return (kb_ap, vb_ap, sp_ap, kbB, vbB, spB)

        load(0)
        xbs = self.prenorm_A(items)
        self.prenorm_B(items, None, self.hT, xbs)
        for g in range(G):
            if g + 1 < G:
                load(g + 1)
                sx = self.prenorm_A1(items)
            proj(g)
            if g + 1 < G:
                xbs = self.prenorm_A2(items, sx)
            self.convert_more(3, after=[sets[g % 2][3]])
            if g >= 1:
                self.prefix_sufs(tiles, bset(g - 1))
            if g + 1 < G:
                self.prenorm_B(items, None, self.hT, xbs)
            if g >= 1:
                self.prefix_ds(tiles, bset(g - 1))
        self.prefix_sufs(tiles, bset(G - 1))
        self.prefix_ds(tiles, bset(G - 1))
        self.convert_more(len(self.conv_todo))
        for p in range(2):
            for hh in range(2):
                r = slice(hh * 64, hh * 64 + 64)
                self.copy("dve", self.S[p].t[r, :], self.S2[p][r, hh * 128:(hh + 1) * 128], [self.S2b[p]], [self.S[p]])
            self.copy_Sbf(p)
        self.wrel(ja)
        self.wrel(jb)

    def sample_setup(self):
        P = self.P
        P.cur_stage = "ssetup"
        ctok_b, cmtok_b = self.Y[0], self.Y[1]
        ctok = ctok_b.t[:, :].bitcast(BF16).rearrange("p (k f) -> p k f", k=4)
        cmtok = cmtok_b.t[:, :].bitcast(BF16).rearrange("p (k f) -> p k f", k=2)
        for b in range(2):
            ckT, cv = self.ckT[b], self.cv[b]
            P.dma("pool", ctok[:, :, :], self.cak[b].rearrange("(k p) f -> p k f", p=128), [], [ctok_b])
            P.dma("pool", cv.t[:, :, :], self.cav[b].rearrange("(k p) f -> p k f", p=128), [], [cv])
            for half in range(2):
                ti = 2 + self.rot("trb", 2)
                ptv = self.B[ti].t[:].bitcast(BF16).rearrange("p (c t) -> p c t", c=8)
                for pl in range(2):
                    p = 2 * half + pl
                    for kb in range(4):
                        self.tr(ptv[:, pl * 4 + kb, :], ctok[:, kb, p * 128:(p + 1) * 128], 128,
                                [ctok_b], self.Bq[ti])
                dst = ckT.t[:, 2 * half:2 * half + 2, :].rearrange("p a (k t) -> p (a k) t", t=128)
                self.evac(dst, ptv[:, :, :], self.Bq[ti], [ckT])
            P.dma("pool", cmtok[:, :, :], self.cmk[b].rearrange("(k p) f -> p k f", p=128), [], [cmtok_b])
            P.dma("pool", self.memVs[b].t[:, :, :], self.cmv[b].rearrange("(k p) f -> p k f", p=128), [],
                  [self.memVs[b]])
            for half in range(2):
                ti = 2 + self.rot("trb", 2)
                ptv = self.B[ti].t[:].bitcast(BF16).rearrange("p (c t) -> p c t", c=8)
                for cl in range(4):
                    c = 4 * half + cl
                    for kb in range(2):
                        self.tr(ptv[:, cl * 2 + kb, :], cmtok[:, kb, c * 128:(c + 1) * 128], 128,
                                [cmtok_b], self.Bq[ti])
                dst = self.memKTs[b].t[:, 4 * half:4 * half + 4, :].rearrange("p c (k t) -> p (c k) t", t=128)
                self.evac(dst, ptv[:, :, :], self.Bq[ti], [self.memKTs[b]])


def _bias_toeplitz(table):
    i = np.arange(128)[:, None]
    j = np.arange(640)[None, :]
    idx = np.clip(512 + i - j, -256, 256) + 256
    bm = table[:, idx]
    d = (j // 64) - (i // 64)
    valid = (d >= 0) & (d <= 8)
    return np.where(valid[None], bm, np.float32(NEG)).astype(np.float32)


_CACHE = {}


def _get_nc(cfg_key):
    if cfg_key not in _CACHE:
        _CACHE[cfg_key] = Builder(Cfg(*cfg_key)).build()
    return _CACHE[cfg_key]


def kernel(x_prompt, x_sample, mem_prompt, cache_a_k, cache_a_v, state_gla, cache_mem_k, cache_mem_v,
           g_pre_mix, w_in, rel_bias, w_alpha2, b_alpha, g_gla_out, w_o, g_post_mix,
           g_pre_mem, g_mem, w_mq, w_mk, w_mv, w_mo, g_post_mem,
           g_pre_ffn, w_ffn_gate, w_ffn_up, w_ffn_down, g_post_ffn, _ring=3, _warm_y=0, _warm_g=0):
    f = lambda a: np.ascontiguousarray(np.asarray(a, dtype=np.float32))
    x_prompt, x_sample, mem_prompt = f(x_prompt), f(x_sample), f(mem_prompt)
    nb, seq, _ = x_prompt.shape
    ncore = 8
    qn = ncore // nb
    tpc = seq // qn
    npt = tpc // 128
    pref_t = (qn - 1) * npt
    cfg_key = (npt, pref_t, 4, _ring, _warm_y, _warm_g)
    nc = _get_nc(cfg_key)
    bm = _bias_toeplitz(f(rel_bias)[0])
    shared = {
        "bm": bm,
        "g_pre_mix": f(g_pre_mix), "g_post_mix": f(g_post_mix), "g_pre_mem": f(g_pre_mem), "g_mem": f(g_mem),
        "g_post_mem": f(g_post_mem), "g_pre_ffn": f(g_pre_ffn), "g_post_ffn": f(g_post_ffn),
        "g_gla_out": f(g_gla_out), "w_alpha2": f(w_alpha2)[0], "b_alpha": f(b_alpha),
        "w_in": f(w_in)[0], "w_o": f(w_o)[0], "w_mq": f(w_mq)[0], "w_mk": f(w_mk)[0], "w_mv": f(w_mv)[0],
        "w_mo": f(w_mo)[0], "w_ffn_gate": f(w_ffn_gate)[0], "w_ffn_up": f(w_ffn_up)[0],
        "w_ffn_down": f(w_ffn_down)[0],
    }
    cak, cav = f(cache_a_k)[0].reshape(16, 512, 512), f(cache_a_v)[0].reshape(16, 512, 512)
    sg = f(state_gla)[0].reshape(16, 256, 128)
    cmk, cmv = f(cache_mem_k)[0].reshape(16, 256, 1024), f(cache_mem_v)[0].reshape(16, 256, 1024)
    in_maps = []
    for c in range(ncore):
        s, q = c // qn, c % qn
        t0 = q * tpc
        xpre = np.zeros((max(pref_t, 1) * 128, D), np.float32)
        if q > 0:
            xpre[pref_t * 128 - t0:] = x_prompt[s, 0:t0]
        xh = np.zeros((512, D), np.float32)
        if q > 0:
            xh[:] = x_prompt[s, t0 - 512:t0]
        hmask = np.full((1, 512), NEG if q == 0 else 0.0, np.float32)
        m = dict(shared)
        m.update({
            "xp": x_prompt[s, t0:t0 + tpc], "xh": xh, "xpre": xpre,
            "hmask": hmask,
            "xs": x_sample[2 * c:2 * c + 2].reshape(64, D), "memp": mem_prompt[s],
            "cak": cak[2 * c:2 * c + 2], "cav": cav[2 * c:2 * c + 2], "sgla": sg[2 * c:2 * c + 2],
            "cmk": cmk[2 * c:2 * c + 2], "cmv": cmv[2 * c:2 * c + 2],
        })
        in_maps.append({k: np.ascontiguousarray(v) for k, v in m.items()})
    import os as _os
    if _os.environ.get("KONE"):
        res = run_bass_kernel_spmd(nc, in_maps[:1], core_ids=[0])
        R = [res.results[0]] * ncore
    else:
        res = run_bass_kernel_spmd(nc, in_maps, core_ids=list(range(ncore)))
        R = res.results
    yp = np.stack([np.concatenate([R[s * qn + q]["yp"] for q in range(qn)], 0) for s in range(nb)])
    ys = np.concatenate([R[c]["ys"] for c in range(ncore)], 0).reshape(16, 32, D)
    last = [s * qn + qn - 1 for s in range(nb)]
    pak = np.stack([R[c]["pak"] for c in last]).reshape(1, nb, 512, 8, 64)
    pav = np.stack([R[c]["pav"] for c in last]).reshape(1, nb, 512, 8, 64)
    pgla = np.stack([R[c]["pgla"] for c in last]).reshape(1, nb, 4, 64, 128)
    firstc = [s * qn for s in range(nb)]
    pmk = np.stack([R[c]["pmk"] for c in firstc]).reshape(1, nb, 256, 4, 256)
    pmv = np.stack([R[c]["pmv"] for c in firstc]).reshape(1, nb, 256, 4, 256)
    sak = np.concatenate([R[c]["sak"] for c in range(ncore)], 0).reshape(1, 16, 32, 8, 64)
    sav = np.concatenate([R[c]["sav"] for c in range(ncore)], 0).reshape(1, 16, 32, 8, 64)
    sgl = np.concatenate([R[c]["sglao"] for c in range(ncore)], 0).reshape(1, 16, 4, 64, 128)
    outs = (yp, ys, pak, pav, pgla, pmk, pmv, sak, sav, sgl)
    return tuple(np.ascontiguousarray(o.astype(np.float32)) for o in outs)
```
